# Optimizing a Trainium2 kernel written in Bass

```python
import math
import jax
import jax.numpy as jnp
from jax import lax
import numpy as np

D_MODEL = 1024
BATCH = 8
SEQ = 8192
DEPTH = 2

CTX_LEN = 256
GRID_W = 64
N_HEADS = 8
N_KV_HEADS = 2
KV_GROUP = N_HEADS // N_KV_HEADS
HEAD_DIM = 128
ATTN_WIDTH = N_HEADS * HEAD_DIM
KV_WIDTH = N_KV_HEADS * HEAD_DIM
ROPE_AXIS_DIM = HEAD_DIM // 2
ROPE_THETA = 10000.0
Q_BLOCK = 128
HYENA_WIDTH = D_MODEL // 2
HYENA_PROJ = 3 * HYENA_WIDTH
SHORT_CONV = 3
FILTER_EMB = 33
FILTER_BANDS = (FILTER_EMB - 1) // 2
FILTER_HIDDEN = 64
FILTER_OUT_GAIN = 0.05
DECAY_TARGET = 1e-2
FAST_DECAY_PCT = 0.3
SLOW_DECAY_PCT = 1.5
N_BRANCHES = 2
GATE_WIDTH = N_BRANCHES * D_MODEL
PROJ_WIDTH = ATTN_WIDTH + 2 * KV_WIDTH + HYENA_PROJ + GATE_WIDTH
PROJ_SPLITS = (ATTN_WIDTH, ATTN_WIDTH + KV_WIDTH, ATTN_WIDTH + 2 * KV_WIDTH,
               ATTN_WIDTH + 2 * KV_WIDTH + HYENA_PROJ)
D_FF = (8 * D_MODEL + 3 * 256 - 1) // (3 * 256) * 256
N_MOD = 6
EPS = 1e-6

kernel_name = 'hybrid_gqa_hyena_diffusion_block'


def rms_norm(x, gain):
    xf = x.astype(jnp.float32)
    y = xf * lax.rsqrt(jnp.mean(xf * xf, axis=-1, keepdims=True) + EPS)
    return (y * gain.astype(jnp.float32)).astype(x.dtype)


def modulate(h, shift, scale):
    return h * (1 + scale) + shift


def split_heads_norm(t, gain, n_heads):
    t = t.reshape(t.shape[0], t.shape[1], n_heads, HEAD_DIM)
    return rms_norm(t, gain)


def axial_rope_tables(rows):
    row = jnp.repeat(jnp.arange(rows, dtype=jnp.float32), GRID_W)
    col = jnp.tile(jnp.arange(GRID_W, dtype=jnp.float32), rows)
    inv_freq = ROPE_THETA ** (-jnp.arange(0, ROPE_AXIS_DIM, 2, dtype=jnp.float32) / ROPE_AXIS_DIM)
    ang = jnp.concatenate([row[:, None] * inv_freq, col[:, None] * inv_freq], axis=-1)
    return jnp.cos(ang), jnp.sin(ang)


def apply_rope(t, cos, sin):
    tf = t.astype(jnp.float32).reshape(*t.shape[:-1], HEAD_DIM // 2, 2)
    c = cos[None, :, None, :]
    s = sin[None, :, None, :]
    t0, t1 = tf[..., 0], tf[..., 1]
    out = jnp.stack([t0 * c - t1 * s, t0 * s + t1 * c], axis=-1)
    return out.reshape(t.shape).astype(t.dtype)


def latent_attention(q, k, v, k_ctx, v_ctx):
    b, length = q.shape[0], q.shape[1]
    n_blocks = length // Q_BLOCK
    keys = jnp.concatenate([k_ctx, k], axis=1)
    vals = jnp.concatenate([v_ctx, v], axis=1)
    qb = q.reshape(b, n_blocks, Q_BLOCK, N_KV_HEADS, KV_GROUP, HEAD_DIM)
    qb = jnp.moveaxis(qb, 1, 0)
    scale = HEAD_DIM ** -0.5

    def one_block(q_blk):
        s = jnp.einsum('bqhgd,bkhd->bhgqk', q_blk, keys, preferred_element_type=jnp.float32) * scale
        p = jax.nn.softmax(s, axis=-1).astype(vals.dtype)
        return jnp.einsum('bhgqk,bkhd->bqhgd', p, vals)

    out = lax.map(one_block, qb)
    return jnp.moveaxis(out, 0, 1).reshape(b, length, ATTN_WIDTH)


def context_attention(q, k, v):
    b, length = q.shape[0], q.shape[1]
    qg = q.reshape(b, length, N_KV_HEADS, KV_GROUP, HEAD_DIM)
    s = jnp.einsum('bqhgd,bkhd->bhgqk', qg, k, preferred_element_type=jnp.float32) * HEAD_DIM ** -0.5
    p = jax.nn.softmax(s, axis=-1).astype(v.dtype)
    return jnp.einsum('bhgqk,bkhd->bqhgd', p, v).reshape(b, length, ATTN_WIDTH)


def short_conv(u, w, bias):
    up = jnp.pad(u, ((0, 0), (1, 1), (0, 0)))
    return up[:, :-2] * w[0] + up[:, 1:-1] * w[1] + up[:, 2:] * w[2] + bias


def hyena_filter(length, fw1, fb1, fw2, fb2, fw3, fb3, fw4, freq):
    t = jnp.linspace(0.0, 1.0, length, dtype=jnp.float32)[:, None]
    w = (2.0 * math.pi / length) * jnp.arange(length, dtype=jnp.float32)[:, None]
    f = jnp.linspace(1e-4, FILTER_BANDS - 1, FILTER_BANDS, dtype=jnp.float32)[None, :]
    z = jnp.concatenate([t, jnp.cos(f * w), -jnp.sin(f * w)], axis=-1)
    h = jnp.sin(freq[0] * (z @ fw1 + fb1))
    h = jnp.sin(freq[1] * (h @ fw2 + fb2))
    h = jnp.sin(freq[2] * (h @ fw3 + fb3))
    h = (h @ fw4).reshape(length, 2, HYENA_WIDTH)
    max_decay = math.log(DECAY_TARGET) / FAST_DECAY_PCT
    min_decay = math.log(DECAY_TARGET) / SLOW_DECAY_PCT
    deltas = jnp.abs(jnp.linspace(min_decay, max_decay, HYENA_WIDTH, dtype=jnp.float32))
    decay = jnp.exp(-t * deltas)
    h = h * decay[:, None, :]
    return h[:, 0], h[:, 1]


def bidirectional_long_conv(v, h_fwd, h_bwd):
    length = v.shape[1]
    n_fft = 2 * length
    k = jnp.concatenate([h_fwd, jnp.zeros((1, HYENA_WIDTH), h_fwd.dtype), h_bwd[:0:-1]], axis=0)
    v_f = jnp.fft.rfft(v.astype(jnp.float32), n=n_fft, axis=1)
    k_f = jnp.fft.rfft(k, n=n_fft, axis=0)
    return jnp.fft.irfft(v_f * k_f[None], n=n_fft, axis=1)[:, :length]


def hyena_mix(u, conv_w, conv_b, fw1, fb1, fw2, fb2, fw3, fb3, fw4, freq, bias):
    length = u.shape[1]
    x0, x1, v = jnp.split(short_conv(u, conv_w, conv_b), 3, axis=-1)
    v = v * x1
    h_fwd, h_bwd = hyena_filter(length, fw1, fb1, fw2, fb2, fw3, fb3, fw4, freq)
    y = bidirectional_long_conv(v, h_fwd, h_bwd) + v.astype(jnp.float32) * bias
    return (y * x0).astype(u.dtype)


def merge_branches(attn_out, hyena_out, gate_logits, w_o_attn, w_o_hyena, w_out):
    g_attn, g_hyena = jnp.split(jax.nn.sigmoid(gate_logits), N_BRANCHES, axis=-1)
    merged = g_attn * (attn_out @ w_o_attn) + g_hyena * (hyena_out @ w_o_hyena)
    return merged @ w_out


def swiglu(h, w_gate_up, w_down):
    g, u = jnp.split(h @ w_gate_up, 2, axis=-1)
    return (jax.nn.silu(g) * u) @ w_down


def setup_inputs(seed: int = 0) -> dict:
    key = jax.random.key(seed)
    ks = jax.random.split(key, 28)
    D = D_MODEL

    def nrm(k, shape, scale=1.0):
        return jax.random.normal(k, shape, jnp.float32) * scale

    return {
        'x': nrm(ks[0], (BATCH, SEQ, D)),
        'c': nrm(ks[1], (BATCH, D)),
        'ctx': nrm(ks[2], (BATCH, CTX_LEN, D)),
        'c_ctx': nrm(ks[3], (D,)),
        'w_mod': nrm(ks[4], (DEPTH, D, N_MOD * D), D ** -0.5),
        'b_mod': nrm(ks[5], (DEPTH, N_MOD * D), 0.01),
        'norm_mix': 1.0 + nrm(ks[6], (DEPTH, D), 0.05),
        'w_in': nrm(ks[7], (DEPTH, D, PROJ_WIDTH), D ** -0.5),
        'q_norm': 1.0 + nrm(ks[8], (DEPTH, HEAD_DIM), 0.05),
        'k_norm': 1.0 + nrm(ks[9], (DEPTH, HEAD_DIM), 0.05),
        'conv_w': nrm(ks[10], (DEPTH, SHORT_CONV, HYENA_PROJ), SHORT_CONV ** -0.5),
        'conv_b': nrm(ks[11], (DEPTH, HYENA_PROJ), 0.01),
        'filt_w1': nrm(ks[12], (DEPTH, FILTER_EMB, FILTER_HIDDEN), FILTER_EMB ** -0.5),
        'filt_b1': nrm(ks[13], (DEPTH, FILTER_HIDDEN), 0.1),
        'filt_w2': nrm(ks[14], (DEPTH, FILTER_HIDDEN, FILTER_HIDDEN), FILTER_HIDDEN ** -0.5),
        'filt_b2': nrm(ks[15], (DEPTH, FILTER_HIDDEN), 0.1),
        'filt_w3': nrm(ks[16], (DEPTH, FILTER_HIDDEN, FILTER_HIDDEN), FILTER_HIDDEN ** -0.5),
        'filt_b3': nrm(ks[17], (DEPTH, FILTER_HIDDEN), 0.1),
        'filt_w4': nrm(ks[18], (DEPTH, FILTER_HIDDEN, 2 * HYENA_WIDTH), FILTER_HIDDEN ** -0.5 * FILTER_OUT_GAIN),
        'filt_freq': 1.0 + nrm(ks[19], (DEPTH, 3, FILTER_HIDDEN), 0.1),
        'hyena_bias': nrm(ks[20], (DEPTH, HYENA_WIDTH), 0.5),
        'w_o_attn': nrm(ks[21], (DEPTH, ATTN_WIDTH, D), ATTN_WIDTH ** -0.5),
        'w_o_hyena': nrm(ks[22], (DEPTH, HYENA_WIDTH, D), HYENA_WIDTH ** -0.5),
        'w_out': nrm(ks[23], (DEPTH, D, D), D ** -0.5),
        'norm_ffn': 1.0 + nrm(ks[24], (DEPTH, D), 0.05),
        'w_gate_up': nrm(ks[25], (DEPTH, D, 2 * D_FF), D ** -0.5),
        'w_down': nrm(ks[26], (DEPTH, D_FF, D), D_FF ** -0.5),
        'norm_final': 1.0 + nrm(ks[27], (D,), 0.05),
    }


def reference(x, c, ctx, c_ctx, w_mod, b_mod, norm_mix, w_in, q_norm, k_norm, conv_w, conv_b,
              filt_w1, filt_b1, filt_w2, filt_b2, filt_w3, filt_b3, filt_w4, filt_freq, hyena_bias,
              w_o_attn, w_o_hyena, w_out, norm_ffn, w_gate_up, w_down, norm_final):
    b, length = x.shape[0], x.shape[1]
    rows = length // GRID_W
    cos, sin = axial_rope_tables(rows)
    silu_c = jax.nn.silu(c)
    silu_cc = jax.nn.silu(c_ctx)
    for layer in range(DEPTH):
        last = layer == DEPTH - 1
        mod = (silu_c @ w_mod[layer] + b_mod[layer])[:, None, :]
        mod_c = silu_cc @ w_mod[layer] + b_mod[layer]
        shift1, scale1, gate1, shift2, scale2, gate2 = jnp.split(mod, N_MOD, axis=-1)
        c_shift1, c_scale1, c_gate1, c_shift2, c_scale2, c_gate2 = jnp.split(mod_c, N_MOD, axis=-1)
        filt = (filt_w1[layer], filt_b1[layer], filt_w2[layer], filt_b2[layer],
                filt_w3[layer], filt_b3[layer], filt_w4[layer], filt_freq[layer])

        h_ctx = modulate(rms_norm(ctx, norm_mix[layer]), c_shift1, c_scale1)
        if last:
            kc, vc = jnp.split(h_ctx @ w_in[layer][:, ATTN_WIDTH:ATTN_WIDTH + 2 * KV_WIDTH], 2, axis=-1)
        else:
            qc, kc, vc, hyc, gc = jnp.split(h_ctx @ w_in[layer], PROJ_SPLITS, axis=-1)
        kc = split_heads_norm(kc, k_norm[layer], N_KV_HEADS)
        vc = vc.reshape(b, vc.shape[1], N_KV_HEADS, HEAD_DIM)

        h = modulate(rms_norm(x, norm_mix[layer]), shift1, scale1)
        q, k, v, hy, g = jnp.split(h @ w_in[layer], PROJ_SPLITS, axis=-1)
        q = apply_rope(split_heads_norm(q, q_norm[layer], N_HEADS), cos, sin)
        k = apply_rope(split_heads_norm(k, k_norm[layer], N_KV_HEADS), cos, sin)
        v = v.reshape(b, length, N_KV_HEADS, HEAD_DIM)
        attn = latent_attention(q, k, v, kc, vc)
        hyo = hyena_mix(hy, conv_w[layer], conv_b[layer], *filt, hyena_bias[layer])
        x_mix = merge_branches(attn, hyo, g, w_o_attn[layer], w_o_hyena[layer], w_out[layer])

        if not last:
            attn_c = context_attention(split_heads_norm(qc, q_norm[layer], N_HEADS), kc, vc)
            hyo_c = hyena_mix(hyc, conv_w[layer], conv_b[layer], *filt, hyena_bias[layer])
            ctx_mix = merge_branches(attn_c, hyo_c, gc, w_o_attn[layer], w_o_hyena[layer], w_out[layer])
            ctx = ctx + c_gate1 * ctx_mix
            ctx = ctx + c_gate2 * swiglu(modulate(rms_norm(ctx, norm_ffn[layer]), c_shift2, c_scale2),
                                         w_gate_up[layer], w_down[layer])

        x = x + gate1 * x_mix
        x = x + gate2 * swiglu(modulate(rms_norm(x, norm_ffn[layer]), shift2, scale2),
                               w_gate_up[layer], w_down[layer])
    return rms_norm(x, norm_final)
```

```python
import math
from contextlib import ExitStack
import numpy as np
import concourse.bass as bass
import concourse.mybir as mybir
from concourse.bass_utils import run_bass_kernel_spmd

F32 = mybir.dt.float32
BF16 = mybir.dt.bfloat16
AF = mybir.ActivationFunctionType
ALU = mybir.AluOpType

D = 1024
KC = 8
NH = 8
NKV = 2
HD = 128
HW = 512
NPROJ = 5120
DFF = 2816
FC = 22
NMOD = 6
CTX = 256
NFFT = 16384
EPS = 1e-6
FEMB = 33
FHID = 64


class Res:
    __slots__ = ("name", "w", "r")

    def __init__(self, name=""):
        self.name = name
        self.w = None
        self.r = []


class View:
    __slots__ = ("ap", "res")

    def __init__(self, ap, res):
        self.ap = ap
        self.res = res


class Tl:
    def __init__(self, h, name):
        self.h = h
        self.res = Res(name)

    def __getitem__(self, k):
        return View(self.h[k], self.res)

    def v(self, ap):
        return View(ap, self.res)


def DV(ap):
    return View(ap, None)


class Eng:
    def __init__(self, name):
        self.name = name
        self.q = []
        self.sem = None
        self.cnt = 0
        self.seen = {}
        self.dsems = []
        self.dnext = 0


class FW:
    def __init__(self, nc, ndma=8):
        self.nc = nc
        self.st = ExitStack()
        self.E = {}
        for nm in ["pe", "act", "dve", "pool", "sp"]:
            e = Eng(nm)
            e.sem = self.st.enter_context(nc.semaphore("cs_" + nm))
            self.E[nm] = e
        for nm in ["sp", "pool", "act"]:
            e = self.E[nm]
            for i in range(ndma):
                s = self.st.enter_context(nc.semaphore("ds_%s%d" % (nm, i)))
                e.dsems.append([s, 0])
        self.n_ops = 0
        self.uid = 0
        self.phase_st = None
        self.stack = []

    def _alloc(self, name, shape, dt, psum):
        self.uid += 1
        nm = "%s_%d" % (name, self.uid)
        st = self.phase_st if self.phase_st is not None else self.st
        if psum:
            h = st.enter_context(self.nc.psum_tensor(nm, list(shape), dt))
        else:
            h = st.enter_context(self.nc.sbuf_tensor(nm, list(shape), dt))
        return Tl(h, nm)

    def sb(self, name, shape, dt):
        return self._alloc(name, shape, dt, False)

    def ps(self, name, shape, dt):
        return self._alloc(name, shape, dt, True)

    def _wait(self, e, deps):
        need = {}
        for d in deps:
            if d is None:
                continue
            sem, val, owner = d
            if owner == e.name and e.name == "pe":
                continue
            k = id(sem)
            if e.seen.get(k, 0) >= val:
                continue
            if k not in need or need[k][1] < val:
                need[k] = (sem, val)
        for k, (sem, val) in need.items():
            e.seen[k] = val
            e.q.append(lambda h, sem=sem, val=val: h.wait_ge(sem, val))

    def _deps(self, reads, writes):
        deps = []
        for r in reads:
            deps.append(r.w)
        for w in writes:
            deps.append(w.w)
            deps.extend(w.r)
        return deps

    def _commit(self, tok, reads, writes):
        for r in reads:
            r.r.append(tok)
        for w in writes:
            w.w = tok
            w.r = []

    def op(self, eng, fn, reads=(), writes=()):
        e = self.E[eng]
        self._wait(e, self._deps(reads, writes))
        e.cnt += 1
        sem = e.sem
        e.q.append(lambda h, fn=fn, sem=sem: fn(h).then_inc(sem, 1))
        tok = (sem, e.cnt, e.name)
        self._commit(tok, reads, writes)
        self.n_ops += 1
        return tok

    def dma(self, eng, out, in_, reads=(), writes=(), **kw):
        e = self.E[eng]
        self._wait(e, self._deps(reads, writes))
        slot = e.dsems[e.dnext]
        e.dnext = (e.dnext + 1) % len(e.dsems)
        sem, val = slot
        if val > 0 and e.seen.get(id(sem), 0) < val:
            e.seen[id(sem)] = val
            e.q.append(lambda h, sem=sem, val=val: h.wait_ge(sem, val))
        slot[1] = val + 16
        e.q.append(lambda h, out=out, in_=in_, sem=sem, kw=kw:
                   h.dma_start(out=out, in_=in_, **kw).then_inc(sem, 16))
        tok = (sem, val + 16, "dma_" + e.name)
        self._commit(tok, reads, writes)
        self.n_ops += 1
        return tok

    def barrier(self):
        toks = []
        for e in self.E.values():
            if e.cnt > 0:
                toks.append((e.sem, e.cnt, e.name))
            for sem, val in e.dsems:
                if val > 0:
                    toks.append((sem, val, "dma_" + e.name))
        for e in self.E.values():
            need = {}
            for sem, val, owner in toks:
                if owner == e.name:
                    continue
                k = id(sem)
                if e.seen.get(k, 0) >= val:
                    continue
                need[k] = (sem, val)
            for k, (sem, val) in need.items():
                e.seen[k] = val
                e.q.append(lambda h, sem=sem, val=val: h.wait_ge(sem, val))

    def begin_phase(self):
        self.barrier()
        self.stack.append(self.phase_st)
        self.phase_st = ExitStack()

    def end_phase(self):
        self.barrier()
        self.phase_st.close()
        self.phase_st = self.stack.pop()

    def finish(self):
        self.barrier()
        nc = self.nc
        E = self.E
        with nc.Block() as block:
            @block.tensor
            def _(h):
                for f in E["pe"].q:
                    f(h)

            @block.scalar
            def _(h):
                for f in E["act"].q:
                    f(h)

            @block.vector
            def _(h):
                for f in E["dve"].q:
                    f(h)

            @block.gpsimd
            def _(h):
                for f in E["pool"].q:
                    f(h)

            @block.sync
            def _(h):
                for f in E["sp"].q:
                    f(h)
        self.st.close()


def _flat(vs):
    out = []
    for v in vs:
        if isinstance(v, View) and v.res is not None:
            if isinstance(v.res, (tuple, list)):
                out.extend(v.res)
            else:
                out.append(v.res)
    return out


def _rw(ins, outs):
    return _flat(ins), _flat(outs)


def _a(x):
    return x.ap if isinstance(x, View) else x


class Stream:
    pass


class Builder:
    def __init__(self, nc, SEQ, DEPTH, dbg=None):
        self.nc = nc
        self.SEQ = SEQ
        self.DEPTH = DEPTH
        self.fw = FW(nc)
        self.dbg = dbg or {}
        self.rr = 0

    def mm(self, out, lhsT, rhs, start, stop, skip=False):
        r, w = _rw([lhsT, rhs], [out])
        o, a, b = out.ap, lhsT.ap, rhs.ap
        self.fw.op("pe", lambda h: h.matmul(o, a, b, start=start, stop=stop, skip_group_check=skip), r, w)

    def tr(self, out, in_, ident):
        r, w = _rw([in_, ident], [out])
        o, a, b = out.ap, in_.ap, ident.ap
        self.fw.op("pe", lambda h: h.transpose(o, a, b), r, w)

    def act(self, out, in_, func, bias=None, scale=None):
        r, w = _rw([in_, bias, scale], [out])
        kw = {}
        if bias is not None:
            kw["bias"] = _a(bias)
        if scale is not None:
            kw["scale"] = _a(scale)
        o, a = out.ap, in_.ap
        self.fw.op("act", lambda h: h.activation(o, a, func, **kw), r, w)

    def tt(self, eng, out, in0, in1, op):
        r, w = _rw([in0, in1], [out])
        o, a, b = out.ap, in0.ap, in1.ap
        self.fw.op(eng, lambda h: h.tensor_tensor(o, a, b, op), r, w)

    def stt(self, eng, out, in0, scalar, in1, op0, op1):
        r, w = _rw([in0, scalar, in1], [out])
        o, a, s, b = out.ap, in0.ap, _a(scalar), in1.ap
        self.fw.op(eng, lambda h: h.scalar_tensor_tensor(o, a, s, b, op0, op1), r, w)

    def ts(self, eng, out, in0, s1, s2, op0, op1):
        r, w = _rw([in0, s1, s2], [out])
        o, a, x1, x2 = out.ap, in0.ap, _a(s1), _a(s2)
        self.fw.op(eng, lambda h: h.tensor_scalar(o, a, x1, x2, op0, op1), r, w)

    def cp(self, eng, out, in_):
        r, w = _rw([in_], [out])
        o, a = out.ap, in_.ap
        if eng == "act":
            self.fw.op("act", lambda h: h.activation(o, a, AF.Copy), r, w)
        else:
            self.fw.op(eng, lambda h: h.tensor_copy(o, a), r, w)

    def recip(self, out, in_):
        r, w = _rw([in_], [out])
        o, a = out.ap, in_.ap
        self.fw.op("dve", lambda h: h.reciprocal(o, a), r, w)

    def memset(self, eng, out, val):
        r, w = _rw([], [out])
        o = out.ap
        self.fw.op(eng, lambda h: h.memset(o, val), r, w)

    def dma(self, eng, out, in_, **kw):
        r, w = _rw([in_], [out])
        return self.fw.dma(eng, out.ap, in_.ap, r, w, **kw)

    def ld(self, out, in_ap, **kw):
        self.dma("sp", out, DV(in_ap), **kw)

    def stq(self, out_ap, in_, **kw):
        self.dma("pool", DV(out_ap), in_, **kw)

    def dram_in(self, name, shape, dt=F32):
        return self.nc.dram_tensor(name, list(shape), dt, kind="ExternalInput").ap()

    def dram_scratch(self, name, shape, dt):
        kind = "ExternalOutput" if name in self.dbg else "Internal"
        return self.nc.dram_tensor(name, list(shape), dt, kind=kind).ap()

    def build(self):
        nc, fw, SEQ, DEPTH = self.nc, self.fw, self.SEQ, self.DEPTH
        I = {}
        I["x"] = self.dram_in("x", [SEQ, D])
        I["c"] = self.dram_in("c", [D])
        I["ctx"] = self.dram_in("ctx", [CTX, D])
        I["c_ctx"] = self.dram_in("c_ctx", [D])
        I["w_mod"] = self.dram_in("w_mod", [DEPTH, D, NMOD * D])
        I["b_mod"] = self.dram_in("b_mod", [DEPTH, NMOD * D])
        I["norm_mix"] = self.dram_in("norm_mix", [DEPTH, D])
        I["w_in"] = self.dram_in("w_in", [DEPTH, D, NPROJ])
        I["q_norm"] = self.dram_in("q_norm", [DEPTH, HD])
        I["k_norm"] = self.dram_in("k_norm", [DEPTH, HD])
        I["conv_w"] = self.dram_in("conv_w", [DEPTH, 3, 3 * HW])
        I["conv_b"] = self.dram_in("conv_b", [DEPTH, 3 * HW])
        I["filt_w1"] = self.dram_in("filt_w1", [DEPTH, FEMB, FHID])
        I["filt_b1"] = self.dram_in("filt_b1", [DEPTH, FHID])
        I["filt_w2"] = self.dram_in("filt_w2", [DEPTH, FHID, FHID])
        I["filt_b2"] = self.dram_in("filt_b2", [DEPTH, FHID])
        I["filt_w3"] = self.dram_in("filt_w3", [DEPTH, FHID, FHID])
        I["filt_b3"] = self.dram_in("filt_b3", [DEPTH, FHID])
        I["filt_w4"] = self.dram_in("filt_w4", [DEPTH, FHID, 2 * HW])
        I["filt_freq"] = self.dram_in("filt_freq", [DEPTH, 3, FHID])
        I["hyena_bias"] = self.dram_in("hyena_bias", [DEPTH, HW])
        I["w_o_attn"] = self.dram_in("w_o_attn", [DEPTH, D, D])
        I["w_o_hyena"] = self.dram_in("w_o_hyena", [DEPTH, HW, D])
        I["w_out"] = self.dram_in("w_out", [DEPTH, D, D])
        I["norm_ffn"] = self.dram_in("norm_ffn", [DEPTH, D])
        I["w_gate_up"] = self.dram_in("w_gate_up", [DEPTH, D, 2 * DFF])
        I["w_down"] = self.dram_in("w_down", [DEPTH, DFF, D])
        I["norm_final"] = self.dram_in("norm_final", [D])
        I["k_ident"] = self.dram_in("k_ident", [128, 128])
        I["k_rmat"] = self.dram_in("k_rmat", [128, 128])
        I["k_ft"] = self.dram_in("k_ft", [128, 4, 128])
        I["k_tw"] = self.dram_in("k_tw", [128, 2, 128])
        I["k_ropec"] = self.dram_in("k_ropec", [128, SEQ])
        I["k_ropes"] = self.dram_in("k_ropes", [128, SEQ])
        I["k_z_lat"] = self.dram_in("k_z_lat", [2, FEMB, SEQ])
        I["k_z_ctx"] = self.dram_in("k_z_ctx", [2, FEMB, CTX])
        I["k_dec_lat"] = self.dram_in("k_dec_lat", [2, HW, SEQ])
        I["k_dec_ctx"] = self.dram_in("k_dec_ctx", [2, HW, CTX])
        self.I = I
        self.out = nc.dram_tensor("out", [SEQ, D], F32, kind="ExternalOutput").ap()

        S = {}
        sc = self.dram_scratch
        S["wb_in"] = sc("wb_in", [DEPTH, D, NPROJ], BF16)
        S["wb_oa"] = sc("wb_oa", [DEPTH, D, D], BF16)
        S["wb_oh"] = sc("wb_oh", [DEPTH, HW, D], BF16)
        S["wb_out"] = sc("wb_out", [DEPTH, D, D], BF16)
        S["wb_gu"] = sc("wb_gu", [DEPTH, D, 2 * DFF], BF16)
        S["wb_dn"] = sc("wb_dn", [DEPTH, DFF, D], BF16)
        for jb in range(3):
            S["KB%d" % jb] = sc("KB%d" % jb, [HW, NFFT], BF16)
            S["KF%d" % jb] = sc("KF%d" % jb, [128, HW, 2, 128], F32)
        self.S = S

        def mkstream(name, T, j, rope, key_off):
            s = Stream()
            s.name, s.T, s.j, s.rope, s.key_off = name, T, j, rope, key_off
            s.TC = min(512, T)
            s.XT = sc("XT_" + name, [D, T], F32)
            s.Qs = sc("Qs_" + name, [NH, HD, T], BF16)
            s.AT = sc("AT_" + name, [D, T], BF16)
            s.U = sc("U_" + name, [3 * HW, T], F32)
            s.G = sc("G_" + name, [2 * D, T], BF16)
            s.VV = sc("VV_" + name, [HW, T], F32)
            s.VB = sc("VB_" + name, [HW, T], BF16)
            s.YB = sc("YB_" + name, [HW, T], F32)
            s.HY = sc("HY_" + name, [HW, T], BF16)
            return s

        self.lat = mkstream("lat", SEQ, 0, True, CTX)
        self.cx = mkstream("ctx", CTX, 1, False, 0)
        self.lat.z, self.lat.dec = I["k_z_lat"], I["k_dec_lat"]
        self.cx.z, self.cx.dec = I["k_z_ctx"], I["k_dec_ctx"]
        self.lat.n_keys = CTX + SEQ
        self.cx.n_keys = CTX

        self.ident = fw.sb("ident", [128, 128], F32)
        self.onesb = fw.sb("onesb", [128, 128], BF16)
        self.onesf = fw.sb("onesf", [128, 128], F32)
        self.rmat = fw.sb("rmat", [128, 128], BF16)
        self.ft = fw.sb("ft", [128, 4, 128], BF16)
        self.tw = fw.sb("tw", [128, 2, 128], F32)
        self.modcol = [fw.sb("modcol%d" % l, [128, 6 * KC, 2], F32) for l in range(DEPTH)]
        self.A1 = [fw.sb("A1_%d" % l, [128, KC, 2], F32) for l in range(DEPTH)]
        self.A2 = [fw.sb("A2_%d" % l, [128, KC, 2], F32) for l in range(DEPTH)]
        self.nfin = fw.sb("nfin", [128, KC], F32)
        self.psw = [fw.ps("psw%d" % i, [128, 1024], F32) for i in range(2)]
        self.ps = []
        for i in range(2):
            for hf in range(2):
                t = Tl(self.psw[i].h[:, hf * 512:(hf + 1) * 512], "psw%d_%d" % (i, hf))
                self.ps.append(t)
        self.ps += [fw.ps("ps%d" % i, [128, 512], F32) for i in range(4, 8)]

        self.jobs = [(self.lat, 0, 0), (self.cx, 0, 1)] + ([(self.lat, 1, 2)] if DEPTH > 1 else [])
        self.prologue()
        for l in range(DEPTH):
            last = (l == DEPTH - 1)
            self.layer_cols(l)
            self.lat.KF = S["KF0"] if l == 0 else S["KF2"]
            self.cx.KF = S["KF1"]
            self.attn_phase(l, last)
            self.ug_phase(l, last)
            streams = [self.lat] if last else [self.lat, self.cx]
            for s in streams:
                self.hyena_a(s, l)
                self.fft_conv(s)
                self.hyena_c(s, l)
            self.merge_phase(l, streams)
            self.ffn_phase(l, streams)
        self.final_phase()
        fw.barrier()
        self.cols_st.close()
        fw.finish()

    def nps(self, lo=0, hi=4):
        p = self.ps[lo + (self.rr % (hi - lo))]
        self.rr += 1
        return p

    def colvec(self, dst, src_ap):
        self.ld(dst, src_ap.rearrange("(c p) -> p c", p=128), allow_slow_non_contiguous=True)

    def prologue(self):
        fw, I, S = self.fw, self.I, self.S
        DEPTH, SEQ = self.DEPTH, self.SEQ
        fw.begin_phase()
        self.ld(self.ident[:], I["k_ident"])
        tmpf = fw.sb("tmpf", [128, 4, 128], F32)
        self.ld(tmpf[:, 0, :], I["k_rmat"])
        self.cp("dve", self.rmat[:], tmpf[:, 0, :])
        tmpf2 = fw.sb("tmpf2", [128, 4, 128], F32)
        self.ld(tmpf2[:], I["k_ft"])
        self.cp("dve", self.ft[:], tmpf2[:])
        self.ld(self.tw[:], I["k_tw"])
        self.memset("dve", self.onesb[:], 1.0)
        self.memset("dve", self.onesf[:], 1.0)
        self.colvec(self.nfin[:], I["norm_final"])
        for _ in self.cast_gen([("w_in", "wb_in", D, NPROJ, 0)]):
            pass
        ccol = fw.sb("ccol", [128, KC, 2], F32)
        scol = fw.sb("scol", [128, KC, 2], F32)
        self.colvec(ccol[:, :, 0], I["c"])
        self.colvec(ccol[:, :, 1], I["c_ctx"])
        self.act(scol[:], ccol[:], AF.Silu)
        bcol = fw.sb("bcol", [128, 6 * KC], F32)
        wm = [fw.sb("wm%d" % i, [128, KC, 512], F32) for i in range(2)]
        pm = self.ps[7]
        n = 0
        for l in range(DEPTH):
            for q4 in range(4):
                self.ld(bcol[:, q4 * 12:(q4 + 1) * 12],
                        I["b_mod"][l, q4 * 1536:(q4 + 1) * 1536].rearrange("(c p) -> p c", p=128),
                        allow_slow_non_contiguous=True)
            for cb in range(12):
                w = wm[n % 2]
                n += 1
                self.ld(w[:], I["w_mod"][l, :, cb * 512:(cb + 1) * 512].rearrange("(kc p) n -> p kc n", p=128))
                for f4 in range(4):
                    f = cb * 4 + f4
                    for kc in range(KC):
                        self.mm(pm[:, f * 2:f * 2 + 2], w[:, kc, f4 * 128:(f4 + 1) * 128], scol[:, kc, :],
                                kc == 0, kc == KC - 1, skip=True)
            pv = pm.v(pm.h[:, 0:96].rearrange("p (f j) -> p f j", j=2))
            self.tt("dve", self.modcol[l][:], pv, bcol.v(bcol.h[:].unsqueeze(2).broadcast_to([128, 48, 2])), ALU.add)
        fw.end_phase()
        fw.begin_phase()
        self.zt = fw.sb("zt", [128, 2048], BF16)
        self.memset("pool", self.zt[:], 0.0)
        xtok = [fw.sb("xtok%d" % i, [128, 4, D], F32) for i in range(2)]
        xts = [fw.sb("xts%d" % i, [128, KC, 512], F32) for i in range(2)]

        xtok_c = [fw.sb("xtokc", [128, 2, D], F32)]
        xts_c = [fw.sb("xtsc", [128, KC, 256], F32)]

        def xt_gen(src, s, xtok, xts):
            TC = s.TC
            nj = TC // 128
            for ch in range(s.T // TC):
                t0 = ch * TC
                xt, xs = xtok[ch % len(xtok)], xts[ch % len(xts)]
                self.ld(xt[:, :nj, :], src[t0:t0 + TC, :].rearrange("(j p) f -> p j f", p=128))
                for c in range(KC):
                    p = self.nps(0, 4)
                    for j in range(nj):
                        self.tr(p[:, j * 128:(j + 1) * 128], xt[:, j, c * 128:(c + 1) * 128], self.ident[:])
                    self.cp("act" if c % 2 == 0 else "dve", xs[:, c, :TC], p[:, :TC])
                    if c % 2 == 1:
                        yield
                self.stq(s.XT[:, t0:t0 + TC].rearrange("(c p) t -> p c t", p=128), xs[:, :, :TC])
                yield

        gens = [xt_gen(I["x"], self.lat, xtok, xts), xt_gen(I["ctx"], self.cx, xtok_c, xts_c)]
        for (st_, l_, jb) in self.jobs:
            gens.append(self.mlp_gen(st_, l_, S["KB%d" % jb], self.ps[4 + jb]))
        while gens:
            for g in list(gens):
                try:
                    next(g)
                except StopIteration:
                    gens.remove(g)
        fw.end_phase()

    def cast_gen(self, items):
        fw, I, S = self.fw, self.I, self.S
        stg = [fw.sb("stg%d" % i, [128, 2048], F32) for i in range(3)]
        stb = [fw.sb("stb%d" % i, [128, 2048], BF16) for i in range(3)]
        n = 0
        for (src, dst, K, N, l) in items:
            for rb in range(K // 128):
                for c0 in range(0, N, 2048):
                    cw = min(2048, N - c0)
                    a, b_ = stg[n % 3], stb[n % 3]
                    self.ld(a[:, :cw], I[src][l, rb * 128:(rb + 1) * 128, c0:c0 + cw])
                    self.cp("pool", b_[:, :cw], a[:, :cw])
                    self.stq(S[dst][l, rb * 128:(rb + 1) * 128, c0:c0 + cw], b_[:, :cw])
                    n += 1
                    yield

    def layer_cols(self, l):
        fw, I = self.fw, self.I
        if l > 0:
            fw.barrier()
            self.cols_st.close()
        self.cols_st = ExitStack()
        assert fw.phase_st is None
        fw.phase_st = self.cols_st
        nm = fw.sb("nmcol", [128, KC], F32)
        nf = fw.sb("nfcol", [128, KC], F32)
        self.colvec(nm[:], I["norm_mix"][l])
        self.colvec(nf[:], I["norm_ffn"][l])
        mc = self.modcol[l]
        self.stt("dve", self.A1[l][:], mc[:, 8:16, :], 1.0, nm.v(nm.h[:].unsqueeze(2).broadcast_to([128, KC, 2])), ALU.add, ALU.mult)
        self.stt("dve", self.A2[l][:], mc[:, 32:40, :], 1.0, nf.v(nf.h[:].unsqueeze(2).broadcast_to([128, KC, 2])), ALU.add, ALU.mult)
        self.gq = fw.sb("gq", [128, 1], F32)
        self.gk = fw.sb("gk", [128, 1], F32)
        self.ld(self.gq[:], I["q_norm"][l].rearrange("(p o) -> p o", o=1), allow_slow_non_contiguous=True)
        self.ld(self.gk[:], I["k_norm"][l].rearrange("(p o) -> p o", o=1), allow_slow_non_contiguous=True)
        grow = fw.sb("grow", [1, 2, 128], F32)
        self.ld(grow[:, 0, :], I["q_norm"][l].rearrange("(o n) -> o n", o=1))
        self.ld(grow[:, 1, :], I["k_norm"][l].rearrange("(o n) -> o n", o=1))
        gmax = fw.sb("gmax", [1, 2], F32)
        r, w = _rw([grow[:]], [gmax[:]])
        go, gi = gmax.h[:], grow.h[:]
        fw.op("dve", lambda h: h.tensor_reduce(go, gi, mybir.AxisListType.X, ALU.max, apply_absolute_value=True), r, w)
        nb = fw.sb("nb", [1, 2], F32)
        self.stt("dve", nb[:, 0:1], gmax[:, 0:1], -math.sqrt(128.0), gmax[:, 1:2], ALU.mult, ALU.mult)
        self.stt("dve", nb[:, 1:2], gmax[:, 0:1], -math.sqrt(128.0), gmax[:, 1:2], ALU.mult, ALU.mult)
        pn = self.ps[6]
        self.mm(pn[:, 0:2], self.onesf[0:1, :], nb[:], True, True)
        self.negB = fw.sb("negB", [128, 1], F32)
        self.cp("dve", self.negB[:], pn[:, 0:1])
        self.cw = fw.sb("cwcol", [128, 3, 12], F32)
        for tap in range(3):
            self.colvec(self.cw[:, tap, :], I["conv_w"][l, tap])
        self.cb = fw.sb("cbcol", [128, 12], F32)
        self.colvec(self.cb[:], I["conv_b"][l])
        self.hb = fw.sb("hbcol", [128, 4], F32)
        self.colvec(self.hb[:], I["hyena_bias"][l])
        fw.phase_st = None

    def rms_mod(self, xs, TC, A, Bm, j, out, T_):
        sq, sd, rstd, tmp = T_["sq"], T_["sd"], T_["rstd"], T_["tmp"]
        self.act(sq[:, :, :TC], xs[:, :, :TC], AF.Square)
        pS = self.ps[7]
        for c in range(KC):
            self.mm(pS[:, :TC], self.onesb[:], sq[:, c, :TC], c == 0, c == KC - 1)
        self.act(sd[:, :TC], pS[:, :TC], AF.Sqrt, bias=EPS, scale=1.0 / D)
        self.recip(rstd[:, :TC], sd[:, :TC])
        for c in range(KC):
            if Bm is None:
                self.stt("dve", out[:, c, :TC], xs[:, c, :TC], A(c, j), rstd[:, :TC], ALU.mult, ALU.mult)
            else:
                t = tmp[c % 2]
                self.stt("dve", t[:, :TC], xs[:, c, :TC], A(c, j), rstd[:, :TC], ALU.mult, ALU.mult)
                self.act(out[:, c, :TC], t[:, :TC], AF.Identity, bias=Bm(c, j))

    def rms_tiles(self, TC):
        fw = self.fw
        return {"sq": fw.sb("sq", [128, KC, TC], BF16), "sd": fw.sb("sd", [128, TC], F32),
                "rstd": fw.sb("rstd", [128, TC], F32), "tmp": [fw.sb("rtmp%d" % i, [128, TC], F32) for i in range(2)]}

    def load_w(self, dst, src, kchunks):
        for kc in range(kchunks):
            self.ld(dst[:, kc, :], src[kc * 128:(kc + 1) * 128, :])

    def attn_phase(self, l, last):
        fw, I, S = self.fw, self.I, self.S
        lat, cx = self.lat, self.cx
        fw.begin_phase()
        NK = lat.n_keys
        KT = fw.sb("KT", [128, NKV, NK], BF16)
        V = fw.sb("V", [128, NK // 128, NKV * HD], BF16)
        fw.begin_phase()
        wq = fw.sb("wqkv", [128, KC, 1536], BF16)
        for kc in range(KC):
            self.ld(wq[:, kc, :], S["wb_in"][l, kc * 128:(kc + 1) * 128, 0:1536])
        RT = self.rms_tiles(512)
        xs2 = [fw.sb("xs%d" % i, [128, KC, 512], F32) for i in range(2)]
        hT2 = [fw.sb("hT%d" % i, [128, KC, 512], BF16) for i in range(2)]
        rc2 = [fw.sb("rc%d" % i, [128, 512], F32) for i in range(2)]
        rs2 = [fw.sb("rs%d" % i, [128, 512], F32) for i in range(2)]
        sqh = [fw.sb("sqh%d" % i, [128, 512], BF16) for i in range(2)]
        qg = [fw.sb("qg%d" % i, [128, 512], BF16) for i in range(2)]
        sdh = [fw.sb("sdh%d" % i, [128, 512], F32) for i in range(2)]
        rsh = [fw.sb("rsh%d" % i, [128, 512], F32) for i in range(2)]
        t1 = [fw.sb("t1_%d" % i, [128, 512], F32) for i in range(2)]
        t2 = [fw.sb("t2_%d" % i, [128, 512], F32) for i in range(2)]
        qo = [fw.sb("qo%d" % i, [128, 512], BF16) for i in range(3)]
        mc = self.modcol[l]
        A1 = self.A1[l]
        hn = 0
        nchunk = 0
        for s in [cx, lat]:
            want_q = (s is lat) or (not last)
            TC = s.TC
            for ch in range(s.T // TC):
                t0 = ch * TC
                xs, hT = xs2[nchunk % 2], hT2[nchunk % 2]
                rc, rs = rc2[nchunk % 2], rs2[nchunk % 2]
                nchunk += 1
                self.ld(xs[:, :, :TC], s.XT[:, t0:t0 + TC].rearrange("(c p) t -> p c t", p=128))
                if s.rope:
                    self.ld(rc[:, :TC], I["k_ropec"][:, t0:t0 + TC])
                    self.ld(rs[:, :TC], I["k_ropes"][:, t0:t0 + TC])
                self.rms_mod(xs, TC, lambda c, j: A1[:, c, j:j + 1], lambda c, j: mc[:, c, j:j + 1], s.j, hT, RT)
                heads = ([("q", j) for j in range(NH)] if want_q else []) + [("k", 0), ("k", 1)]
                for (kind, j) in heads:
                    col0 = j * 128 if kind == "q" else D + j * 128
                    gcol = self.gq if kind == "q" else self.gk
                    i2 = hn % 2
                    hn += 1
                    p = self.nps(0, 4)
                    for kc in range(KC):
                        self.mm(p[:, :TC], wq[:, kc, col0:col0 + 128], hT[:, kc, :TC], kc == 0, kc == KC - 1)
                    self.act(sqh[i2][:, :TC], p[:, :TC], AF.Square)
                    self.act(qg[i2][:, :TC], p[:, :TC], AF.Identity, scale=gcol[:])
                    pa = self.ps[4 + i2]
                    self.mm(pa[:, :TC], self.onesb[:], sqh[i2][:, :TC], True, True)
                    self.act(sdh[i2][:, :TC], pa[:, :TC], AF.Sqrt, bias=EPS, scale=1.0 / HD)
                    self.recip(rsh[i2][:, :TC], sdh[i2][:, :TC])
                    if kind == "q":
                        qoi = qo[hn % 3]
                        dest = qoi[:, :TC]
                    else:
                        dest = KT[:, j, s.key_off + t0: s.key_off + t0 + TC]
                    if s.rope:
                        pb = self.ps[6]
                        self.mm(pb[:, :TC], self.rmat[:], qg[i2][:, :TC], True, True)
                        self.tt("dve", t1[i2][:, :TC], qg[i2][:, :TC], rc[:, :TC], ALU.mult)
                        self.tt("dve", t2[i2][:, :TC], pb[:, :TC], rs[:, :TC], ALU.mult)
                        self.tt("pool", t1[i2][:, :TC], t1[i2][:, :TC], t2[i2][:, :TC], ALU.add)
                        self.tt("dve", dest, t1[i2][:, :TC], rsh[i2][:, :TC], ALU.mult)
                    else:
                        self.tt("dve", dest, qg[i2][:, :TC], rsh[i2][:, :TC], ALU.mult)
                    if kind == "q":
                        self.stq(s.Qs[j, :, t0:t0 + TC], qoi[:, :TC])
                for tsub in range(TC // 128):
                    p = self.nps(0, 4)
                    for kc in range(KC):
                        self.mm(p[:, 0:256], hT[:, kc, tsub * 128:(tsub + 1) * 128], wq[:, kc, 1280:1536], kc == 0, kc == KC - 1)
                    self.cp("act", V[:, (s.key_off + t0) // 128 + tsub, :], p[:, 0:256])
        fw.end_phase()
        fw.begin_phase()
        qt3 = [fw.sb("qt%d" % i, [128, 512], BF16) for i in range(2)]
        pt3 = [fw.sb("pt%d" % i, [128, 2, 512], BF16) for i in range(4)]
        rl2 = [fw.sb("rl%d" % i, [128, 512], F32) for i in range(2)]
        ao2 = [fw.sb("ao%d" % i, [128, 512], BF16) for i in range(2)]
        SCALE = HD ** -0.5
        GRP = 4
        acc2s = [fw.sb("acc2_%d" % i, [128, 2, 512], BF16) for i in range(2)]
        accfs = [fw.sb("accf_%d" % i, [128, 512], BF16) for i in range(2)]
        SHIFT = -8.0
        pairs = []
        for s in ([lat] if last else [lat, cx]):
            for h in range(NH):
                for qc in range(s.T // s.TC):
                    for j in range(s.n_keys // 256):
                        pairs.append((s, h, qc, j))
        st = {}
        pend = []

        def emit_S(pi):
            s, h, qc, j = pairs[pi]
            TC = s.TC
            kvh = h // (NH // NKV)
            if j == 0:
                nq = st.get("nq", 0)
                st["nq"] = nq + 1
                qt = qt3[nq % 2]
                self.ld(qt[:, :TC], s.Qs[h, :, qc * TC:(qc + 1) * TC])
                st[("qt", s.name, h, qc)] = (qt, nq)
            qt, nq = st[("qt", s.name, h, qc)]
            for a in range(2):
                kt = 2 * j + a
                p = self.ps[(pi % 2) * 2 + a]
                self.mm(p[:, :TC], KT[:, kvh, kt * 128:(kt + 1) * 128], qt[:, :TC], True, True)

        def emit_rest(pi):
            s, h, qc, j = pairs[pi]
            TC = s.TC
            kvh = h // (NH // NKV)
            n_kt = s.n_keys // 128
            qt, nq = st[("qt", s.name, h, qc)]
            po = self.ps[4]
            pl = self.ps[5]
            pw = self.psw[pi % 2]
            pin = View(pw.h[:].rearrange("p (a n) -> p a n", a=2)[:, :, :TC], (self.ps[(pi % 2) * 2].res, self.ps[(pi % 2) * 2 + 1].res))
            pt = pt3[pi % 4]
            self.act(pt[:, :, :TC], pin, AF.Exp, bias=SHIFT, scale=SCALE)
            due = list(pend)
            del pend[:]
            for a in range(2):
                kt = 2 * j + a
                self.mm(po[:, :TC], V[:, kt, kvh * HD:(kvh + 1) * HD], pt[:, a, :TC], kt == 0, kt == n_kt - 1)
            for f_ in due:
                f_()
            npairs = n_kt // 2
            g0 = (j // GRP) * GRP
            gsz = min(GRP, npairs - g0)
            jj = j - g0
            ai = (st.get("na", 0)) % 2
            acc2, accf = acc2s[ai], accfs[ai]
            if gsz == 1:
                self.tt("dve", accf[:, :TC], pt[:, 0, :TC], pt[:, 1, :TC], ALU.add)
            elif jj == 0:
                st["ptprev"] = pt
            elif jj == 1:
                self.tt("dve", acc2[:, :, :TC], st["ptprev"][:, :, :TC], pt[:, :, :TC], ALU.add)
            else:
                self.tt("dve", acc2[:, :, :TC], acc2[:, :, :TC], pt[:, :, :TC], ALU.add)
            if jj == gsz - 1:
                if gsz > 1:
                    self.tt("dve", accf[:, :TC], acc2[:, 0, :TC], acc2[:, 1, :TC], ALU.add)
                first, lastg = (g0 == 0), (g0 + gsz == npairs)

                def emit_L(accf=accf, TC=TC, first=first, lastg=lastg):
                    self.mm(pl[:, :TC], self.onesb[:], accf[:, :TC], first, lastg)
                if lastg:
                    for f_ in pend:
                        f_()
                    del pend[:]
                    emit_L()
                else:
                    pend.append(emit_L)
                st["na"] = st.get("na", 0) + 1
            if j == n_kt // 2 - 1:
                rl, ao = rl2[nq % 2], ao2[nq % 2]
                self.act(rl[:, :TC], pl[:, :TC], AF.Ln)
                self.act(rl[:, :TC], rl[:, :TC], AF.Exp, scale=-1.0)
                self.tt("dve", ao[:, :TC], po[:, :TC], rl[:, :TC], ALU.mult)
                self.stq(s.AT[h * HD:(h + 1) * HD, qc * TC:(qc + 1) * TC], ao[:, :TC])

        bg = None
        if l == 0:
            items = []
            for ll in range(self.DEPTH):
                for (src, dst, K_, N_) in [("w_in", "wb_in", D, NPROJ), ("w_o_attn", "wb_oa", D, D), ("w_o_hyena", "wb_oh", HW, D),
                                           ("w_out", "wb_out", D, D), ("w_gate_up", "wb_gu", D, 2 * DFF), ("w_down", "wb_dn", DFF, D)]:
                    if not (src == "w_in" and ll == 0):
                        items.append((src, dst, K_, N_, ll))

            def chain():
                for x_ in self.spectrum_gen([jb for (_, _, jb) in self.jobs], self.ps[6], self.ps[7]):
                    yield
                for x_ in self.cast_gen(items):
                    yield
            bg = chain()
        import os
        if bg is not None and os.environ.get("BG_FIRST"):
            for x_ in bg:
                pass
            bg = None
        emit_S(0)
        for pi in range(len(pairs)):
            if pi + 1 < len(pairs):
                emit_S(pi + 1)
            emit_rest(pi)
            if bg is not None and pi % 3 == 2:
                if next(bg, "done") == "done":
                    bg = None
        if bg is not None:
            for x_ in bg:
                pass
        fw.end_phase()
        fw.end_phase()

    def ug_phase(self, l, last):
        fw, S = self.fw, self.S
        fw.begin_phase()
        NW = NPROJ - 1536
        w = fw.sb("wug", [128, KC, NW], BF16)
        for kc in range(KC):
            self.ld(w[:, kc, :], S["wb_in"][l, kc * 128:(kc + 1) * 128, 1536:NPROJ])
        RT = self.rms_tiles(512)
        xs2 = [fw.sb("xs%d" % i, [128, KC, 512], F32) for i in range(2)]
        hT2 = [fw.sb("hT%d" % i, [128, KC, 512], BF16) for i in range(2)]
        us2 = [fw.sb("us%d" % i, [128, 4, 512], F32) for i in range(2)]
        gs2 = [fw.sb("gs%d" % i, [128, 4, 512], BF16) for i in range(2)]
        mc, A1 = self.modcol[l], self.A1[l]
        n = 0
        nu = 0
        ng = 0
        for s in ([self.lat] if last else [self.lat, self.cx]):
            TC = s.TC
            for ch in range(s.T // TC):
                t0 = ch * TC
                xs, hT = xs2[n % 2], hT2[n % 2]
                n += 1
                self.ld(xs[:, :, :TC], s.XT[:, t0:t0 + TC].rearrange("(c p) t -> p c t", p=128))
                self.rms_mod(xs, TC, lambda c, j: A1[:, c, j:j + 1], lambda c, j: mc[:, c, j:j + 1], s.j, hT, RT)
                for o4 in range(3):
                    us = us2[nu % 2]
                    nu += 1
                    for oi in range(4):
                        oc = o4 * 4 + oi
                        p = self.nps(0, 6)
                        for kc in range(KC):
                            self.mm(p[:, :TC], w[:, kc, oc * 128:(oc + 1) * 128], hT[:, kc, :TC], kc == 0, kc == KC - 1)
                        self.cp("act" if oi % 2 == 0 else "dve", us[:, oi, :TC], p[:, :TC])
                    self.stq(s.U[o4 * 512:(o4 + 1) * 512, t0:t0 + TC].rearrange("(c p) t -> p c t", p=128), us[:, :, :TC])
                for o4 in range(4):
                    gs = gs2[ng % 2]
                    ng += 1
                    for oi in range(4):
                        oc = 12 + o4 * 4 + oi
                        p = self.nps(0, 6)
                        for kc in range(KC):
                            self.mm(p[:, :TC], w[:, kc, oc * 128:(oc + 1) * 128], hT[:, kc, :TC], kc == 0, kc == KC - 1)
                        self.act(gs[:, oi, :TC], p[:, :TC], AF.Sigmoid)
                    self.stq(s.G[o4 * 512:(o4 + 1) * 512, t0:t0 + TC].rearrange("(c p) t -> p c t", p=128), gs[:, :, :TC])
        fw.end_phase()

    def conv3(self, s, base_chunk, cc, t0, TCH, ub, out):
        T = s.T
        row0 = (base_chunk + cc) * 128
        lo, hi = t0 - 1, t0 + TCH + 1
        a = 0
        if lo < 0:
            self.memset("pool", ub[:, 0:1], 0.0)
            lo, a = 0, 1
        b = TCH + 2
        if hi > T:
            self.memset("pool", ub[:, TCH + 1:TCH + 2], 0.0)
            hi, b = T, TCH + 1
        self.ld(ub[:, a:b], s.U[row0:row0 + 128, lo:hi])
        k = base_chunk + cc
        cw, cb = self.cw, self.cb
        self.act(out[:, :TCH], ub[:, 1:TCH + 1], AF.Identity, bias=cb[:, k:k + 1], scale=cw[:, 1, k:k + 1])
        self.stt("dve", out[:, :TCH], ub[:, 0:TCH], cw[:, 0, k:k + 1], out[:, :TCH], ALU.mult, ALU.add)
        self.stt("dve", out[:, :TCH], ub[:, 2:TCH + 2], cw[:, 2, k:k + 1], out[:, :TCH], ALU.mult, ALU.add)

    def hyena_a(self, s, l):
        fw = self.fw
        fw.begin_phase()
        TCH = min(2048, s.T)
        ub2 = [fw.sb("ub%d" % i, [128, TCH + 2], F32) for i in range(4)]
        cx1 = [fw.sb("cx1_%d" % i, [128, TCH], F32) for i in range(2)]
        cv = [fw.sb("cv_%d" % i, [128, TCH], F32) for i in range(2)]
        vb = [fw.sb("vb_%d" % i, [128, TCH], BF16) for i in range(2)]
        n = 0
        for cc in range(4):
            for tch in range(s.T // TCH):
                t0 = tch * TCH
                i2 = n % 2
                self.conv3(s, 4, cc, t0, TCH, ub2[(2 * n) % 4], cx1[i2])
                self.conv3(s, 8, cc, t0, TCH, ub2[(2 * n + 1) % 4], cv[i2])
                n += 1
                self.tt("dve", cv[i2][:], cv[i2][:], cx1[i2][:], ALU.mult)
                self.cp("act", vb[i2][:], cv[i2][:])
                self.stq(s.VV[cc * 128:(cc + 1) * 128, t0:t0 + TCH], cv[i2][:])
                self.stq(s.VB[cc * 128:(cc + 1) * 128, t0:t0 + TCH], vb[i2][:])
        fw.end_phase()

    def hyena_c(self, s, l):
        fw = self.fw
        fw.begin_phase()
        TCH = min(2048, s.T)
        ub2 = [fw.sb("ub%d" % i, [128, TCH + 2], F32) for i in range(2)]
        cx0 = [fw.sb("cx0_%d" % i, [128, TCH], F32) for i in range(2)]
        yr = [fw.sb("yr_%d" % i, [128, TCH], F32) for i in range(2)]
        vv = [fw.sb("vv_%d" % i, [128, TCH], F32) for i in range(2)]
        hy = [fw.sb("hy_%d" % i, [128, TCH], BF16) for i in range(2)]
        n = 0
        for cc in range(4):
            for tch in range(s.T // TCH):
                t0 = tch * TCH
                i2 = n % 2
                n += 1
                self.conv3(s, 0, cc, t0, TCH, ub2[i2], cx0[i2])
                self.ld(yr[i2][:], s.YB[cc * 128:(cc + 1) * 128, t0:t0 + TCH])
                self.ld(vv[i2][:], s.VV[cc * 128:(cc + 1) * 128, t0:t0 + TCH])
                self.stt("dve", yr[i2][:], vv[i2][:], self.hb[:, cc:cc + 1], yr[i2][:], ALU.mult, ALU.add)
                self.tt("pool", hy[i2][:], yr[i2][:], cx0[i2][:], ALU.mult)
                self.stq(s.HY[cc * 128:(cc + 1) * 128, t0:t0 + TCH], hy[i2][:])
        fw.end_phase()

    def cmul(self, out, pin, tre, tim, conj, tmp1, tmp2):
        self.tt("dve", tmp1[:], pin, tre, ALU.mult)
        self.tt("dve", tmp2[:], pin, tim, ALU.mult)
        if not conj:
            self.tt("dve", out[:, :, 0, :], tmp1[:, :, 0, :], tmp2[:, :, 1, :], ALU.subtract)
            self.tt("pool", out[:, :, 1, :], tmp2[:, :, 0, :], tmp1[:, :, 1, :], ALU.add)
        else:
            self.tt("dve", out[:, :, 0, :], tmp1[:, :, 0, :], tmp2[:, :, 1, :], ALU.add)
            self.tt("pool", out[:, :, 1, :], tmp1[:, :, 1, :], tmp2[:, :, 0, :], ALU.subtract)

    def p4(self, p):
        return p.v(p.h[:].rearrange("p (c r k) -> p c r k", c=2, r=2))

    def tw_b(self, idx):
        return self.tw.v(self.tw.h[:, idx, :].unsqueeze(1).unsqueeze(1).broadcast_to([128, 2, 2, 128]))

    def fft_s1(self, v2, nK, c0, Bt, tmp1, tmp2, pA):
        ft = self.ft
        for c in range(2):
            self.mm(pA[:, c * 256:(c + 1) * 256], v2[0:nK, c0 + c, :],
                    ft.v(ft.h[0:nK, 0:2, :]), c == 0, c == 1, skip=True)
        self.cmul(Bt, self.p4(pA), self.tw_b(0), self.tw_b(1), False, tmp1, tmp2)

    def fft_s2(self, Bt, pX):
        ft = self.ft
        self.mm(pX[:], ft[:, 0, :], Bt[:], True, False, skip=True)
        pX4 = self.p4(pX)
        self.mm(View(pX4.ap[:, :, 0, :], pX.res), ft[:, 3, :], Bt[:, :, 1, :], False, False, skip=True)
        self.mm(View(pX4.ap[:, :, 1, :], pX.res), ft[:, 1, :], Bt[:, :, 0, :], False, True, skip=True)

    def fft_conv(self, s):
        fw = self.fw
        fw.begin_phase()
        T = s.T
        nK = T // 128
        ft = self.ft
        v2s = [fw.sb("v2_%d" % i, [max(nK, 2), 128, 128], BF16) for i in range(2)]
        y2 = fw.sb("y2", [max(nK, 2), 128, 128], F32)
        Bt = [fw.sb("Bt%d" % i, [128, 2, 2, 128], BF16) for i in range(2)]
        Yt = [fw.sb("Yt%d" % i, [128, 2, 2, 128], BF16) for i in range(2)]
        Ut = [fw.sb("Ut%d" % i, [128, 2, 2, 128], BF16) for i in range(2)]
        kf = [fw.sb("kf%d" % i, [128, 2, 2, 128], F32) for i in range(3)]
        tA = [[fw.sb("cmA%d_%d" % (k, i), [128, 2, 2, 128], F32) for i in range(2)] for k in range(3)]
        tB = [[fw.sb("cmB%d_%d" % (k, i), [128, 2, 2, 128], F32) for i in range(2)] for k in range(3)]
        NG = 64
        for cc in range(4):
            v2 = v2s[cc % 2]
            for q4 in range(4):
                self.ld(v2[0:nK, q4 * 32:(q4 + 1) * 32, :],
                        s.VB[cc * 128 + q4 * 32: cc * 128 + (q4 + 1) * 32, :].rearrange("ch (n1 n2) -> n1 ch n2", n2=128))
            for it in range(NG + 3):
                g = it
                if 0 <= g < NG:
                    i2 = g % 2
                    self.ld(kf[g % 3][:], s.KF[:, cc * 128 + g * 2: cc * 128 + g * 2 + 2, :, :])
                    self.fft_s1(v2, nK, g * 2, Bt[i2], tA[0][i2], tB[0][i2], self.ps[i2])
                g = it - 1
                if 0 <= g < NG:
                    i2 = g % 2
                    kfi = kf[g % 3]
                    pX = self.ps[2 + i2]
                    self.fft_s2(Bt[i2], pX)
                    kre = kfi.v(kfi.h[:, :, 0:1, :].broadcast_to([128, 2, 2, 128]))
                    kim = kfi.v(kfi.h[:, :, 1:2, :].broadcast_to([128, 2, 2, 128]))
                    self.cmul(Yt[i2], self.p4(pX), kre, kim, False, tA[1][i2], tB[1][i2])
                g = it - 2
                if 0 <= g < NG:
                    i2 = g % 2
                    pU = self.ps[4 + i2]
                    for c in range(2):
                        self.mm(pU[:, c * 256:(c + 1) * 256], Yt[i2][:, c, 0, :], ft.v(ft.h[:, 2:4, :]), c == 0, False, skip=True)
                        self.mm(pU[:, c * 256:(c + 1) * 256], Yt[i2][:, c, 1, :], ft.v(ft.h[:, 1:3, :]), False, c == 1, skip=True)
                    self.cmul(Ut[i2], self.p4(pU), self.tw_b(0), self.tw_b(1), True, tA[2][i2], tB[2][i2])
                g = it - 3
                if 0 <= g < NG:
                    i2 = g % 2
                    pY = self.ps[6 + i2]
                    pYv = View(pY.h[0:nK, 0:256].rearrange("p (c k) -> p c k", c=2), pY.res)
                    self.mm(pYv, ft[:, 0, 0:nK], Ut[i2][:, :, 0, :], True, False, skip=True)
                    self.mm(pYv, ft[:, 1, 0:nK], Ut[i2][:, :, 1, :], False, True, skip=True)
                    self.cp("act", y2[0:nK, g * 2:g * 2 + 2, :], pYv)
            for q4 in range(4):
                self.stq(s.YB[cc * 128 + q4 * 32: cc * 128 + (q4 + 1) * 32, :].rearrange("ch (m1 m2) -> m1 ch m2", m2=128),
                         y2[0:nK, q4 * 32:(q4 + 1) * 32, :])
        fw.end_phase()

    def mlp_gen(self, s, l, KB, pbank):
        fw, I = self.fw, self.I
        L = s.T
        TC = min(512, L)
        w1 = fw.sb("fw1", [FEMB, FHID], F32)
        w2 = fw.sb("fw2", [FHID, FHID], F32)
        w3 = fw.sb("fw3", [FHID, FHID], F32)
        w4 = fw.sb("fw4", [FHID, 2 * HW], F32)
        self.ld(w1[:], I["filt_w1"][l])
        self.ld(w2[:], I["filt_w2"][l])
        self.ld(w3[:], I["filt_w3"][l])
        self.ld(w4[:], I["filt_w4"][l])
        fq = fw.sb("fq", [FHID, 3], F32)
        fb = fw.sb("fb", [FHID, 3], F32)
        self.ld(fq[:], I["filt_freq"][l].rearrange("i p -> p i"), allow_slow_non_contiguous=True)
        for i, nm in enumerate(["filt_b1", "filt_b2", "filt_b3"]):
            self.ld(fb[:, i:i + 1], I[nm][l].rearrange("(p o) -> p o", o=1), allow_slow_non_contiguous=True)
        fsc = fw.sb("fsc", [FHID, 3], F32)
        fbc = fw.sb("fbc", [FHID, 3], F32)
        self.ts("dve", fsc[:], fq[:], 1.0 / 3.0, 0.0, ALU.mult, ALU.add)
        self.tt("dve", fbc[:], fsc[:], fb[:], ALU.mult)
        zt = self.zt
        z0, z1 = L, NFFT - L + 1
        for cc in range(4):
            p0 = z0
            while p0 < z1:
                n = min(2048, z1 - p0)
                self.stq(KB[cc * 128:(cc + 1) * 128, p0:p0 + n], zt[:, :n], allow_slow_non_contiguous=True)
                p0 += n
            yield
        zin = [fw.sb("zin%d" % i, [FEMB, TC], F32) for i in range(2)]
        hs = fw.sb("hs", [FHID, TC], F32)
        s2 = fw.sb("s2", [FHID, TC], F32)
        hh = [fw.sb("hh%d" % k, [FHID, TC], F32) for k in range(3)]
        dct = [fw.sb("dct%d" % i, [128, TC], F32) for i in range(2)]
        kr = [fw.sb("kr%d" % i, [128, TC], BF16) for i in range(2)]
        n = 0
        nd = 0
        p = pbank
        for d_ in range(2):
            for ch in range(L // TC):
                t0 = ch * TC
                i2 = n % 2
                n += 1
                self.ld(zin[i2][:], s.z[d_, :, t0:t0 + TC])
                cur = zin[i2]
                curK = FEMB
                for k, wk in enumerate([w1, w2, w3]):
                    self.mm(p[0:FHID, :TC], wk[0:curK, :], cur[0:curK, :], True, True)
                    self.act(hs[:], p[0:FHID, :TC], AF.Sin, bias=fbc[:, k:k + 1], scale=fsc[:, k:k + 1])
                    self.tt("dve", s2[:], hs[:], hs[:], ALU.mult)
                    self.ts("dve", s2[:], s2[:], -4.0, 3.0, ALU.mult, ALU.add)
                    self.tt("dve", hh[k][:], s2[:], hs[:], ALU.mult)
                    cur = hh[k]
                    curK = FHID
                    yield
                for oc in range(4):
                    self.mm(p[:, :TC], w4[:, d_ * HW + oc * 128: d_ * HW + (oc + 1) * 128], cur[:], True, True)
                    dc, krr = dct[nd % 2], kr[nd % 2]
                    nd += 1
                    self.ld(dc[:], s.dec[d_, oc * 128:(oc + 1) * 128, t0:t0 + TC])
                    self.tt("dve", krr[:], p[:, :TC], dc[:], ALU.mult)
                    if d_ == 0:
                        self.stq(KB[oc * 128:(oc + 1) * 128, t0:t0 + TC], krr[:])
                    else:
                        pos = NFFT - L + 1 + t0
                        nn = TC if t0 + TC < L else TC - 1
                        self.stq(KB[oc * 128:(oc + 1) * 128, pos:pos + nn], krr[:, :nn])
                    yield

    def spectrum_gen(self, jobs, pA, pX):
        fw, S = self.fw, self.S
        k2s = [fw.sb("k2_%d" % i, [128, 32, 128], BF16) for i in range(2)]
        Bt = [fw.sb("Bt%d" % i, [128, 2, 2, 128], BF16) for i in range(2)]
        tmp1 = [fw.sb("cm1_%d" % i, [128, 2, 2, 128], F32) for i in range(2)]
        tmp2 = [fw.sb("cm2_%d" % i, [128, 2, 2, 128], F32) for i in range(2)]
        xo = [fw.sb("xo%d" % i, [128, 2, 2, 128], F32) for i in range(3)]
        nq = 0
        for jb in jobs:
            KB, KF = S["KB%d" % jb], S["KF%d" % jb]
            for q in range(16):
                k2 = k2s[nq % 2]
                nq += 1
                self.ld(k2[:], KB[q * 32:(q + 1) * 32, :].rearrange("ch (n1 n2) -> n1 ch n2", n2=128))
                for it in range(17):
                    g = it
                    if 0 <= g < 16:
                        i2 = g % 2
                        self.fft_s1(k2, 128, g * 2, Bt[i2], tmp1[i2], tmp2[i2], pA)
                    g = it - 1
                    if 0 <= g < 16:
                        i2 = g % 2
                        xoi = xo[g % 3]
                        self.fft_s2(Bt[i2], pX)
                        self.act(xoi[:], self.p4(pX), AF.Identity, scale=1.0 / NFFT)
                        self.stq(KF[:, q * 32 + g * 2: q * 32 + g * 2 + 2, :, :], xoi[:])
                    yield

    def merge_phase(self, l, streams):
        fw, S = self.fw, self.S
        fw.begin_phase()
        woa = fw.sb("woa", [128, KC, D], BF16)
        woh = fw.sb("woh", [128, 4, D], BF16)
        wout = fw.sb("wout", [128, KC, D], BF16)
        self.load_w(woa, S["wb_oa"][l], KC)
        self.load_w(woh, S["wb_oh"][l], 4)
        self.load_w(wout, S["wb_out"][l], KC)
        xs2 = [fw.sb("xs%d" % i, [128, KC, 512], F32) for i in range(2)]
        at2 = [fw.sb("at%d" % i, [128, KC, 512], BF16) for i in range(2)]
        hy2 = [fw.sb("hy%d" % i, [128, 4, 512], BF16) for i in range(2)]
        g2 = [fw.sb("g%d" % i, [128, 16, 512], BF16) for i in range(2)]
        mg = fw.sb("mg", [128, KC, 512], BF16)
        m1 = [fw.sb("m1_%d" % i, [128, 512], F32) for i in range(2)]
        m2 = [fw.sb("m2_%d" % i, [128, 512], F32) for i in range(2)]
        mc = self.modcol[l]
        n = 0
        no = 0
        for s in streams:
            TC = s.TC
            for ch in range(s.T // TC):
                t0 = ch * TC
                xs, at, hy, g = xs2[n % 2], at2[n % 2], hy2[n % 2], g2[n % 2]
                n += 1
                self.ld(xs[:, :, :TC], s.XT[:, t0:t0 + TC].rearrange("(c p) t -> p c t", p=128))
                self.ld(at[:, :, :TC], s.AT[:, t0:t0 + TC].rearrange("(c p) t -> p c t", p=128))
                self.ld(hy[:, :, :TC], s.HY[:, t0:t0 + TC].rearrange("(c p) t -> p c t", p=128))
                self.ld(g[:, :, :TC], s.G[:, t0:t0 + TC].rearrange("(c p) t -> p c t", p=128))
                for oc in range(KC):
                    i2 = no % 2
                    no += 1
                    pa, pb = self.ps[i2], self.ps[2 + i2]
                    for kc in range(KC):
                        self.mm(pa[:, :TC], woa[:, kc, oc * 128:(oc + 1) * 128], at[:, kc, :TC], kc == 0, kc == KC - 1)
                    for kc in range(4):
                        self.mm(pb[:, :TC], woh[:, kc, oc * 128:(oc + 1) * 128], hy[:, kc, :TC], kc == 0, kc == 3)
                    self.tt("dve", m1[i2][:, :TC], pa[:, :TC], g[:, oc, :TC], ALU.mult)
                    self.tt("dve", m2[i2][:, :TC], pb[:, :TC], g[:, 8 + oc, :TC], ALU.mult)
                    self.tt("pool", mg[:, oc, :TC], m1[i2][:, :TC], m2[i2][:, :TC], ALU.add)
                for oc in range(KC):
                    p = self.ps[4 + oc % 4]
                    for kc in range(KC):
                        self.mm(p[:, :TC], wout[:, kc, oc * 128:(oc + 1) * 128], mg[:, kc, :TC], kc == 0, kc == KC - 1)
                    self.stt("dve", xs[:, oc, :TC], p[:, :TC], mc[:, 16 + oc, s.j:s.j + 1], xs[:, oc, :TC], ALU.mult, ALU.add)
                self.stq(s.XT[:, t0:t0 + TC].rearrange("(c p) t -> p c t", p=128), xs[:, :, :TC])
        fw.end_phase()

    def ffn_phase(self, l, streams):
        fw, S = self.fw, self.S
        fw.begin_phase()
        TC = 256
        wgu = fw.sb("wgu", [128, KC, 2 * DFF], BF16)
        wdn = fw.sb("wdn", [128, FC, D], BF16)
        self.load_w(wgu, S["wb_gu"][l], KC)
        self.load_w(wdn, S["wb_dn"][l], FC)
        RT = self.rms_tiles(TC)
        xs2 = [fw.sb("xs%d" % i, [128, KC, TC], F32) for i in range(2)]
        h2 = fw.sb("h2", [128, KC, TC], BF16)
        sg = [fw.sb("sg%d" % i, [128, TC], F32) for i in range(2)]
        sT = fw.sb("sT", [128, FC, TC], BF16)
        mc, A2 = self.modcol[l], self.A2[l]
        n = 0
        nj = 0
        for s in streams:
            for ch in range(s.T // TC):
                t0 = ch * TC
                xs = xs2[n % 2]
                n += 1
                self.ld(xs[:], s.XT[:, t0:t0 + TC].rearrange("(c p) t -> p c t", p=128))
                self.rms_mod(xs, TC, lambda c, j: A2[:, c, j:j + 1], lambda c, j: mc[:, 24 + c, j:j + 1], s.j, h2, RT)
                for j2 in range(FC):
                    i2 = nj % 2
                    nj += 1
                    pg, pu = self.ps[i2], self.ps[2 + i2]
                    for kc in range(KC):
                        self.mm(pg[:, :TC], wgu[:, kc, j2 * 128:(j2 + 1) * 128], h2[:, kc, :], kc == 0, kc == KC - 1)
                    for kc in range(KC):
                        self.mm(pu[:, :TC], wgu[:, kc, DFF + j2 * 128: DFF + (j2 + 1) * 128], h2[:, kc, :], kc == 0, kc == KC - 1)
                    self.act(sg[i2][:], pg[:, :TC], AF.Silu)
                    self.tt("dve", sT[:, j2, :], sg[i2][:], pu[:, :TC], ALU.mult)
                for oc in range(KC):
                    p = self.ps[4 + oc % 3]
                    for j2 in range(FC):
                        self.mm(p[:, :TC], wdn[:, j2, oc * 128:(oc + 1) * 128], sT[:, j2, :], j2 == 0, j2 == FC - 1)
                    self.stt("dve", xs[:, oc, :], p[:, :TC], mc[:, 40 + oc, s.j:s.j + 1], xs[:, oc, :], ALU.mult, ALU.add)
                self.stq(s.XT[:, t0:t0 + TC].rearrange("(c p) t -> p c t", p=128), xs[:])
        fw.end_phase()

    def final_phase(self):
        fw = self.fw
        s = self.lat
        fw.begin_phase()
        TC = 512
        RT = self.rms_tiles(TC)
        xs2 = [fw.sb("xs%d" % i, [128, KC, TC], F32) for i in range(2)]
        yT = fw.sb("yT", [128, KC, TC], F32)
        yt2 = [fw.sb("ytok%d" % i, [128, 4, D], F32) for i in range(2)]
        nf = self.nfin
        for ch in range(s.T // TC):
            t0 = ch * TC
            xs, yt = xs2[ch % 2], yt2[ch % 2]
            self.ld(xs[:], s.XT[:, t0:t0 + TC].rearrange("(c p) t -> p c t", p=128))
            self.rms_mod(xs, TC, lambda c, j: nf[:, c:c + 1], None, 0, yT, RT)
            for j in range(4):
                for half in range(2):
                    p = self.nps(0, 4)
                    for c4 in range(4):
                        c = half * 4 + c4
                        self.tr(p[:, c4 * 128:(c4 + 1) * 128], yT[:, c, j * 128:(j + 1) * 128], self.ident[:])
                    self.cp("act" if half == 0 else "dve", yt[:, j, half * 512:(half + 1) * 512], p[:])
            self.stq(self.out[t0:t0 + TC, :].rearrange("(j p) f -> p j f", p=128), yt[:])
        fw.end_phase()


def host_consts(SEQ):
    K = {}
    K["k_ident"] = np.eye(128, dtype=np.float32)
    r = np.zeros((128, 128), np.float32)
    for i in range(64):
        r[2 * i + 1, 2 * i] = -1.0
        r[2 * i, 2 * i + 1] = 1.0
    K["k_rmat"] = r
    a = np.arange(128, dtype=np.float64)
    ang = -2.0 * np.pi * np.outer(a, a) / 128.0
    Fr, Fi = np.cos(ang), np.sin(ang)
    K["k_ft"] = np.ascontiguousarray(np.stack([Fr, Fi, Fr, -Fi], axis=1)).astype(np.float32)
    angt = -2.0 * np.pi * np.outer(a, a) / NFFT
    K["k_tw"] = np.ascontiguousarray(np.stack([np.cos(angt), np.sin(angt)], axis=1)).astype(np.float32)
    GRID_W = 64
    rows = SEQ // GRID_W
    row = np.repeat(np.arange(rows, dtype=np.float32), GRID_W)
    col = np.tile(np.arange(GRID_W, dtype=np.float32), rows)
    inv_freq = (np.float32(10000.0) ** (-np.arange(0, 64, 2, dtype=np.float32) / np.float32(64))).astype(np.float32)
    angr = np.concatenate([row[:, None] * inv_freq, col[:, None] * inv_freq], axis=-1).astype(np.float32)
    cs, sn = np.cos(angr), np.sin(angr)
    K["k_ropec"] = np.ascontiguousarray(np.repeat(cs, 2, axis=1).T).astype(np.float32)
    K["k_ropes"] = np.ascontiguousarray(np.repeat(sn, 2, axis=1).T).astype(np.float32)

    def ztab(L):
        t = np.linspace(0.0, 1.0, L, dtype=np.float32)[:, None]
        w = (np.float32(2.0 * math.pi / L) * np.arange(L, dtype=np.float32))[:, None]
        f = np.linspace(1e-4, 15.0, 16, dtype=np.float32)[None, :]
        z = np.concatenate([t, np.cos(f * w), -np.sin(f * w)], axis=-1).astype(np.float32)
        max_decay = math.log(1e-2) / 0.3
        min_decay = math.log(1e-2) / 1.5
        deltas = np.abs(np.linspace(min_decay, max_decay, HW, dtype=np.float32))
        dec = np.exp(-t * deltas).astype(np.float32)
        zz = np.stack([z.T, z[::-1].T], axis=0)
        dd = np.stack([dec.T, dec[::-1].T], axis=0)
        return np.ascontiguousarray(zz).astype(np.float32), np.ascontiguousarray(dd).astype(np.float32)

    K["k_z_lat"], K["k_dec_lat"] = ztab(SEQ)
    K["k_z_ctx"], K["k_dec_ctx"] = ztab(CTX)
    return K


_NC_CACHE = {}


def run_cores(inputs, n_cores, dbg=None):
    x = np.asarray(inputs["x"], dtype=np.float32)
    SEQ = x.shape[1]
    DEPTH = np.asarray(inputs["w_mod"]).shape[0]
    key = (SEQ, DEPTH, tuple(sorted(dbg or [])))
    if key not in _NC_CACHE:
        nc = bass.Bass("TRN2", target_bir_lowering=False)
        b = Builder(nc, SEQ, DEPTH, dbg=dbg)
        b.build()
        _NC_CACHE[key] = nc
    nc = _NC_CACHE[key]
    K = host_consts(SEQ)
    shared = {k: np.ascontiguousarray(np.asarray(v, dtype=np.float32)) for k, v in inputs.items()
              if k not in ("x", "c", "ctx")}
    shared.update(K)
    in_maps = []
    for b_ in range(n_cores):
        m = dict(shared)
        m["x"] = np.ascontiguousarray(x[b_])
        m["c"] = np.ascontiguousarray(np.asarray(inputs["c"], dtype=np.float32)[b_])
        m["ctx"] = np.ascontiguousarray(np.asarray(inputs["ctx"], dtype=np.float32)[b_])
        in_maps.append(m)
    res = run_bass_kernel_spmd(nc, in_maps, core_ids=list(range(n_cores)))
    return res


def kernel(**inputs):
    res = run_cores(inputs, 8)
    out = np.stack([np.asarray(r["out"], dtype=np.float32) for r in res.results], axis=0)
    return out
```

```python
import math
from contextlib import ExitStack
import numpy as np
import concourse.bass as bass
import concourse.mybir as mybir
from concourse.bass_utils import run_bass_kernel_spmd

F32 = mybir.dt.float32
BF16 = mybir.dt.bfloat16
AF = mybir.ActivationFunctionType
ALU = mybir.AluOpType

D = 1024
KC = 8
NH = 8
NKV = 2
HD = 128
HW = 512
NPROJ = 5120
DFF = 2816
FC = 22
NMOD = 6
CTX = 256
NFFT = 16384
EPS = 1e-6
FEMB = 33
FHID = 64


class Res:
    __slots__ = ("name", "w", "r")

    def __init__(self, name=""):
        self.name = name
        self.w = None
        self.r = []


class View:
    __slots__ = ("ap", "res")

    def __init__(self, ap, res):
        self.ap = ap
        self.res = res


class Tl:
    def __init__(self, h, name):
        self.h = h
        self.res = Res(name)

    def __getitem__(self, k):
        return View(self.h[k], self.res)

    def v(self, ap):
        return View(ap, self.res)


def DV(ap):
    return View(ap, None)


class Eng:
    def __init__(self, name):
        self.name = name
        self.q = []
        self.sem = None
        self.cnt = 0
        self.seen = {}
        self.dsems = []
        self.dnext = 0


class FW:
    def __init__(self, nc, ndma=8):
        self.nc = nc
        self.st = ExitStack()
        self.E = {}
        for nm in ["pe", "act", "dve", "pool", "sp"]:
            e = Eng(nm)
            e.sem = self.st.enter_context(nc.semaphore("cs_" + nm))
            self.E[nm] = e
        for nm in ["sp", "pool", "act"]:
            e = self.E[nm]
            for i in range(ndma):
                s = self.st.enter_context(nc.semaphore("ds_%s%d" % (nm, i)))
                e.dsems.append([s, 0])
        self.n_ops = 0
        self.uid = 0
        self.phase_st = None
        self.stack = []

    def _alloc(self, name, shape, dt, psum):
        self.uid += 1
        nm = "%s_%d" % (name, self.uid)
        st = self.phase_st if self.phase_st is not None else self.st
        if psum:
            h = st.enter_context(self.nc.psum_tensor(nm, list(shape), dt))
        else:
            h = st.enter_context(self.nc.sbuf_tensor(nm, list(shape), dt))
        return Tl(h, nm)

    def sb(self, name, shape, dt):
        return self._alloc(name, shape, dt, False)

    def ps(self, name, shape, dt):
        return self._alloc(name, shape, dt, True)

    def _wait(self, e, deps):
        need = {}
        for d in deps:
            if d is None:
                continue
            sem, val, owner = d
            if owner == e.name and e.name == "pe":
                continue
            k = id(sem)
            if e.seen.get(k, 0) >= val:
                continue
            if k not in need or need[k][1] < val:
                need[k] = (sem, val)
        for k, (sem, val) in need.items():
            e.seen[k] = val
            e.q.append(lambda h, sem=sem, val=val: h.wait_ge(sem, val))

    def _deps(self, reads, writes):
        deps = []
        for r in reads:
            deps.append(r.w)
        for w in writes:
            deps.append(w.w)
            deps.extend(w.r)
        return deps

    def _commit(self, tok, reads, writes):
        for r in reads:
            r.r.append(tok)
        for w in writes:
            w.w = tok
            w.r = []

    def op(self, eng, fn, reads=(), writes=()):
        e = self.E[eng]
        self._wait(e, self._deps(reads, writes))
        e.cnt += 1
        sem = e.sem
        e.q.append(lambda h, fn=fn, sem=sem: fn(h).then_inc(sem, 1))
        tok = (sem, e.cnt, e.name)
        self._commit(tok, reads, writes)
        self.n_ops += 1
        return tok

    def dma(self, eng, out, in_, reads=(), writes=(), **kw):
        e = self.E[eng]
        self._wait(e, self._deps(reads, writes))
        slot = e.dsems[e.dnext]
        e.dnext = (e.dnext + 1) % len(e.dsems)
        sem, val = slot
        if val > 0 and e.seen.get(id(sem), 0) < val:
            e.seen[id(sem)] = val
            e.q.append(lambda h, sem=sem, val=val: h.wait_ge(sem, val))
        slot[1] = val + 16
        e.q.append(lambda h, out=out, in_=in_, sem=sem, kw=kw:
                   h.dma_start(out=out, in_=in_, **kw).then_inc(sem, 16))
        tok = (sem, val + 16, "dma_" + e.name)
        self._commit(tok, reads, writes)
        self.n_ops += 1
        return tok

    def barrier(self):
        toks = []
        for e in self.E.values():
            if e.cnt > 0:
                toks.append((e.sem, e.cnt, e.name))
            for sem, val in e.dsems:
                if val > 0:
                    toks.append((sem, val, "dma_" + e.name))
        for e in self.E.values():
            need = {}
            for sem, val, owner in toks:
                if owner == e.name:
                    continue
                k = id(sem)
                if e.seen.get(k, 0) >= val:
                    continue
                need[k] = (sem, val)
            for k, (sem, val) in need.items():
                e.seen[k] = val
                e.q.append(lambda h, sem=sem, val=val: h.wait_ge(sem, val))

    def begin_phase(self):
        self.barrier()
        self.stack.append(self.phase_st)
        self.phase_st = ExitStack()

    def end_phase(self):
        self.barrier()
        self.phase_st.close()
        self.phase_st = self.stack.pop()

    def finish(self):
        self.barrier()
        nc = self.nc
        E = self.E
        with nc.Block() as block:
            @block.tensor
            def _(h):
                for f in E["pe"].q:
                    f(h)

            @block.scalar
            def _(h):
                for f in E["act"].q:
                    f(h)

            @block.vector
            def _(h):
                for f in E["dve"].q:
                    f(h)

            @block.gpsimd
            def _(h):
                for f in E["pool"].q:
                    f(h)

            @block.sync
            def _(h):
                for f in E["sp"].q:
                    f(h)
        self.st.close()


def _flat(vs):
    out = []
    for v in vs:
        if isinstance(v, View) and v.res is not None:
            if isinstance(v.res, (tuple, list)):
                out.extend(v.res)
            else:
                out.append(v.res)
    return out


def _rw(ins, outs):
    return _flat(ins), _flat(outs)


def _a(x):
    return x.ap if isinstance(x, View) else x


class Stream:
    pass


class Builder:
    def __init__(self, nc, SEQ, DEPTH, dbg=None):
        self.nc = nc
        self.SEQ = SEQ
        self.DEPTH = DEPTH
        self.fw = FW(nc)
        self.dbg = dbg or {}
        self.rr = 0

    def mm(self, out, lhsT, rhs, start, stop, skip=False):
        r, w = _rw([lhsT, rhs], [out])
        o, a, b = out.ap, lhsT.ap, rhs.ap
        self.fw.op("pe", lambda h: h.matmul(o, a, b, start=start, stop=stop, skip_group_check=skip), r, w)

    def tr(self, out, in_, ident):
        r, w = _rw([in_, ident], [out])
        o, a, b = out.ap, in_.ap, ident.ap
        self.fw.op("pe", lambda h: h.transpose(o, a, b), r, w)

    def act(self, out, in_, func, bias=None, scale=None):
        r, w = _rw([in_, bias, scale], [out])
        kw = {}
        if bias is not None:
            kw["bias"] = _a(bias)
        if scale is not None:
            kw["scale"] = _a(scale)
        o, a = out.ap, in_.ap
        self.fw.op("act", lambda h: h.activation(o, a, func, **kw), r, w)

    def tt(self, eng, out, in0, in1, op):
        r, w = _rw([in0, in1], [out])
        o, a, b = out.ap, in0.ap, in1.ap
        self.fw.op(eng, lambda h: h.tensor_tensor(o, a, b, op), r, w)

    def stt(self, eng, out, in0, scalar, in1, op0, op1):
        r, w = _rw([in0, scalar, in1], [out])
        o, a, s, b = out.ap, in0.ap, _a(scalar), in1.ap
        self.fw.op(eng, lambda h: h.scalar_tensor_tensor(o, a, s, b, op0, op1), r, w)

    def ts(self, eng, out, in0, s1, s2, op0, op1):
        r, w = _rw([in0, s1, s2], [out])
        o, a, x1, x2 = out.ap, in0.ap, _a(s1), _a(s2)
        self.fw.op(eng, lambda h: h.tensor_scalar(o, a, x1, x2, op0, op1), r, w)

    def cp(self, eng, out, in_):
        r, w = _rw([in_], [out])
        o, a = out.ap, in_.ap
        if eng == "act":
            self.fw.op("act", lambda h: h.activation(o, a, AF.Copy), r, w)
        else:
            self.fw.op(eng, lambda h: h.tensor_copy(o, a), r, w)

    def recip(self, out, in_):
        r, w = _rw([in_], [out])
        o, a = out.ap, in_.ap
        self.fw.op("dve", lambda h: h.reciprocal(o, a), r, w)

    def memset(self, eng, out, val):
        r, w = _rw([], [out])
        o = out.ap
        self.fw.op(eng, lambda h: h.memset(o, val), r, w)

    def dma(self, eng, out, in_, **kw):
        r, w = _rw([in_], [out])
        return self.fw.dma(eng, out.ap, in_.ap, r, w, **kw)

    def ld(self, out, in_ap, **kw):
        self.dma("sp", out, DV(in_ap), **kw)

    def stq(self, out_ap, in_, **kw):
        self.dma("pool", DV(out_ap), in_, **kw)

    def dram_in(self, name, shape, dt=F32):
        return self.nc.dram_tensor(name, list(shape), dt, kind="ExternalInput").ap()

    def dram_scratch(self, name, shape, dt):
        kind = "ExternalOutput" if name in self.dbg else "Internal"
        return self.nc.dram_tensor(name, list(shape), dt, kind=kind).ap()

    def build(self):
        nc, fw, SEQ, DEPTH = self.nc, self.fw, self.SEQ, self.DEPTH
        I = {}
        I["x"] = self.dram_in("x", [SEQ, D])
        I["c"] = self.dram_in("c", [D])
        I["ctx"] = self.dram_in("ctx", [CTX, D])
        I["c_ctx"] = self.dram_in("c_ctx", [D])
        I["w_mod"] = self.dram_in("w_mod", [DEPTH, D, NMOD * D])
        I["b_mod"] = self.dram_in("b_mod", [DEPTH, NMOD * D])
        I["norm_mix"] = self.dram_in("norm_mix", [DEPTH, D])
        I["w_in"] = self.dram_in("w_in", [DEPTH, D, NPROJ])
        I["q_norm"] = self.dram_in("q_norm", [DEPTH, HD])
        I["k_norm"] = self.dram_in("k_norm", [DEPTH, HD])
        I["conv_w"] = self.dram_in("conv_w", [DEPTH, 3, 3 * HW])
        I["conv_b"] = self.dram_in("conv_b", [DEPTH, 3 * HW])
        I["filt_w1"] = self.dram_in("filt_w1", [DEPTH, FEMB, FHID])
        I["filt_b1"] = self.dram_in("filt_b1", [DEPTH, FHID])
        I["filt_w2"] = self.dram_in("filt_w2", [DEPTH, FHID, FHID])
        I["filt_b2"] = self.dram_in("filt_b2", [DEPTH, FHID])
        I["filt_w3"] = self.dram_in("filt_w3", [DEPTH, FHID, FHID])
        I["filt_b3"] = self.dram_in("filt_b3", [DEPTH, FHID])
        I["filt_w4"] = self.dram_in("filt_w4", [DEPTH, FHID, 2 * HW])
        I["filt_freq"] = self.dram_in("filt_freq", [DEPTH, 3, FHID])
        I["hyena_bias"] = self.dram_in("hyena_bias", [DEPTH, HW])
        I["w_o_attn"] = self.dram_in("w_o_attn", [DEPTH, D, D])
        I["w_o_hyena"] = self.dram_in("w_o_hyena", [DEPTH, HW, D])
        I["w_out"] = self.dram_in("w_out", [DEPTH, D, D])
        I["norm_ffn"] = self.dram_in("norm_ffn", [DEPTH, D])
        I["w_gate_up"] = self.dram_in("w_gate_up", [DEPTH, D, 2 * DFF])
        I["w_down"] = self.dram_in("w_down", [DEPTH, DFF, D])
        I["norm_final"] = self.dram_in("norm_final", [D])
        I["k_ident"] = self.dram_in("k_ident", [128, 128])
        I["k_rmat"] = self.dram_in("k_rmat", [128, 128])
        I["k_ft"] = self.dram_in("k_ft", [128, 4, 128])
        I["k_tw"] = self.dram_in("k_tw", [128, 2, 128])
        I["k_ropec"] = self.dram_in("k_ropec", [128, SEQ])
        I["k_ropes"] = self.dram_in("k_ropes", [128, SEQ])
        I["k_z_lat"] = self.dram_in("k_z_lat", [2, FEMB, SEQ])
        I["k_z_ctx"] = self.dram_in("k_z_ctx", [2, FEMB, CTX])
        I["k_dec_lat"] = self.dram_in("k_dec_lat", [2, HW, SEQ])
        I["k_dec_ctx"] = self.dram_in("k_dec_ctx", [2, HW, CTX])
        self.I = I
        self.out = nc.dram_tensor("out", [SEQ, D], F32, kind="ExternalOutput").ap()

        S = {}
        sc = self.dram_scratch
        S["wb_in"] = sc("wb_in", [DEPTH, D, NPROJ], BF16)
        S["wb_oa"] = sc("wb_oa", [DEPTH, D, D], BF16)
        S["wb_oh"] = sc("wb_oh", [DEPTH, HW, D], BF16)
        S["wb_out"] = sc("wb_out", [DEPTH, D, D], BF16)
        S["wb_gu"] = sc("wb_gu", [DEPTH, D, 2 * DFF], BF16)
        S["wb_dn"] = sc("wb_dn", [DEPTH, DFF, D], BF16)
        for jb in range(3):
            S["KB%d" % jb] = sc("KB%d" % jb, [HW, NFFT], BF16)
            S["KF%d" % jb] = sc("KF%d" % jb, [128, HW, 2, 128], F32)
        self.S = S

        def mkstream(name, T, j, rope, key_off):
            s = Stream()
            s.name, s.T, s.j, s.rope, s.key_off = name, T, j, rope, key_off
            s.TC = min(512, T)
            s.XT = sc("XT_" + name, [D, T], F32)
            s.Qs = sc("Qs_" + name, [NH, HD, T], BF16)
            s.AT = sc("AT_" + name, [D, T], BF16)
            s.U = sc("U_" + name, [3 * HW, T], F32)
            s.G = sc("G_" + name, [2 * D, T], BF16)
            s.VV = sc("VV_" + name, [HW, T], F32)
            s.VB = sc("VB_" + name, [HW, T], BF16)
            s.YB = sc("YB_" + name, [HW, T], F32)
            s.HY = sc("HY_" + name, [HW, T], BF16)
            return s

        self.lat = mkstream("lat", SEQ, 0, True, CTX)
        self.cx = mkstream("ctx", CTX, 1, False, 0)
        self.lat.z, self.lat.dec = I["k_z_lat"], I["k_dec_lat"]
        self.cx.z, self.cx.dec = I["k_z_ctx"], I["k_dec_ctx"]
        self.lat.n_keys = CTX + SEQ
        self.cx.n_keys = CTX

        self.ident = fw.sb("ident", [128, 128], F32)
        self.onesb = fw.sb("onesb", [128, 128], BF16)
        self.onesf = fw.sb("onesf", [128, 128], F32)
        self.rmat = fw.sb("rmat", [128, 128], BF16)
        self.ft = fw.sb("ft", [128, 4, 128], BF16)
        self.tw = fw.sb("tw", [128, 2, 128], F32)
        self.modcol = [fw.sb("modcol%d" % l, [128, 6 * KC, 2], F32) for l in range(DEPTH)]
        self.A1 = [fw.sb("A1_%d" % l, [128, KC, 2], F32) for l in range(DEPTH)]
        self.A2 = [fw.sb("A2_%d" % l, [128, KC, 2], F32) for l in range(DEPTH)]
        self.nfin = fw.sb("nfin", [128, KC], F32)
        self.psw = [fw.ps("psw%d" % i, [128, 1024], F32) for i in range(4)]
        self.ps = []
        for i in range(4):
            for hf in range(2):
                t = Tl(self.psw[i].h[:, hf * 512:(hf + 1) * 512], "psw%d_%d" % (i, hf))
                self.ps.append(t)

        self.jobs = [(self.lat, 0, 0), (self.cx, 0, 1)] + ([(self.lat, 1, 2)] if DEPTH > 1 else [])
        self.prologue()
        for l in range(DEPTH):
            last = (l == DEPTH - 1)
            self.layer_cols(l)
            self.lat.KF = S["KF0"] if l == 0 else S["KF2"]
            self.cx.KF = S["KF1"]
            self.attn_phase(l, last)
            self.ug_phase(l, last)
            streams = [self.lat] if last else [self.lat, self.cx]
            for s in streams:
                self.hyena_a(s, l)
                self.fft_conv(s)
                self.hyena_c(s, l)
            self.merge_phase(l, streams)
            self.ffn_phase(l, streams)
        self.final_phase()
        fw.barrier()
        self.cols_st.close()
        fw.finish()

    def nps(self, lo=0, hi=4):
        p = self.ps[lo + (self.rr % (hi - lo))]
        self.rr += 1
        return p

    def colvec(self, dst, src_ap):
        self.ld(dst, src_ap.rearrange("(c p) -> p c", p=128), allow_slow_non_contiguous=True)

    def prologue(self):
        fw, I, S = self.fw, self.I, self.S
        DEPTH, SEQ = self.DEPTH, self.SEQ
        fw.begin_phase()
        self.ld(self.ident[:], I["k_ident"])
        tmpf = fw.sb("tmpf", [128, 4, 128], F32)
        self.ld(tmpf[:, 0, :], I["k_rmat"])
        self.cp("dve", self.rmat[:], tmpf[:, 0, :])
        tmpf2 = fw.sb("tmpf2", [128, 4, 128], F32)
        self.ld(tmpf2[:], I["k_ft"])
        self.cp("dve", self.ft[:], tmpf2[:])
        self.ld(self.tw[:], I["k_tw"])
        self.memset("dve", self.onesb[:], 1.0)
        self.memset("dve", self.onesf[:], 1.0)
        self.colvec(self.nfin[:], I["norm_final"])
        for _ in self.cast_gen([("w_in", "wb_in", D, NPROJ, 0)]):
            pass
        ccol = fw.sb("ccol", [128, KC, 2], F32)
        scol = fw.sb("scol", [128, KC, 2], F32)
        self.colvec(ccol[:, :, 0], I["c"])
        self.colvec(ccol[:, :, 1], I["c_ctx"])
        self.act(scol[:], ccol[:], AF.Silu)
        bcol = fw.sb("bcol", [128, 6 * KC], F32)
        wm = [fw.sb("wm%d" % i, [128, KC, 512], F32) for i in range(2)]
        pm = self.ps[7]
        n = 0
        for l in range(DEPTH):
            for q4 in range(4):
                self.ld(bcol[:, q4 * 12:(q4 + 1) * 12],
                        I["b_mod"][l, q4 * 1536:(q4 + 1) * 1536].rearrange("(c p) -> p c", p=128),
                        allow_slow_non_contiguous=True)
            for cb in range(12):
                w = wm[n % 2]
                n += 1
                self.ld(w[:], I["w_mod"][l, :, cb * 512:(cb + 1) * 512].rearrange("(kc p) n -> p kc n", p=128))
                for f4 in range(4):
                    f = cb * 4 + f4
                    for kc in range(KC):
                        self.mm(pm[:, f * 2:f * 2 + 2], w[:, kc, f4 * 128:(f4 + 1) * 128], scol[:, kc, :],
                                kc == 0, kc == KC - 1, skip=True)
            pv = pm.v(pm.h[:, 0:96].rearrange("p (f j) -> p f j", j=2))
            self.tt("dve", self.modcol[l][:], pv, bcol.v(bcol.h[:].unsqueeze(2).broadcast_to([128, 48, 2])), ALU.add)
        fw.end_phase()
        fw.begin_phase()
        self.zt = fw.sb("zt", [128, 2048], BF16)
        self.memset("pool", self.zt[:], 0.0)
        xtok = [fw.sb("xtok%d" % i, [128, 4, D], F32) for i in range(2)]
        xts = [fw.sb("xts%d" % i, [128, KC, 512], F32) for i in range(2)]

        xtok_c = [fw.sb("xtokc", [128, 2, D], F32)]
        xts_c = [fw.sb("xtsc", [128, KC, 256], F32)]

        def xt_gen(src, s, xtok, xts):
            TC = s.TC
            nj = TC // 128
            for ch in range(s.T // TC):
                t0 = ch * TC
                xt, xs = xtok[ch % len(xtok)], xts[ch % len(xts)]
                self.ld(xt[:, :nj, :], src[t0:t0 + TC, :].rearrange("(j p) f -> p j f", p=128))
                for c in range(KC):
                    p = self.nps(0, 4)
                    for j in range(nj):
                        self.tr(p[:, j * 128:(j + 1) * 128], xt[:, j, c * 128:(c + 1) * 128], self.ident[:])
                    self.cp("act" if c % 2 == 0 else "dve", xs[:, c, :TC], p[:, :TC])
                    if c % 2 == 1:
                        yield
                self.stq(s.XT[:, t0:t0 + TC].rearrange("(c p) t -> p c t", p=128), xs[:, :, :TC])
                yield

        gens = [xt_gen(I["x"], self.lat, xtok, xts), xt_gen(I["ctx"], self.cx, xtok_c, xts_c)]
        for (st_, l_, jb) in self.jobs:
            gens.append(self.mlp_gen(st_, l_, S["KB%d" % jb], self.ps[4 + jb]))
        while gens:
            for g in list(gens):
                try:
                    next(g)
                except StopIteration:
                    gens.remove(g)
        fw.end_phase()

    def cast_gen(self, items):
        fw, I, S = self.fw, self.I, self.S
        stg = [fw.sb("stg%d" % i, [128, 2048], F32) for i in range(3)]
        stb = [fw.sb("stb%d" % i, [128, 2048], BF16) for i in range(3)]
        n = 0
        for (src, dst, K, N, l) in items:
            for rb in range(K // 128):
                for c0 in range(0, N, 2048):
                    cw = min(2048, N - c0)
                    a, b_ = stg[n % 3], stb[n % 3]
                    self.ld(a[:, :cw], I[src][l, rb * 128:(rb + 1) * 128, c0:c0 + cw])
                    self.cp("pool", b_[:, :cw], a[:, :cw])
                    self.stq(S[dst][l, rb * 128:(rb + 1) * 128, c0:c0 + cw], b_[:, :cw])
                    n += 1
                    yield

    def layer_cols(self, l):
        fw, I = self.fw, self.I
        if l > 0:
            fw.barrier()
            self.cols_st.close()
        self.cols_st = ExitStack()
        assert fw.phase_st is None
        fw.phase_st = self.cols_st
        nm = fw.sb("nmcol", [128, KC], F32)
        nf = fw.sb("nfcol", [128, KC], F32)
        self.colvec(nm[:], I["norm_mix"][l])
        self.colvec(nf[:], I["norm_ffn"][l])
        mc = self.modcol[l]
        self.stt("dve", self.A1[l][:], mc[:, 8:16, :], 1.0, nm.v(nm.h[:].unsqueeze(2).broadcast_to([128, KC, 2])), ALU.add, ALU.mult)
        self.stt("dve", self.A2[l][:], mc[:, 32:40, :], 1.0, nf.v(nf.h[:].unsqueeze(2).broadcast_to([128, KC, 2])), ALU.add, ALU.mult)
        self.gq = fw.sb("gq", [128, 1], F32)
        self.gk = fw.sb("gk", [128, 1], F32)
        self.ld(self.gq[:], I["q_norm"][l].rearrange("(p o) -> p o", o=1), allow_slow_non_contiguous=True)
        self.ld(self.gk[:], I["k_norm"][l].rearrange("(p o) -> p o", o=1), allow_slow_non_contiguous=True)
        grow = fw.sb("grow", [1, 2, 128], F32)
        self.ld(grow[:, 0, :], I["q_norm"][l].rearrange("(o n) -> o n", o=1))
        self.ld(grow[:, 1, :], I["k_norm"][l].rearrange("(o n) -> o n", o=1))
        gmax = fw.sb("gmax", [1, 2], F32)
        r, w = _rw([grow[:]], [gmax[:]])
        go, gi = gmax.h[:], grow.h[:]
        fw.op("dve", lambda h: h.tensor_reduce(go, gi, mybir.AxisListType.X, ALU.max, apply_absolute_value=True), r, w)
        nb = fw.sb("nb", [1, 2], F32)
        self.stt("dve", nb[:, 0:1], gmax[:, 0:1], -math.sqrt(128.0), gmax[:, 1:2], ALU.mult, ALU.mult)
        self.stt("dve", nb[:, 1:2], gmax[:, 0:1], -math.sqrt(128.0), gmax[:, 1:2], ALU.mult, ALU.mult)
        pn = self.ps[6]
        self.mm(pn[:, 0:2], self.onesf[0:1, :], nb[:], True, True)
        self.negB = fw.sb("negB", [128, 1], F32)
        self.cp("dve", self.negB[:], pn[:, 0:1])
        self.cw = fw.sb("cwcol", [128, 3, 12], F32)
        for tap in range(3):
            self.colvec(self.cw[:, tap, :], I["conv_w"][l, tap])
        self.cb = fw.sb("cbcol", [128, 12], F32)
        self.colvec(self.cb[:], I["conv_b"][l])
        self.hb = fw.sb("hbcol", [128, 4], F32)
        self.colvec(self.hb[:], I["hyena_bias"][l])
        fw.phase_st = None

    def rms_mod(self, xs, TC, A, Bm, j, out, T_):
        sq, sd, rstd, tmp = T_["sq"], T_["sd"], T_["rstd"], T_["tmp"]
        self.act(sq[:, :, :TC], xs[:, :, :TC], AF.Square)
        pS = self.ps[7]
        for c in range(KC):
            self.mm(pS[:, :TC], self.onesb[:], sq[:, c, :TC], c == 0, c == KC - 1)
        self.act(sd[:, :TC], pS[:, :TC], AF.Sqrt, bias=EPS, scale=1.0 / D)
        self.recip(rstd[:, :TC], sd[:, :TC])
        for c in range(KC):
            if Bm is None:
                self.stt("dve", out[:, c, :TC], xs[:, c, :TC], A(c, j), rstd[:, :TC], ALU.mult, ALU.mult)
            else:
                t = tmp[c % 2]
                self.stt("dve", t[:, :TC], xs[:, c, :TC], A(c, j), rstd[:, :TC], ALU.mult, ALU.mult)
                self.act(out[:, c, :TC], t[:, :TC], AF.Identity, bias=Bm(c, j))

    def rms_tiles(self, TC):
        fw = self.fw
        return {"sq": fw.sb("sq", [128, KC, TC], BF16), "sd": fw.sb("sd", [128, TC], F32),
                "rstd": fw.sb("rstd", [128, TC], F32), "tmp": [fw.sb("rtmp%d" % i, [128, TC], F32) for i in range(2)]}

    def load_w(self, dst, src, kchunks):
        for kc in range(kchunks):
            self.ld(dst[:, kc, :], src[kc * 128:(kc + 1) * 128, :])

    def attn_phase(self, l, last):
        fw, I, S = self.fw, self.I, self.S
        lat, cx = self.lat, self.cx
        fw.begin_phase()
        NK = lat.n_keys
        KT = fw.sb("KT", [128, NKV, NK], BF16)
        V = fw.sb("V", [128, NK // 128, NKV * HD], BF16)
        fw.begin_phase()
        wq = fw.sb("wqkv", [128, KC, 1536], BF16)
        for kc in range(KC):
            self.ld(wq[:, kc, :], S["wb_in"][l, kc * 128:(kc + 1) * 128, 0:1536])
        RT = self.rms_tiles(512)
        xs2 = [fw.sb("xs%d" % i, [128, KC, 512], F32) for i in range(2)]
        hT2 = [fw.sb("hT%d" % i, [128, KC, 512], BF16) for i in range(2)]
        rc2 = [fw.sb("rc%d" % i, [128, 512], F32) for i in range(2)]
        rs2 = [fw.sb("rs%d" % i, [128, 512], F32) for i in range(2)]
        sqh = [fw.sb("sqh%d" % i, [128, 512], BF16) for i in range(2)]
        qg = [fw.sb("qg%d" % i, [128, 512], BF16) for i in range(2)]
        sdh = [fw.sb("sdh%d" % i, [128, 512], F32) for i in range(2)]
        rsh = [fw.sb("rsh%d" % i, [128, 512], F32) for i in range(2)]
        t1 = [fw.sb("t1_%d" % i, [128, 512], F32) for i in range(2)]
        t2 = [fw.sb("t2_%d" % i, [128, 512], F32) for i in range(2)]
        qo = [fw.sb("qo%d" % i, [128, 512], BF16) for i in range(3)]
        mc = self.modcol[l]
        A1 = self.A1[l]
        hn = 0
        nchunk = 0
        for s in [cx, lat]:
            want_q = (s is lat) or (not last)
            TC = s.TC
            for ch in range(s.T // TC):
                t0 = ch * TC
                xs, hT = xs2[nchunk % 2], hT2[nchunk % 2]
                rc, rs = rc2[nchunk % 2], rs2[nchunk % 2]
                nchunk += 1
                self.ld(xs[:, :, :TC], s.XT[:, t0:t0 + TC].rearrange("(c p) t -> p c t", p=128))
                if s.rope:
                    self.ld(rc[:, :TC], I["k_ropec"][:, t0:t0 + TC])
                    self.ld(rs[:, :TC], I["k_ropes"][:, t0:t0 + TC])
                self.rms_mod(xs, TC, lambda c, j: A1[:, c, j:j + 1], lambda c, j: mc[:, c, j:j + 1], s.j, hT, RT)
                heads = ([("q", j) for j in range(NH)] if want_q else []) + [("k", 0), ("k", 1)]
                for (kind, j) in heads:
                    col0 = j * 128 if kind == "q" else D + j * 128
                    gcol = self.gq if kind == "q" else self.gk
                    i2 = hn % 2
                    hn += 1
                    p = self.nps(0, 4)
                    for kc in range(KC):
                        self.mm(p[:, :TC], wq[:, kc, col0:col0 + 128], hT[:, kc, :TC], kc == 0, kc == KC - 1)
                    self.act(sqh[i2][:, :TC], p[:, :TC], AF.Square)
                    self.act(qg[i2][:, :TC], p[:, :TC], AF.Identity, scale=gcol[:])
                    pa = self.ps[4 + i2]
                    self.mm(pa[:, :TC], self.onesb[:], sqh[i2][:, :TC], True, True)
                    self.act(sdh[i2][:, :TC], pa[:, :TC], AF.Sqrt, bias=EPS, scale=1.0 / HD)
                    self.recip(rsh[i2][:, :TC], sdh[i2][:, :TC])
                    if kind == "q":
                        qoi = qo[hn % 3]
                        dest = qoi[:, :TC]
                    else:
                        dest = KT[:, j, s.key_off + t0: s.key_off + t0 + TC]
                    if s.rope:
                        pb = self.ps[6]
                        self.mm(pb[:, :TC], self.rmat[:], qg[i2][:, :TC], True, True)
                        self.tt("dve", t1[i2][:, :TC], qg[i2][:, :TC], rc[:, :TC], ALU.mult)
                        self.tt("dve", t2[i2][:, :TC], pb[:, :TC], rs[:, :TC], ALU.mult)
                        self.tt("pool", t1[i2][:, :TC], t1[i2][:, :TC], t2[i2][:, :TC], ALU.add)
                        self.tt("dve", dest, t1[i2][:, :TC], rsh[i2][:, :TC], ALU.mult)
                    else:
                        self.tt("dve", dest, qg[i2][:, :TC], rsh[i2][:, :TC], ALU.mult)
                    if kind == "q":
                        self.stq(s.Qs[j, :, t0:t0 + TC], qoi[:, :TC])
                for tsub in range(TC // 128):
                    p = self.nps(0, 4)
                    for kc in range(KC):
                        self.mm(p[:, 0:256], hT[:, kc, tsub * 128:(tsub + 1) * 128], wq[:, kc, 1280:1536], kc == 0, kc == KC - 1)
                    self.cp("act", V[:, (s.key_off + t0) // 128 + tsub, :], p[:, 0:256])
        fw.end_phase()
        fw.begin_phase()
        qt3 = [fw.sb("qt%d" % i, [128, 512], BF16) for i in range(2)]
        pt3 = [fw.sb("pt%d" % i, [128, 2, 512], BF16) for i in range(4)]
        rl2 = [fw.sb("rl%d" % i, [128, 512], F32) for i in range(2)]
        ao2 = [fw.sb("ao%d" % i, [128, 512], BF16) for i in range(2)]
        SCALE = HD ** -0.5
        GRP = 4
        acc2s = [fw.sb("acc2_%d" % i, [128, 2, 512], BF16) for i in range(2)]
        accfs = [fw.sb("accf_%d" % i, [128, 512], BF16) for i in range(2)]
        SHIFT = -8.0
        pairs = []
        for s in ([lat] if last else [lat, cx]):
            for h in range(NH):
                for qc in range(s.T // s.TC):
                    for j in range(s.n_keys // 256):
                        pairs.append((s, h, qc, j))
        st = {}
        pend = []

        def emit_S(pi):
            s, h, qc, j = pairs[pi]
            TC = s.TC
            kvh = h // (NH // NKV)
            if j == 0:
                nq = st.get("nq", 0)
                st["nq"] = nq + 1
                qt = qt3[nq % 2]
                self.ld(qt[:, :TC], s.Qs[h, :, qc * TC:(qc + 1) * TC])
                st[("qt", s.name, h, qc)] = (qt, nq)
            qt, nq = st[("qt", s.name, h, qc)]
            bufs = [0, 1] if st["bg_on"] else [0, 1, 3]
            wb = bufs[st["nS"] % len(bufs)]
            st["nS"] += 1
            st[("wb", pi)] = wb
            for a in range(2):
                kt = 2 * j + a
                p = self.ps[wb * 2 + a]
                self.mm(p[:, :TC], KT[:, kvh, kt * 128:(kt + 1) * 128], qt[:, :TC], True, True)

        def emit_rest(pi):
            s, h, qc, j = pairs[pi]
            TC = s.TC
            kvh = h // (NH // NKV)
            n_kt = s.n_keys // 128
            qt, nq = st[("qt", s.name, h, qc)]
            po = self.ps[4]
            pl = self.ps[5]
            wb = st[("wb", pi)]
            pw = self.psw[wb]
            pin = View(pw.h[:].rearrange("p (a n) -> p a n", a=2)[:, :, :TC], (self.ps[wb * 2].res, self.ps[wb * 2 + 1].res))
            pt = pt3[pi % 4]
            self.act(pt[:, :, :TC], pin, AF.Exp, bias=SHIFT, scale=SCALE)
            due = list(pend)
            del pend[:]
            for a in range(2):
                kt = 2 * j + a
                self.mm(po[:, :TC], V[:, kt, kvh * HD:(kvh + 1) * HD], pt[:, a, :TC], kt == 0, kt == n_kt - 1)
            for f_ in due:
                f_()
            npairs = n_kt // 2
            g0 = (j // GRP) * GRP
            gsz = min(GRP, npairs - g0)
            jj = j - g0
            ai = (st.get("na", 0)) % 2
            acc2, accf = acc2s[ai], accfs[ai]
            if gsz == 1:
                self.tt("dve", accf[:, :TC], pt[:, 0, :TC], pt[:, 1, :TC], ALU.add)
            elif jj == 0:
                st["ptprev"] = pt
            elif jj == 1:
                self.tt("dve", acc2[:, :, :TC], st["ptprev"][:, :, :TC], pt[:, :, :TC], ALU.add)
            else:
                self.tt("dve", acc2[:, :, :TC], acc2[:, :, :TC], pt[:, :, :TC], ALU.add)
            if jj == gsz - 1:
                if gsz > 1:
                    self.tt("dve", accf[:, :TC], acc2[:, 0, :TC], acc2[:, 1, :TC], ALU.add)
                first, lastg = (g0 == 0), (g0 + gsz == npairs)

                def emit_L(accf=accf, TC=TC, first=first, lastg=lastg):
                    self.mm(pl[:, :TC], self.onesb[:], accf[:, :TC], first, lastg)
                if lastg:
                    for f_ in pend:
                        f_()
                    del pend[:]
                    emit_L()
                else:
                    pend.append(emit_L)
                st["na"] = st.get("na", 0) + 1
            if j == n_kt // 2 - 1:
                rl, ao = rl2[nq % 2], ao2[nq % 2]
                self.act(rl[:, :TC], pl[:, :TC], AF.Ln)
                self.act(rl[:, :TC], rl[:, :TC], AF.Exp, scale=-1.0)
                self.tt("dve", ao[:, :TC], po[:, :TC], rl[:, :TC], ALU.mult)
                self.stq(s.AT[h * HD:(h + 1) * HD, qc * TC:(qc + 1) * TC], ao[:, :TC])

        bg = None
        if l == 0:
            items = []
            for ll in range(self.DEPTH):
                for (src, dst, K_, N_) in [("w_in", "wb_in", D, NPROJ), ("w_o_attn", "wb_oa", D, D), ("w_o_hyena", "wb_oh", HW, D),
                                           ("w_out", "wb_out", D, D), ("w_gate_up", "wb_gu", D, 2 * DFF), ("w_down", "wb_dn", DFF, D)]:
                    if not (src == "w_in" and ll == 0):
                        items.append((src, dst, K_, N_, ll))

            def chain():
                for x_ in self.spectrum_gen([jb for (_, _, jb) in self.jobs], self.ps[6], self.ps[7]):
                    yield
                for x_ in self.cast_gen(items):
                    yield
            bg = chain()
        import os
        if bg is not None and os.environ.get("BG_FIRST"):
            for x_ in bg:
                pass
            bg = None
        st["nS"] = 0
        nextS = 0
        for pi in range(len(pairs)):
            st["bg_on"] = bg is not None
            look = 1 if bg is not None else 2
            while nextS <= pi + look and nextS < len(pairs):
                emit_S(nextS)
                nextS += 1
            emit_rest(pi)
            if bg is not None and pi % 3 == 2:
                if next(bg, "done") == "done":
                    bg = None
        if bg is not None:
            for x_ in bg:
                pass
        fw.end_phase()
        fw.end_phase()

    def ug_phase(self, l, last):
        fw, S = self.fw, self.S
        fw.begin_phase()
        NW = NPROJ - 1536
        w = fw.sb("wug", [128, KC, NW], BF16)
        for kc in range(KC):
            self.ld(w[:, kc, :], S["wb_in"][l, kc * 128:(kc + 1) * 128, 1536:NPROJ])
        RT = self.rms_tiles(512)
        xs2 = [fw.sb("xs%d" % i, [128, KC, 512], F32) for i in range(2)]
        hT2 = [fw.sb("hT%d" % i, [128, KC, 512], BF16) for i in range(2)]
        us2 = [fw.sb("us%d" % i, [128, 4, 512], F32) for i in range(2)]
        gs2 = [fw.sb("gs%d" % i, [128, 4, 512], BF16) for i in range(2)]
        mc, A1 = self.modcol[l], self.A1[l]
        n = 0
        nu = 0
        ng = 0
        for s in ([self.lat] if last else [self.lat, self.cx]):
            TC = s.TC
            for ch in range(s.T // TC):
                t0 = ch * TC
                xs, hT = xs2[n % 2], hT2[n % 2]
                n += 1
                self.ld(xs[:, :, :TC], s.XT[:, t0:t0 + TC].rearrange("(c p) t -> p c t", p=128))
                self.rms_mod(xs, TC, lambda c, j: A1[:, c, j:j + 1], lambda c, j: mc[:, c, j:j + 1], s.j, hT, RT)
                for o4 in range(3):
                    us = us2[nu % 2]
                    nu += 1
                    for oi in range(4):
                        oc = o4 * 4 + oi
                        p = self.nps(0, 6)
                        for kc in range(KC):
                            self.mm(p[:, :TC], w[:, kc, oc * 128:(oc + 1) * 128], hT[:, kc, :TC], kc == 0, kc == KC - 1)
                        self.cp("act" if oi % 2 == 0 else "dve", us[:, oi, :TC], p[:, :TC])
                    self.stq(s.U[o4 * 512:(o4 + 1) * 512, t0:t0 + TC].rearrange("(c p) t -> p c t", p=128), us[:, :, :TC])
                for o4 in range(4):
                    gs = gs2[ng % 2]
                    ng += 1
                    for oi in range(4):
                        oc = 12 + o4 * 4 + oi
                        p = self.nps(0, 6)
                        for kc in range(KC):
                            self.mm(p[:, :TC], w[:, kc, oc * 128:(oc + 1) * 128], hT[:, kc, :TC], kc == 0, kc == KC - 1)
                        self.act(gs[:, oi, :TC], p[:, :TC], AF.Sigmoid)
                    self.stq(s.G[o4 * 512:(o4 + 1) * 512, t0:t0 + TC].rearrange("(c p) t -> p c t", p=128), gs[:, :, :TC])
        fw.end_phase()

    def conv3(self, s, base_chunk, cc, t0, TCH, ub, out):
        T = s.T
        row0 = (base_chunk + cc) * 128
        lo, hi = t0 - 1, t0 + TCH + 1
        a = 0
        if lo < 0:
            self.memset("pool", ub[:, 0:1], 0.0)
            lo, a = 0, 1
        b = TCH + 2
        if hi > T:
            self.memset("pool", ub[:, TCH + 1:TCH + 2], 0.0)
            hi, b = T, TCH + 1
        self.ld(ub[:, a:b], s.U[row0:row0 + 128, lo:hi])
        k = base_chunk + cc
        cw, cb = self.cw, self.cb
        self.act(out[:, :TCH], ub[:, 1:TCH + 1], AF.Identity, bias=cb[:, k:k + 1], scale=cw[:, 1, k:k + 1])
        self.stt("dve", out[:, :TCH], ub[:, 0:TCH], cw[:, 0, k:k + 1], out[:, :TCH], ALU.mult, ALU.add)
        self.stt("dve", out[:, :TCH], ub[:, 2:TCH + 2], cw[:, 2, k:k + 1], out[:, :TCH], ALU.mult, ALU.add)

    def hyena_a(self, s, l):
        fw = self.fw
        fw.begin_phase()
        TCH = min(2048, s.T)
        ub2 = [fw.sb("ub%d" % i, [128, TCH + 2], F32) for i in range(4)]
        cx1 = [fw.sb("cx1_%d" % i, [128, TCH], F32) for i in range(2)]
        cv = [fw.sb("cv_%d" % i, [128, TCH], F32) for i in range(2)]
        vb = [fw.sb("vb_%d" % i, [128, TCH], BF16) for i in range(2)]
        n = 0
        for cc in range(4):
            for tch in range(s.T // TCH):
                t0 = tch * TCH
                i2 = n % 2
                self.conv3(s, 4, cc, t0, TCH, ub2[(2 * n) % 4], cx1[i2])
                self.conv3(s, 8, cc, t0, TCH, ub2[(2 * n + 1) % 4], cv[i2])
                n += 1
                self.tt("dve", cv[i2][:], cv[i2][:], cx1[i2][:], ALU.mult)
                self.cp("act", vb[i2][:], cv[i2][:])
                self.stq(s.VV[cc * 128:(cc + 1) * 128, t0:t0 + TCH], cv[i2][:])
                self.stq(s.VB[cc * 128:(cc + 1) * 128, t0:t0 + TCH], vb[i2][:])
        fw.end_phase()

    def hyena_c(self, s, l):
        fw = self.fw
        fw.begin_phase()
        TCH = min(2048, s.T)
        ub2 = [fw.sb("ub%d" % i, [128, TCH + 2], F32) for i in range(2)]
        cx0 = [fw.sb("cx0_%d" % i, [128, TCH], F32) for i in range(2)]
        yr = [fw.sb("yr_%d" % i, [128, TCH], F32) for i in range(2)]
        vv = [fw.sb("vv_%d" % i, [128, TCH], F32) for i in range(2)]
        hy = [fw.sb("hy_%d" % i, [128, TCH], BF16) for i in range(2)]
        n = 0
        for cc in range(4):
            for tch in range(s.T // TCH):
                t0 = tch * TCH
                i2 = n % 2
                n += 1
                self.conv3(s, 0, cc, t0, TCH, ub2[i2], cx0[i2])
                self.ld(yr[i2][:], s.YB[cc * 128:(cc + 1) * 128, t0:t0 + TCH])
                self.ld(vv[i2][:], s.VV[cc * 128:(cc + 1) * 128, t0:t0 + TCH])
                self.stt("dve", yr[i2][:], vv[i2][:], self.hb[:, cc:cc + 1], yr[i2][:], ALU.mult, ALU.add)
                self.tt("pool", hy[i2][:], yr[i2][:], cx0[i2][:], ALU.mult)
                self.stq(s.HY[cc * 128:(cc + 1) * 128, t0:t0 + TCH], hy[i2][:])
        fw.end_phase()

    def cmul(self, out, pin, tre, tim, conj, tmp1, tmp2):
        self.tt("dve", tmp1[:], pin, tre, ALU.mult)
        self.tt("dve", tmp2[:], pin, tim, ALU.mult)
        if not conj:
            self.tt("dve", out[:, :, 0, :], tmp1[:, :, 0, :], tmp2[:, :, 1, :], ALU.subtract)
            self.tt("pool", out[:, :, 1, :], tmp2[:, :, 0, :], tmp1[:, :, 1, :], ALU.add)
        else:
            self.tt("dve", out[:, :, 0, :], tmp1[:, :, 0, :], tmp2[:, :, 1, :], ALU.add)
            self.tt("pool", out[:, :, 1, :], tmp1[:, :, 1, :], tmp2[:, :, 0, :], ALU.subtract)

    def p4(self, p):
        return p.v(p.h[:].rearrange("p (c r k) -> p c r k", c=2, r=2))

    def tw_b(self, idx):
        return self.tw.v(self.tw.h[:, idx, :].unsqueeze(1).unsqueeze(1).broadcast_to([128, 2, 2, 128]))

    def fft_s1(self, v2, nK, c0, Bt, tmp1, tmp2, pA):
        ft = self.ft
        for c in range(2):
            self.mm(pA[:, c * 256:(c + 1) * 256], v2[0:nK, c0 + c, :],
                    ft.v(ft.h[0:nK, 0:2, :]), c == 0, c == 1, skip=True)
        self.cmul(Bt, self.p4(pA), self.tw_b(0), self.tw_b(1), False, tmp1, tmp2)

    def fft_s2(self, Bt, pX):
        ft = self.ft
        self.mm(pX[:], ft[:, 0, :], Bt[:], True, False, skip=True)
        pX4 = self.p4(pX)
        self.mm(View(pX4.ap[:, :, 0, :], pX.res), ft[:, 3, :], Bt[:, :, 1, :], False, False, skip=True)
        self.mm(View(pX4.ap[:, :, 1, :], pX.res), ft[:, 1, :], Bt[:, :, 0, :], False, True, skip=True)

    def fft_conv(self, s):
        fw = self.fw
        fw.begin_phase()
        T = s.T
        nK = T // 128
        ft = self.ft
        v2s = [fw.sb("v2_%d" % i, [max(nK, 2), 128, 128], BF16) for i in range(2)]
        y2 = fw.sb("y2", [max(nK, 2), 128, 128], F32)
        Bt = [fw.sb("Bt%d" % i, [128, 2, 2, 128], BF16) for i in range(2)]
        Yt = [fw.sb("Yt%d" % i, [128, 2, 2, 128], BF16) for i in range(2)]
        Ut = [fw.sb("Ut%d" % i, [128, 2, 2, 128], BF16) for i in range(2)]
        kf = [fw.sb("kf%d" % i, [128, 2, 2, 128], F32) for i in range(3)]
        tA = [[fw.sb("cmA%d_%d" % (k, i), [128, 2, 2, 128], F32) for i in range(2)] for k in range(3)]
        tB = [[fw.sb("cmB%d_%d" % (k, i), [128, 2, 2, 128], F32) for i in range(2)] for k in range(3)]
        NG = 64
        for cc in range(4):
            v2 = v2s[cc % 2]
            for q4 in range(4):
                self.ld(v2[0:nK, q4 * 32:(q4 + 1) * 32, :],
                        s.VB[cc * 128 + q4 * 32: cc * 128 + (q4 + 1) * 32, :].rearrange("ch (n1 n2) -> n1 ch n2", n2=128))
            for it in range(NG + 3):
                g = it
                if 0 <= g < NG:
                    i2 = g % 2
                    self.ld(kf[g % 3][:], s.KF[:, cc * 128 + g * 2: cc * 128 + g * 2 + 2, :, :])
                    self.fft_s1(v2, nK, g * 2, Bt[i2], tA[0][i2], tB[0][i2], self.ps[i2])
                g = it - 1
                if 0 <= g < NG:
                    i2 = g % 2
                    kfi = kf[g % 3]
                    pX = self.ps[2 + i2]
                    self.fft_s2(Bt[i2], pX)
                    kre = kfi.v(kfi.h[:, :, 0:1, :].broadcast_to([128, 2, 2, 128]))
                    kim = kfi.v(kfi.h[:, :, 1:2, :].broadcast_to([128, 2, 2, 128]))
                    self.cmul(Yt[i2], self.p4(pX), kre, kim, False, tA[1][i2], tB[1][i2])
                g = it - 2
                if 0 <= g < NG:
                    i2 = g % 2
                    pU = self.ps[4 + i2]
                    for c in range(2):
                        self.mm(pU[:, c * 256:(c + 1) * 256], Yt[i2][:, c, 0, :], ft.v(ft.h[:, 2:4, :]), c == 0, False, skip=True)
                        self.mm(pU[:, c * 256:(c + 1) * 256], Yt[i2][:, c, 1, :], ft.v(ft.h[:, 1:3, :]), False, c == 1, skip=True)
                    self.cmul(Ut[i2], self.p4(pU), self.tw_b(0), self.tw_b(1), True, tA[2][i2], tB[2][i2])
                g = it - 3
                if 0 <= g < NG:
                    i2 = g % 2
                    pY = self.ps[6 + i2]
                    pYv = View(pY.h[0:nK, 0:256].rearrange("p (c k) -> p c k", c=2), pY.res)
                    self.mm(pYv, ft[:, 0, 0:nK], Ut[i2][:, :, 0, :], True, False, skip=True)
                    self.mm(pYv, ft[:, 1, 0:nK], Ut[i2][:, :, 1, :], False, True, skip=True)
                    self.cp("act", y2[0:nK, g * 2:g * 2 + 2, :], pYv)
            for q4 in range(4):
                self.stq(s.YB[cc * 128 + q4 * 32: cc * 128 + (q4 + 1) * 32, :].rearrange("ch (m1 m2) -> m1 ch m2", m2=128),
                         y2[0:nK, q4 * 32:(q4 + 1) * 32, :])
        fw.end_phase()

    def mlp_gen(self, s, l, KB, pbank):
        fw, I = self.fw, self.I
        L = s.T
        TC = min(512, L)
        w1 = fw.sb("fw1", [FEMB, FHID], F32)
        w2 = fw.sb("fw2", [FHID, FHID], F32)
        w3 = fw.sb("fw3", [FHID, FHID], F32)
        w4 = fw.sb("fw4", [FHID, 2 * HW], F32)
        self.ld(w1[:], I["filt_w1"][l])
        self.ld(w2[:], I["filt_w2"][l])
        self.ld(w3[:], I["filt_w3"][l])
        self.ld(w4[:], I["filt_w4"][l])
        fq = fw.sb("fq", [FHID, 3], F32)
        fb = fw.sb("fb", [FHID, 3], F32)
        self.ld(fq[:], I["filt_freq"][l].rearrange("i p -> p i"), allow_slow_non_contiguous=True)
        for i, nm in enumerate(["filt_b1", "filt_b2", "filt_b3"]):
            self.ld(fb[:, i:i + 1], I[nm][l].rearrange("(p o) -> p o", o=1), allow_slow_non_contiguous=True)
        fsc = fw.sb("fsc", [FHID, 3], F32)
        fbc = fw.sb("fbc", [FHID, 3], F32)
        self.ts("dve", fsc[:], fq[:], 1.0 / 3.0, 0.0, ALU.mult, ALU.add)
        self.tt("dve", fbc[:], fsc[:], fb[:], ALU.mult)
        zt = self.zt
        z0, z1 = L, NFFT - L + 1
        for cc in range(4):
            p0 = z0
            while p0 < z1:
                n = min(2048, z1 - p0)
                self.stq(KB[cc * 128:(cc + 1) * 128, p0:p0 + n], zt[:, :n], allow_slow_non_contiguous=True)
                p0 += n
            yield
        zin = [fw.sb("zin%d" % i, [FEMB, TC], F32) for i in range(2)]
        hs = fw.sb("hs", [FHID, TC], F32)
        s2 = fw.sb("s2", [FHID, TC], F32)
        hh = [fw.sb("hh%d" % k, [FHID, TC], F32) for k in range(3)]
        dct = [fw.sb("dct%d" % i, [128, TC], F32) for i in range(2)]
        kr = [fw.sb("kr%d" % i, [128, TC], BF16) for i in range(2)]
        n = 0
        nd = 0
        p = pbank
        for d_ in range(2):
            for ch in range(L // TC):
                t0 = ch * TC
                i2 = n % 2
                n += 1
                self.ld(zin[i2][:], s.z[d_, :, t0:t0 + TC])
                cur = zin[i2]
                curK = FEMB
                for k, wk in enumerate([w1, w2, w3]):
                    self.mm(p[0:FHID, :TC], wk[0:curK, :], cur[0:curK, :], True, True)
                    self.act(hs[:], p[0:FHID, :TC], AF.Sin, bias=fbc[:, k:k + 1], scale=fsc[:, k:k + 1])
                    self.tt("dve", s2[:], hs[:], hs[:], ALU.mult)
                    self.ts("dve", s2[:], s2[:], -4.0, 3.0, ALU.mult, ALU.add)
                    self.tt("dve", hh[k][:], s2[:], hs[:], ALU.mult)
                    cur = hh[k]
                    curK = FHID
                    yield
                for oc in range(4):
                    self.mm(p[:, :TC], w4[:, d_ * HW + oc * 128: d_ * HW + (oc + 1) * 128], cur[:], True, True)
                    dc, krr = dct[nd % 2], kr[nd % 2]
                    nd += 1
                    self.ld(dc[:], s.dec[d_, oc * 128:(oc + 1) * 128, t0:t0 + TC])
                    self.tt("dve", krr[:], p[:, :TC], dc[:], ALU.mult)
                    if d_ == 0:
                        self.stq(KB[oc * 128:(oc + 1) * 128, t0:t0 + TC], krr[:])
                    else:
                        pos = NFFT - L + 1 + t0
                        nn = TC if t0 + TC < L else TC - 1
                        self.stq(KB[oc * 128:(oc + 1) * 128, pos:pos + nn], krr[:, :nn])
                    yield

    def spectrum_gen(self, jobs, pA, pX):
        fw, S = self.fw, self.S
        k2s = [fw.sb("k2_%d" % i, [128, 32, 128], BF16) for i in range(2)]
        Bt = [fw.sb("Bt%d" % i, [128, 2, 2, 128], BF16) for i in range(2)]
        tmp1 = [fw.sb("cm1_%d" % i, [128, 2, 2, 128], F32) for i in range(2)]
        tmp2 = [fw.sb("cm2_%d" % i, [128, 2, 2, 128], F32) for i in range(2)]
        xo = [fw.sb("xo%d" % i, [128, 2, 2, 128], F32) for i in range(3)]
        nq = 0
        for jb in jobs:
            KB, KF = S["KB%d" % jb], S["KF%d" % jb]
            for q in range(16):
                k2 = k2s[nq % 2]
                nq += 1
                self.ld(k2[:], KB[q * 32:(q + 1) * 32, :].rearrange("ch (n1 n2) -> n1 ch n2", n2=128))
                for it in range(17):
                    g = it
                    if 0 <= g < 16:
                        i2 = g % 2
                        self.fft_s1(k2, 128, g * 2, Bt[i2], tmp1[i2], tmp2[i2], pA)
                    g = it - 1
                    if 0 <= g < 16:
                        i2 = g % 2
                        xoi = xo[g % 3]
                        self.fft_s2(Bt[i2], pX)
                        self.act(xoi[:], self.p4(pX), AF.Identity, scale=1.0 / NFFT)
                        self.stq(KF[:, q * 32 + g * 2: q * 32 + g * 2 + 2, :, :], xoi[:])
                    yield

    def merge_phase(self, l, streams):
        fw, S = self.fw, self.S
        fw.begin_phase()
        woa = fw.sb("woa", [128, KC, D], BF16)
        woh = fw.sb("woh", [128, 4, D], BF16)
        wout = fw.sb("wout", [128, KC, D], BF16)
        self.load_w(woa, S["wb_oa"][l], KC)
        self.load_w(woh, S["wb_oh"][l], 4)
        self.load_w(wout, S["wb_out"][l], KC)
        xs2 = [fw.sb("xs%d" % i, [128, KC, 512], F32) for i in range(2)]
        at2 = [fw.sb("at%d" % i, [128, KC, 512], BF16) for i in range(2)]
        hy2 = [fw.sb("hy%d" % i, [128, 4, 512], BF16) for i in range(2)]
        g2 = [fw.sb("g%d" % i, [128, 16, 512], BF16) for i in range(2)]
        mg = fw.sb("mg", [128, KC, 512], BF16)
        m1 = [fw.sb("m1_%d" % i, [128, 512], F32) for i in range(2)]
        m2 = [fw.sb("m2_%d" % i, [128, 512], F32) for i in range(2)]
        mc = self.modcol[l]
        n = 0
        no = 0
        for s in streams:
            TC = s.TC
            for ch in range(s.T // TC):
                t0 = ch * TC
                xs, at, hy, g = xs2[n % 2], at2[n % 2], hy2[n % 2], g2[n % 2]
                n += 1
                self.ld(xs[:, :, :TC], s.XT[:, t0:t0 + TC].rearrange("(c p) t -> p c t", p=128))
                self.ld(at[:, :, :TC], s.AT[:, t0:t0 + TC].rearrange("(c p) t -> p c t", p=128))
                self.ld(hy[:, :, :TC], s.HY[:, t0:t0 + TC].rearrange("(c p) t -> p c t", p=128))
                self.ld(g[:, :, :TC], s.G[:, t0:t0 + TC].rearrange("(c p) t -> p c t", p=128))
                for oc in range(KC):
                    i2 = no % 2
                    no += 1
                    pa, pb = self.ps[i2], self.ps[2 + i2]
                    for kc in range(KC):
                        self.mm(pa[:, :TC], woa[:, kc, oc * 128:(oc + 1) * 128], at[:, kc, :TC], kc == 0, kc == KC - 1)
                    for kc in range(4):
                        self.mm(pb[:, :TC], woh[:, kc, oc * 128:(oc + 1) * 128], hy[:, kc, :TC], kc == 0, kc == 3)
                    self.tt("dve", m1[i2][:, :TC], pa[:, :TC], g[:, oc, :TC], ALU.mult)
                    self.tt("dve", m2[i2][:, :TC], pb[:, :TC], g[:, 8 + oc, :TC], ALU.mult)
                    self.tt("pool", mg[:, oc, :TC], m1[i2][:, :TC], m2[i2][:, :TC], ALU.add)
                for oc in range(KC):
                    p = self.ps[4 + oc % 4]
                    for kc in range(KC):
                        self.mm(p[:, :TC], wout[:, kc, oc * 128:(oc + 1) * 128], mg[:, kc, :TC], kc == 0, kc == KC - 1)
                    self.stt("dve", xs[:, oc, :TC], p[:, :TC], mc[:, 16 + oc, s.j:s.j + 1], xs[:, oc, :TC], ALU.mult, ALU.add)
                self.stq(s.XT[:, t0:t0 + TC].rearrange("(c p) t -> p c t", p=128), xs[:, :, :TC])
        fw.end_phase()

    def ffn_phase(self, l, streams):
        fw, S = self.fw, self.S
        fw.begin_phase()
        TC = 256
        wgu = fw.sb("wgu", [128, KC, 2 * DFF], BF16)
        wdn = fw.sb("wdn", [128, FC, D], BF16)
        self.load_w(wgu, S["wb_gu"][l], KC)
        self.load_w(wdn, S["wb_dn"][l], FC)
        RT = self.rms_tiles(TC)
        xs2 = [fw.sb("xs%d" % i, [128, KC, TC], F32) for i in range(2)]
        h2 = fw.sb("h2", [128, KC, TC], BF16)
        sg = [fw.sb("sg%d" % i, [128, TC], F32) for i in range(2)]
        sT = fw.sb("sT", [128, FC, TC], BF16)
        mc, A2 = self.modcol[l], self.A2[l]
        n = 0
        nj = 0
        for s in streams:
            for ch in range(s.T // TC):
                t0 = ch * TC
                xs = xs2[n % 2]
                n += 1
                self.ld(xs[:], s.XT[:, t0:t0 + TC].rearrange("(c p) t -> p c t", p=128))
                self.rms_mod(xs, TC, lambda c, j: A2[:, c, j:j + 1], lambda c, j: mc[:, 24 + c, j:j + 1], s.j, h2, RT)
                for j2 in range(FC):
                    i2 = nj % 2
                    nj += 1
                    pg, pu = self.ps[i2], self.ps[2 + i2]
                    for kc in range(KC):
                        self.mm(pg[:, :TC], wgu[:, kc, j2 * 128:(j2 + 1) * 128], h2[:, kc, :], kc == 0, kc == KC - 1)
                    for kc in range(KC):
                        self.mm(pu[:, :TC], wgu[:, kc, DFF + j2 * 128: DFF + (j2 + 1) * 128], h2[:, kc, :], kc == 0, kc == KC - 1)
                    self.act(sg[i2][:], pg[:, :TC], AF.Silu)
                    self.tt("dve", sT[:, j2, :], sg[i2][:], pu[:, :TC], ALU.mult)
                for oc in range(KC):
                    p = self.ps[4 + oc % 3]
                    for j2 in range(FC):
                        self.mm(p[:, :TC], wdn[:, j2, oc * 128:(oc + 1) * 128], sT[:, j2, :], j2 == 0, j2 == FC - 1)
                    self.stt("dve", xs[:, oc, :], p[:, :TC], mc[:, 40 + oc, s.j:s.j + 1], xs[:, oc, :], ALU.mult, ALU.add)
                self.stq(s.XT[:, t0:t0 + TC].rearrange("(c p) t -> p c t", p=128), xs[:])
        fw.end_phase()

    def final_phase(self):
        fw = self.fw
        s = self.lat
        fw.begin_phase()
        TC = 512
        RT = self.rms_tiles(TC)
        xs2 = [fw.sb("xs%d" % i, [128, KC, TC], F32) for i in range(2)]
        yT = fw.sb("yT", [128, KC, TC], F32)
        yt2 = [fw.sb("ytok%d" % i, [128, 4, D], F32) for i in range(2)]
        nf = self.nfin
        for ch in range(s.T // TC):
            t0 = ch * TC
            xs, yt = xs2[ch % 2], yt2[ch % 2]
            self.ld(xs[:], s.XT[:, t0:t0 + TC].rearrange("(c p) t -> p c t", p=128))
            self.rms_mod(xs, TC, lambda c, j: nf[:, c:c + 1], None, 0, yT, RT)
            for j in range(4):
                for half in range(2):
                    p = self.nps(0, 4)
                    for c4 in range(4):
                        c = half * 4 + c4
                        self.tr(p[:, c4 * 128:(c4 + 1) * 128], yT[:, c, j * 128:(j + 1) * 128], self.ident[:])
                    self.cp("act" if half == 0 else "dve", yt[:, j, half * 512:(half + 1) * 512], p[:])
            self.stq(self.out[t0:t0 + TC, :].rearrange("(j p) f -> p j f", p=128), yt[:])
        fw.end_phase()


def host_consts(SEQ):
    K = {}
    K["k_ident"] = np.eye(128, dtype=np.float32)
    r = np.zeros((128, 128), np.float32)
    for i in range(64):
        r[2 * i + 1, 2 * i] = -1.0
        r[2 * i, 2 * i + 1] = 1.0
    K["k_rmat"] = r
    a = np.arange(128, dtype=np.float64)
    ang = -2.0 * np.pi * np.outer(a, a) / 128.0
    Fr, Fi = np.cos(ang), np.sin(ang)
    K["k_ft"] = np.ascontiguousarray(np.stack([Fr, Fi, Fr, -Fi], axis=1)).astype(np.float32)
    angt = -2.0 * np.pi * np.outer(a, a) / NFFT
    K["k_tw"] = np.ascontiguousarray(np.stack([np.cos(angt), np.sin(angt)], axis=1)).astype(np.float32)
    GRID_W = 64
    rows = SEQ // GRID_W
    row = np.repeat(np.arange(rows, dtype=np.float32), GRID_W)
    col = np.tile(np.arange(GRID_W, dtype=np.float32), rows)
    inv_freq = (np.float32(10000.0) ** (-np.arange(0, 64, 2, dtype=np.float32) / np.float32(64))).astype(np.float32)
    angr = np.concatenate([row[:, None] * inv_freq, col[:, None] * inv_freq], axis=-1).astype(np.float32)
    cs, sn = np.cos(angr), np.sin(angr)
    K["k_ropec"] = np.ascontiguousarray(np.repeat(cs, 2, axis=1).T).astype(np.float32)
    K["k_ropes"] = np.ascontiguousarray(np.repeat(sn, 2, axis=1).T).astype(np.float32)

    def ztab(L):
        t = np.linspace(0.0, 1.0, L, dtype=np.float32)[:, None]
        w = (np.float32(2.0 * math.pi / L) * np.arange(L, dtype=np.float32))[:, None]
        f = np.linspace(1e-4, 15.0, 16, dtype=np.float32)[None, :]
        z = np.concatenate([t, np.cos(f * w), -np.sin(f * w)], axis=-1).astype(np.float32)
        max_decay = math.log(1e-2) / 0.3
        min_decay = math.log(1e-2) / 1.5
        deltas = np.abs(np.linspace(min_decay, max_decay, HW, dtype=np.float32))
        dec = np.exp(-t * deltas).astype(np.float32)
        zz = np.stack([z.T, z[::-1].T], axis=0)
        dd = np.stack([dec.T, dec[::-1].T], axis=0)
        return np.ascontiguousarray(zz).astype(np.float32), np.ascontiguousarray(dd).astype(np.float32)

    K["k_z_lat"], K["k_dec_lat"] = ztab(SEQ)
    K["k_z_ctx"], K["k_dec_ctx"] = ztab(CTX)
    return K


_NC_CACHE = {}


def run_cores(inputs, n_cores, dbg=None):
    x = np.asarray(inputs["x"], dtype=np.float32)
    SEQ = x.shape[1]
    DEPTH = np.asarray(inputs["w_mod"]).shape[0]
    key = (SEQ, DEPTH, tuple(sorted(dbg or [])))
    if key not in _NC_CACHE:
        nc = bass.Bass("TRN2", target_bir_lowering=False)
        b = Builder(nc, SEQ, DEPTH, dbg=dbg)
        b.build()
        _NC_CACHE[key] = nc
    nc = _NC_CACHE[key]
    K = host_consts(SEQ)
    shared = {k: np.ascontiguousarray(np.asarray(v, dtype=np.float32)) for k, v in inputs.items()
              if k not in ("x", "c", "ctx")}
    shared.update(K)
    in_maps = []
    for b_ in range(n_cores):
        m = dict(shared)
        m["x"] = np.ascontiguousarray(x[b_])
        m["c"] = np.ascontiguousarray(np.asarray(inputs["c"], dtype=np.float32)[b_])
        m["ctx"] = np.ascontiguousarray(np.asarray(inputs["ctx"], dtype=np.float32)[b_])
        in_maps.append(m)
    res = run_bass_kernel_spmd(nc, in_maps, core_ids=list(range(n_cores)))
    return res


def kernel(**inputs):
    res = run_cores(inputs, 8)
    out = np.stack([np.asarray(r["out"], dtype=np.float32) for r in res.results], axis=0)
    return out
```

```python
import math
from contextlib import ExitStack
import numpy as np
import concourse.bass as bass
import concourse.mybir as mybir
from concourse.bass_utils import run_bass_kernel_spmd

F32 = mybir.dt.float32
BF16 = mybir.dt.bfloat16
AF = mybir.ActivationFunctionType
ALU = mybir.AluOpType

D = 1024
KC = 8
NH = 8
NKV = 2
HD = 128
HW = 512
NPROJ = 5120
DFF = 2816
FC = 22
NMOD = 6
CTX = 256
NFFT = 16384
EPS = 1e-6
FEMB = 33
FHID = 64


class Res:
    __slots__ = ("name", "w", "r")

    def __init__(self, name=""):
        self.name = name
        self.w = None
        self.r = []


class View:
    __slots__ = ("ap", "res")

    def __init__(self, ap, res):
        self.ap = ap
        self.res = res


class Tl:
    def __init__(self, h, name):
        self.h = h
        self.res = Res(name)

    def __getitem__(self, k):
        return View(self.h[k], self.res)

    def v(self, ap):
        return View(ap, self.res)


def DV(ap):
    return View(ap, None)


class Eng:
    def __init__(self, name):
        self.name = name
        self.q = []
        self.sem = None
        self.cnt = 0
        self.seen = {}
        self.dsems = []
        self.dnext = 0


class FW:
    def __init__(self, nc, ndma=8):
        self.nc = nc
        self.st = ExitStack()
        self.E = {}
        for nm in ["pe", "act", "dve", "pool", "sp"]:
            e = Eng(nm)
            e.sem = self.st.enter_context(nc.semaphore("cs_" + nm))
            self.E[nm] = e
        for nm in ["sp", "pool", "act"]:
            e = self.E[nm]
            for i in range(ndma):
                s = self.st.enter_context(nc.semaphore("ds_%s%d" % (nm, i)))
                e.dsems.append([s, 0])
        self.n_ops = 0
        self.uid = 0
        self.phase_st = None
        self.stack = []

    def _alloc(self, name, shape, dt, psum):
        self.uid += 1
        nm = "%s_%d" % (name, self.uid)
        st = self.phase_st if self.phase_st is not None else self.st
        if psum:
            h = st.enter_context(self.nc.psum_tensor(nm, list(shape), dt))
        else:
            h = st.enter_context(self.nc.sbuf_tensor(nm, list(shape), dt))
        return Tl(h, nm)

    def sb(self, name, shape, dt):
        return self._alloc(name, shape, dt, False)

    def ps(self, name, shape, dt):
        return self._alloc(name, shape, dt, True)

    def _wait(self, e, deps):
        need = {}
        for d in deps:
            if d is None:
                continue
            sem, val, owner = d
            if owner == e.name and e.name == "pe":
                continue
            k = id(sem)
            if e.seen.get(k, 0) >= val:
                continue
            if k not in need or need[k][1] < val:
                need[k] = (sem, val)
        for k, (sem, val) in need.items():
            e.seen[k] = val
            e.q.append(lambda h, sem=sem, val=val: h.wait_ge(sem, val))

    def _deps(self, reads, writes):
        deps = []
        for r in reads:
            deps.append(r.w)
        for w in writes:
            deps.append(w.w)
            deps.extend(w.r)
        return deps

    def _commit(self, tok, reads, writes):
        for r in reads:
            r.r.append(tok)
        for w in writes:
            w.w = tok
            w.r = []

    def op(self, eng, fn, reads=(), writes=()):
        e = self.E[eng]
        self._wait(e, self._deps(reads, writes))
        e.cnt += 1
        sem = e.sem
        e.q.append(lambda h, fn=fn, sem=sem: fn(h).then_inc(sem, 1))
        tok = (sem, e.cnt, e.name)
        self._commit(tok, reads, writes)
        self.n_ops += 1
        return tok

    def dma(self, eng, out, in_, reads=(), writes=(), **kw):
        e = self.E[eng]
        self._wait(e, self._deps(reads, writes))
        slot = e.dsems[e.dnext]
        e.dnext = (e.dnext + 1) % len(e.dsems)
        sem, val = slot
        if val > 0 and e.seen.get(id(sem), 0) < val:
            e.seen[id(sem)] = val
            e.q.append(lambda h, sem=sem, val=val: h.wait_ge(sem, val))
        slot[1] = val + 16
        e.q.append(lambda h, out=out, in_=in_, sem=sem, kw=kw:
                   h.dma_start(out=out, in_=in_, **kw).then_inc(sem, 16))
        tok = (sem, val + 16, "dma_" + e.name)
        self._commit(tok, reads, writes)
        self.n_ops += 1
        return tok

    def barrier(self):
        toks = []
        for e in self.E.values():
            if e.cnt > 0:
                toks.append((e.sem, e.cnt, e.name))
            for sem, val in e.dsems:
                if val > 0:
                    toks.append((sem, val, "dma_" + e.name))
        for e in self.E.values():
            need = {}
            for sem, val, owner in toks:
                if owner == e.name:
                    continue
                k = id(sem)
                if e.seen.get(k, 0) >= val:
                    continue
                need[k] = (sem, val)
            for k, (sem, val) in need.items():
                e.seen[k] = val
                e.q.append(lambda h, sem=sem, val=val: h.wait_ge(sem, val))

    def begin_phase(self):
        self.barrier()
        self.stack.append(self.phase_st)
        self.phase_st = ExitStack()

    def end_phase(self):
        self.barrier()
        self.phase_st.close()
        self.phase_st = self.stack.pop()

    def finish(self):
        self.barrier()
        nc = self.nc
        E = self.E
        with nc.Block() as block:
            @block.tensor
            def _(h):
                for f in E["pe"].q:
                    f(h)

            @block.scalar
            def _(h):
                for f in E["act"].q:
                    f(h)

            @block.vector
            def _(h):
                for f in E["dve"].q:
                    f(h)

            @block.gpsimd
            def _(h):
                for f in E["pool"].q:
                    f(h)

            @block.sync
            def _(h):
                for f in E["sp"].q:
                    f(h)
        self.st.close()


def _flat(vs):
    out = []
    for v in vs:
        if isinstance(v, View) and v.res is not None:
            if isinstance(v.res, (tuple, list)):
                out.extend(v.res)
            else:
                out.append(v.res)
    return out


def _rw(ins, outs):
    return _flat(ins), _flat(outs)


def _a(x):
    return x.ap if isinstance(x, View) else x


class Stream:
    pass


class Builder:
    def __init__(self, nc, SEQ, DEPTH, dbg=None):
        self.nc = nc
        self.SEQ = SEQ
        self.DEPTH = DEPTH
        self.fw = FW(nc)
        self.dbg = dbg or {}
        self.rr = 0

    def mm(self, out, lhsT, rhs, start, stop, skip=False):
        r, w = _rw([lhsT, rhs], [out])
        o, a, b = out.ap, lhsT.ap, rhs.ap
        self.fw.op("pe", lambda h: h.matmul(o, a, b, start=start, stop=stop, skip_group_check=skip), r, w)

    def tr(self, out, in_, ident):
        r, w = _rw([in_, ident], [out])
        o, a, b = out.ap, in_.ap, ident.ap
        self.fw.op("pe", lambda h: h.transpose(o, a, b), r, w)

    def act(self, out, in_, func, bias=None, scale=None):
        r, w = _rw([in_, bias, scale], [out])
        kw = {}
        if bias is not None:
            kw["bias"] = _a(bias)
        if scale is not None:
            kw["scale"] = _a(scale)
        o, a = out.ap, in_.ap
        self.fw.op("act", lambda h: h.activation(o, a, func, **kw), r, w)

    def tt(self, eng, out, in0, in1, op):
        r, w = _rw([in0, in1], [out])
        o, a, b = out.ap, in0.ap, in1.ap
        self.fw.op(eng, lambda h: h.tensor_tensor(o, a, b, op), r, w)

    def stt(self, eng, out, in0, scalar, in1, op0, op1):
        r, w = _rw([in0, scalar, in1], [out])
        o, a, s, b = out.ap, in0.ap, _a(scalar), in1.ap
        self.fw.op(eng, lambda h: h.scalar_tensor_tensor(o, a, s, b, op0, op1), r, w)

    def ts(self, eng, out, in0, s1, s2, op0, op1):
        r, w = _rw([in0, s1, s2], [out])
        o, a, x1, x2 = out.ap, in0.ap, _a(s1), _a(s2)
        self.fw.op(eng, lambda h: h.tensor_scalar(o, a, x1, x2, op0, op1), r, w)

    def cp(self, eng, out, in_):
        r, w = _rw([in_], [out])
        o, a = out.ap, in_.ap
        if eng == "act":
            self.fw.op("act", lambda h: h.activation(o, a, AF.Copy), r, w)
        else:
            self.fw.op(eng, lambda h: h.tensor_copy(o, a), r, w)

    def recip(self, out, in_):
        r, w = _rw([in_], [out])
        o, a = out.ap, in_.ap
        self.fw.op("dve", lambda h: h.reciprocal(o, a), r, w)

    def memset(self, eng, out, val):
        r, w = _rw([], [out])
        o = out.ap
        self.fw.op(eng, lambda h: h.memset(o, val), r, w)

    def dma(self, eng, out, in_, **kw):
        r, w = _rw([in_], [out])
        return self.fw.dma(eng, out.ap, in_.ap, r, w, **kw)

    def ld(self, out, in_ap, **kw):
        self.dma("sp", out, DV(in_ap), **kw)

    def stq(self, out_ap, in_, **kw):
        self.dma("pool", DV(out_ap), in_, **kw)

    def dram_in(self, name, shape, dt=F32):
        return self.nc.dram_tensor(name, list(shape), dt, kind="ExternalInput").ap()

    def dram_scratch(self, name, shape, dt):
        kind = "ExternalOutput" if name in self.dbg else "Internal"
        return self.nc.dram_tensor(name, list(shape), dt, kind=kind).ap()

    def build(self):
        nc, fw, SEQ, DEPTH = self.nc, self.fw, self.SEQ, self.DEPTH
        I = {}
        I["x"] = self.dram_in("x", [SEQ, D])
        I["c"] = self.dram_in("c", [D])
        I["ctx"] = self.dram_in("ctx", [CTX, D])
        I["c_ctx"] = self.dram_in("c_ctx", [D])
        I["w_mod"] = self.dram_in("w_mod", [DEPTH, D, NMOD * D])
        I["b_mod"] = self.dram_in("b_mod", [DEPTH, NMOD * D])
        I["norm_mix"] = self.dram_in("norm_mix", [DEPTH, D])
        I["w_in"] = self.dram_in("w_in", [DEPTH, D, NPROJ])
        I["q_norm"] = self.dram_in("q_norm", [DEPTH, HD])
        I["k_norm"] = self.dram_in("k_norm", [DEPTH, HD])
        I["conv_w"] = self.dram_in("conv_w", [DEPTH, 3, 3 * HW])
        I["conv_b"] = self.dram_in("conv_b", [DEPTH, 3 * HW])
        I["filt_w1"] = self.dram_in("filt_w1", [DEPTH, FEMB, FHID])
        I["filt_b1"] = self.dram_in("filt_b1", [DEPTH, FHID])
        I["filt_w2"] = self.dram_in("filt_w2", [DEPTH, FHID, FHID])
        I["filt_b2"] = self.dram_in("filt_b2", [DEPTH, FHID])
        I["filt_w3"] = self.dram_in("filt_w3", [DEPTH, FHID, FHID])
        I["filt_b3"] = self.dram_in("filt_b3", [DEPTH, FHID])
        I["filt_w4"] = self.dram_in("filt_w4", [DEPTH, FHID, 2 * HW])
        I["filt_freq"] = self.dram_in("filt_freq", [DEPTH, 3, FHID])
        I["hyena_bias"] = self.dram_in("hyena_bias", [DEPTH, HW])
        I["w_o_attn"] = self.dram_in("w_o_attn", [DEPTH, D, D])
        I["w_o_hyena"] = self.dram_in("w_o_hyena", [DEPTH, HW, D])
        I["w_out"] = self.dram_in("w_out", [DEPTH, D, D])
        I["norm_ffn"] = self.dram_in("norm_ffn", [DEPTH, D])
        I["w_gate_up"] = self.dram_in("w_gate_up", [DEPTH, D, 2 * DFF])
        I["w_down"] = self.dram_in("w_down", [DEPTH, DFF, D])
        I["norm_final"] = self.dram_in("norm_final", [D])
        I["k_ident"] = self.dram_in("k_ident", [128, 128])
        I["k_rmat"] = self.dram_in("k_rmat", [128, 128])
        I["k_ft"] = self.dram_in("k_ft", [128, 4, 128])
        I["k_tw"] = self.dram_in("k_tw", [128, 2, 128])
        I["k_ropec"] = self.dram_in("k_ropec", [128, SEQ])
        I["k_ropes"] = self.dram_in("k_ropes", [128, SEQ])
        I["k_z_lat"] = self.dram_in("k_z_lat", [2, FEMB, SEQ])
        I["k_z_ctx"] = self.dram_in("k_z_ctx", [2, FEMB, CTX])
        I["k_dec_lat"] = self.dram_in("k_dec_lat", [2, HW, SEQ])
        I["k_dec_ctx"] = self.dram_in("k_dec_ctx", [2, HW, CTX])
        self.I = I
        self.out = nc.dram_tensor("out", [SEQ, D], F32, kind="ExternalOutput").ap()

        S = {}
        sc = self.dram_scratch
        S["wb_in"] = sc("wb_in", [DEPTH, D, NPROJ], BF16)
        S["wb_oa"] = sc("wb_oa", [DEPTH, D, D], BF16)
        S["wb_oh"] = sc("wb_oh", [DEPTH, HW, D], BF16)
        S["wb_out"] = sc("wb_out", [DEPTH, D, D], BF16)
        S["wb_gu"] = sc("wb_gu", [DEPTH, D, 2 * DFF], BF16)
        S["wb_dn"] = sc("wb_dn", [DEPTH, DFF, D], BF16)
        for jb in range(3):
            S["KB%d" % jb] = sc("KB%d" % jb, [HW, NFFT], BF16)
            S["KF%d" % jb] = sc("KF%d" % jb, [128, HW, 2, 128], F32)
        self.S = S

        def mkstream(name, T, j, rope, key_off):
            s = Stream()
            s.name, s.T, s.j, s.rope, s.key_off = name, T, j, rope, key_off
            s.TC = min(512, T)
            s.XT = sc("XT_" + name, [D, T], F32)
            s.Qs = sc("Qs_" + name, [NH, HD, T], BF16)
            s.AT = sc("AT_" + name, [D, T], BF16)
            s.U = sc("U_" + name, [3 * HW, T], F32)
            s.G = sc("G_" + name, [2 * D, T], BF16)
            s.VV = sc("VV_" + name, [HW, T], F32)
            s.VB = sc("VB_" + name, [HW, T], BF16)
            s.YB = sc("YB_" + name, [HW, T], F32)
            s.HY = sc("HY_" + name, [HW, T], BF16)
            return s

        self.lat = mkstream("lat", SEQ, 0, True, CTX)
        self.cx = mkstream("ctx", CTX, 1, False, 0)
        self.lat.z, self.lat.dec = I["k_z_lat"], I["k_dec_lat"]
        self.cx.z, self.cx.dec = I["k_z_ctx"], I["k_dec_ctx"]
        self.lat.n_keys = CTX + SEQ
        self.cx.n_keys = CTX

        self.ident = fw.sb("ident", [128, 128], F32)
        self.onesb = fw.sb("onesb", [128, 128], BF16)
        self.onesf = fw.sb("onesf", [128, 128], F32)
        self.epsc = fw.sb("epsc", [128, 1], F32)
        self.rmat = fw.sb("rmat", [128, 128], BF16)
        self.ft = fw.sb("ft", [128, 4, 128], BF16)
        self.tw = fw.sb("tw", [128, 2, 128], F32)
        self.modcol = [fw.sb("modcol%d" % l, [128, 6 * KC, 2], F32) for l in range(DEPTH)]
        self.A1 = [fw.sb("A1_%d" % l, [128, KC, 2], F32) for l in range(DEPTH)]
        self.A2 = [fw.sb("A2_%d" % l, [128, KC, 2], F32) for l in range(DEPTH)]
        self.nfin = fw.sb("nfin", [128, KC], F32)
        self.psw = [fw.ps("psw%d" % i, [128, 1024], F32) for i in range(4)]
        self.ps = []
        for i in range(4):
            for hf in range(2):
                t = Tl(self.psw[i].h[:, hf * 512:(hf + 1) * 512], "psw%d_%d" % (i, hf))
                self.ps.append(t)

        self.jobs = [(self.lat, 0, 0), (self.cx, 0, 1)] + ([(self.lat, 1, 2)] if DEPTH > 1 else [])
        self.prologue()
        for l in range(DEPTH):
            last = (l == DEPTH - 1)
            self.layer_cols(l)
            self.lat.KF = S["KF0"] if l == 0 else S["KF2"]
            self.cx.KF = S["KF1"]
            self.attn_phase(l, last)
            self.ug_phase(l, last)
            streams = [self.lat] if last else [self.lat, self.cx]
            for s in streams:
                self.hyena_a(s, l)
                self.fft_conv(s)
                self.hyena_c(s, l)
            self.merge_phase(l, streams)
            self.ffn_phase(l, streams)
        self.final_phase()
        fw.barrier()
        self.cols_st.close()
        fw.finish()

    def nps(self, lo=0, hi=4):
        p = self.ps[lo + (self.rr % (hi - lo))]
        self.rr += 1
        return p

    def colvec(self, dst, src_ap):
        self.ld(dst, src_ap.rearrange("(c p) -> p c", p=128), allow_slow_non_contiguous=True)

    def prologue(self):
        fw, I, S = self.fw, self.I, self.S
        DEPTH, SEQ = self.DEPTH, self.SEQ
        fw.begin_phase()
        self.ld(self.ident[:], I["k_ident"])
        tmpf = fw.sb("tmpf", [128, 4, 128], F32)
        self.ld(tmpf[:, 0, :], I["k_rmat"])
        self.cp("dve", self.rmat[:], tmpf[:, 0, :])
        tmpf2 = fw.sb("tmpf2", [128, 4, 128], F32)
        self.ld(tmpf2[:], I["k_ft"])
        self.cp("dve", self.ft[:], tmpf2[:])
        self.ld(self.tw[:], I["k_tw"])
        self.memset("dve", self.onesb[:], 1.0)
        self.memset("dve", self.onesf[:], 1.0)
        self.memset("dve", self.epsc[:], EPS)
        self.colvec(self.nfin[:], I["norm_final"])
        for _ in self.cast_gen([("w_in", "wb_in", D, NPROJ, 0)]):
            pass
        ccol = fw.sb("ccol", [128, KC, 2], F32)
        scol = fw.sb("scol", [128, KC, 2], F32)
        self.colvec(ccol[:, :, 0], I["c"])
        self.colvec(ccol[:, :, 1], I["c_ctx"])
        self.act(scol[:], ccol[:], AF.Silu)
        bcol = fw.sb("bcol", [128, 6 * KC], F32)
        wm = [fw.sb("wm%d" % i, [128, KC, 512], F32) for i in range(2)]
        pm = self.ps[7]
        n = 0
        for l in range(DEPTH):
            for q4 in range(4):
                self.ld(bcol[:, q4 * 12:(q4 + 1) * 12],
                        I["b_mod"][l, q4 * 1536:(q4 + 1) * 1536].rearrange("(c p) -> p c", p=128),
                        allow_slow_non_contiguous=True)
            for cb in range(12):
                w = wm[n % 2]
                n += 1
                self.ld(w[:], I["w_mod"][l, :, cb * 512:(cb + 1) * 512].rearrange("(kc p) n -> p kc n", p=128))
                for f4 in range(4):
                    f = cb * 4 + f4
                    for kc in range(KC):
                        self.mm(pm[:, f * 2:f * 2 + 2], w[:, kc, f4 * 128:(f4 + 1) * 128], scol[:, kc, :],
                                kc == 0, kc == KC - 1, skip=True)
            pv = pm.v(pm.h[:, 0:96].rearrange("p (f j) -> p f j", j=2))
            self.tt("dve", self.modcol[l][:], pv, bcol.v(bcol.h[:].unsqueeze(2).broadcast_to([128, 48, 2])), ALU.add)
        fw.end_phase()
        fw.begin_phase()
        self.zt = fw.sb("zt", [128, 2048], BF16)
        self.memset("pool", self.zt[:], 0.0)
        xtok = [fw.sb("xtok%d" % i, [128, 4, D], F32) for i in range(2)]
        xts = [fw.sb("xts%d" % i, [128, KC, 512], F32) for i in range(2)]

        xtok_c = [fw.sb("xtokc", [128, 2, D], F32)]
        xts_c = [fw.sb("xtsc", [128, KC, 256], F32)]

        def xt_gen(src, s, xtok, xts):
            TC = s.TC
            nj = TC // 128
            for ch in range(s.T // TC):
                t0 = ch * TC
                xt, xs = xtok[ch % len(xtok)], xts[ch % len(xts)]
                self.ld(xt[:, :nj, :], src[t0:t0 + TC, :].rearrange("(j p) f -> p j f", p=128))
                for c in range(KC):
                    p = self.nps(0, 4)
                    for j in range(nj):
                        self.tr(p[:, j * 128:(j + 1) * 128], xt[:, j, c * 128:(c + 1) * 128], self.ident[:])
                    self.cp("act" if c % 2 == 0 else "dve", xs[:, c, :TC], p[:, :TC])
                    if c % 2 == 1:
                        yield
                self.stq(s.XT[:, t0:t0 + TC].rearrange("(c p) t -> p c t", p=128), xs[:, :, :TC])
                yield

        gens = [xt_gen(I["x"], self.lat, xtok, xts), xt_gen(I["ctx"], self.cx, xtok_c, xts_c)]
        for (st_, l_, jb) in self.jobs:
            gens.append(self.mlp_gen(st_, l_, S["KB%d" % jb], self.ps[4 + jb]))
        while gens:
            for g in list(gens):
                try:
                    next(g)
                except StopIteration:
                    gens.remove(g)
        fw.end_phase()

    def cast_gen(self, items):
        fw, I, S = self.fw, self.I, self.S
        stg = [fw.sb("stg%d" % i, [128, 2048], F32) for i in range(3)]
        stb = [fw.sb("stb%d" % i, [128, 2048], BF16) for i in range(3)]
        n = 0
        for (src, dst, K, N, l) in items:
            for rb in range(K // 128):
                for c0 in range(0, N, 2048):
                    cw = min(2048, N - c0)
                    a, b_ = stg[n % 3], stb[n % 3]
                    self.ld(a[:, :cw], I[src][l, rb * 128:(rb + 1) * 128, c0:c0 + cw])
                    self.cp("pool", b_[:, :cw], a[:, :cw])
                    self.stq(S[dst][l, rb * 128:(rb + 1) * 128, c0:c0 + cw], b_[:, :cw])
                    n += 1
                    yield

    def layer_cols(self, l):
        fw, I = self.fw, self.I
        if l > 0:
            fw.barrier()
            self.cols_st.close()
        self.cols_st = ExitStack()
        assert fw.phase_st is None
        fw.phase_st = self.cols_st
        nm = fw.sb("nmcol", [128, KC], F32)
        nf = fw.sb("nfcol", [128, KC], F32)
        self.colvec(nm[:], I["norm_mix"][l])
        self.colvec(nf[:], I["norm_ffn"][l])
        mc = self.modcol[l]
        self.stt("dve", self.A1[l][:], mc[:, 8:16, :], 1.0, nm.v(nm.h[:].unsqueeze(2).broadcast_to([128, KC, 2])), ALU.add, ALU.mult)
        self.stt("dve", self.A2[l][:], mc[:, 32:40, :], 1.0, nf.v(nf.h[:].unsqueeze(2).broadcast_to([128, KC, 2])), ALU.add, ALU.mult)
        self.gq = fw.sb("gq", [128, 1], F32)
        self.gk = fw.sb("gk", [128, 1], F32)
        self.ld(self.gq[:], I["q_norm"][l].rearrange("(p o) -> p o", o=1), allow_slow_non_contiguous=True)
        self.ld(self.gk[:], I["k_norm"][l].rearrange("(p o) -> p o", o=1), allow_slow_non_contiguous=True)
        grow = fw.sb("grow", [1, 2, 128], F32)
        self.ld(grow[:, 0, :], I["q_norm"][l].rearrange("(o n) -> o n", o=1))
        self.ld(grow[:, 1, :], I["k_norm"][l].rearrange("(o n) -> o n", o=1))
        gmax = fw.sb("gmax", [1, 2], F32)
        r, w = _rw([grow[:]], [gmax[:]])
        go, gi = gmax.h[:], grow.h[:]
        fw.op("dve", lambda h: h.tensor_reduce(go, gi, mybir.AxisListType.X, ALU.max, apply_absolute_value=True), r, w)
        nb = fw.sb("nb", [1, 2], F32)
        self.stt("dve", nb[:, 0:1], gmax[:, 0:1], -math.sqrt(128.0), gmax[:, 1:2], ALU.mult, ALU.mult)
        self.stt("dve", nb[:, 1:2], gmax[:, 0:1], -math.sqrt(128.0), gmax[:, 1:2], ALU.mult, ALU.mult)
        pn = self.ps[6]
        self.mm(pn[:, 0:2], self.onesf[0:1, :], nb[:], True, True)
        self.negB = fw.sb("negB", [128, 1], F32)
        self.cp("dve", self.negB[:], pn[:, 0:1])
        self.cw = fw.sb("cwcol", [128, 3, 12], F32)
        for tap in range(3):
            self.colvec(self.cw[:, tap, :], I["conv_w"][l, tap])
        self.cb = fw.sb("cbcol", [128, 12], F32)
        self.colvec(self.cb[:], I["conv_b"][l])
        self.hb = fw.sb("hbcol", [128, 4], F32)
        self.colvec(self.hb[:], I["hyena_bias"][l])
        fw.phase_st = None

    def rms_mod(self, xs, TC, A, Bm, j, out, T_):
        sq, sd, rstd, tmp = T_["sq"], T_["sd"], T_["rstd"], T_["tmp"]
        self.act(sq[:, :, :TC], xs[:, :, :TC], AF.Square)
        pS = self.ps[7]
        for c in range(KC):
            self.mm(pS[:, :TC], self.onesb[:], sq[:, c, :TC], c == 0, c == KC - 1)
        self.act(sd[:, :TC], pS[:, :TC], AF.Ln, bias=self.epsc[:], scale=1.0 / D)
        self.act(rstd[:, :TC], sd[:, :TC], AF.Exp, scale=-0.5)
        for c in range(KC):
            if Bm is None:
                self.stt("dve", out[:, c, :TC], xs[:, c, :TC], A(c, j), rstd[:, :TC], ALU.mult, ALU.mult)
            else:
                t = tmp[c % 2]
                self.stt("dve", t[:, :TC], xs[:, c, :TC], A(c, j), rstd[:, :TC], ALU.mult, ALU.mult)
                self.act(out[:, c, :TC], t[:, :TC], AF.Identity, bias=Bm(c, j))

    def rms_tiles(self, TC):
        fw = self.fw
        return {"sq": fw.sb("sq", [128, KC, TC], BF16), "sd": fw.sb("sd", [128, TC], F32),
                "rstd": fw.sb("rstd", [128, TC], F32), "tmp": [fw.sb("rtmp%d" % i, [128, TC], F32) for i in range(2)]}

    def load_w(self, dst, src, kchunks):
        for kc in range(kchunks):
            self.ld(dst[:, kc, :], src[kc * 128:(kc + 1) * 128, :])

    def attn_phase(self, l, last):
        fw, I, S = self.fw, self.I, self.S
        lat, cx = self.lat, self.cx
        fw.begin_phase()
        NK = lat.n_keys
        KT = fw.sb("KT", [128, NKV, NK], BF16)
        V = fw.sb("V", [128, NK // 128, NKV * HD], BF16)
        fw.begin_phase()
        wq = fw.sb("wqkv", [128, KC, 1536], BF16)
        for kc in range(KC):
            self.ld(wq[:, kc, :], S["wb_in"][l, kc * 128:(kc + 1) * 128, 0:1536])
        RT = self.rms_tiles(512)
        xs2 = [fw.sb("xs%d" % i, [128, KC, 512], F32) for i in range(2)]
        hT2 = [fw.sb("hT%d" % i, [128, KC, 512], BF16) for i in range(2)]
        rc2 = [fw.sb("rc%d" % i, [128, 512], F32) for i in range(2)]
        rs2 = [fw.sb("rs%d" % i, [128, 512], F32) for i in range(2)]
        sqh = [fw.sb("sqh%d" % i, [128, 512], BF16) for i in range(2)]
        qg = [fw.sb("qg%d" % i, [128, 512], BF16) for i in range(2)]
        sdh = [fw.sb("sdh%d" % i, [128, 512], F32) for i in range(2)]
        rsh = [fw.sb("rsh%d" % i, [128, 512], F32) for i in range(2)]
        t1 = [fw.sb("t1_%d" % i, [128, 512], F32) for i in range(2)]
        t2 = [fw.sb("t2_%d" % i, [128, 512], F32) for i in range(2)]
        qo = [fw.sb("qo%d" % i, [128, 512], BF16) for i in range(3)]
        mc = self.modcol[l]
        A1 = self.A1[l]
        hn = 0
        nchunk = 0
        for s in [cx, lat]:
            want_q = (s is lat) or (not last)
            TC = s.TC
            for ch in range(s.T // TC):
                t0 = ch * TC
                xs, hT = xs2[nchunk % 2], hT2[nchunk % 2]
                rc, rs = rc2[nchunk % 2], rs2[nchunk % 2]
                nchunk += 1
                self.ld(xs[:, :, :TC], s.XT[:, t0:t0 + TC].rearrange("(c p) t -> p c t", p=128))
                if s.rope:
                    self.ld(rc[:, :TC], I["k_ropec"][:, t0:t0 + TC])
                    self.ld(rs[:, :TC], I["k_ropes"][:, t0:t0 + TC])
                self.rms_mod(xs, TC, lambda c, j: A1[:, c, j:j + 1], lambda c, j: mc[:, c, j:j + 1], s.j, hT, RT)
                heads = ([("q", j) for j in range(NH)] if want_q else []) + [("k", 0), ("k", 1)]
                for (kind, j) in heads:
                    col0 = j * 128 if kind == "q" else D + j * 128
                    gcol = self.gq if kind == "q" else self.gk
                    i2 = hn % 2
                    hn += 1
                    p = self.nps(0, 4)
                    for kc in range(KC):
                        self.mm(p[:, :TC], wq[:, kc, col0:col0 + 128], hT[:, kc, :TC], kc == 0, kc == KC - 1)
                    self.act(sqh[i2][:, :TC], p[:, :TC], AF.Square)
                    self.act(qg[i2][:, :TC], p[:, :TC], AF.Identity, scale=gcol[:])
                    pa = self.ps[4 + i2]
                    self.mm(pa[:, :TC], self.onesb[:], sqh[i2][:, :TC], True, True)
                    self.act(sdh[i2][:, :TC], pa[:, :TC], AF.Ln, bias=self.epsc[:], scale=1.0 / HD)
                    self.act(rsh[i2][:, :TC], sdh[i2][:, :TC], AF.Exp, scale=-0.5)
                    if kind == "q":
                        qoi = qo[hn % 3]
                        dest = qoi[:, :TC]
                    else:
                        dest = KT[:, j, s.key_off + t0: s.key_off + t0 + TC]
                    if s.rope:
                        pb = self.ps[6]
                        self.mm(pb[:, :TC], self.rmat[:], qg[i2][:, :TC], True, True)
                        self.tt("dve", t1[i2][:, :TC], qg[i2][:, :TC], rc[:, :TC], ALU.mult)
                        self.tt("dve", t2[i2][:, :TC], pb[:, :TC], rs[:, :TC], ALU.mult)
                        self.tt("pool", t1[i2][:, :TC], t1[i2][:, :TC], t2[i2][:, :TC], ALU.add)
                        self.tt("dve", dest, t1[i2][:, :TC], rsh[i2][:, :TC], ALU.mult)
                    else:
                        self.tt("dve", dest, qg[i2][:, :TC], rsh[i2][:, :TC], ALU.mult)
                    if kind == "q":
                        self.stq(s.Qs[j, :, t0:t0 + TC], qoi[:, :TC])
                for tsub in range(TC // 128):
                    p = self.nps(0, 4)
                    for kc in range(KC):
                        self.mm(p[:, 0:256], hT[:, kc, tsub * 128:(tsub + 1) * 128], wq[:, kc, 1280:1536], kc == 0, kc == KC - 1)
                    self.cp("act", V[:, (s.key_off + t0) // 128 + tsub, :], p[:, 0:256])
        fw.end_phase()
        fw.begin_phase()
        qt3 = [fw.sb("qt%d" % i, [128, 512], BF16) for i in range(2)]
        pt3 = [fw.sb("pt%d" % i, [128, 2, 512], BF16) for i in range(4)]
        rl2 = [fw.sb("rl%d" % i, [128, 512], F32) for i in range(2)]
        ao2 = [fw.sb("ao%d" % i, [128, 512], BF16) for i in range(2)]
        SCALE = HD ** -0.5
        GRP = 4
        osb2 = [fw.sb("osb%d" % i, [128, 512], F32) for i in range(2)]
        acc2s = [fw.sb("acc2_%d" % i, [128, 2, 512], BF16) for i in range(2)]
        accfs = [fw.sb("accf_%d" % i, [128, 512], BF16) for i in range(2)]
        SHIFT = -8.0
        pairs = []
        for s in ([lat] if last else [lat, cx]):
            for h in range(NH):
                for qc in range(s.T // s.TC):
                    for j in range(s.n_keys // 256):
                        pairs.append((s, h, qc, j))
        st = {"nS": 0}
        pend = []
        SLOTS = [0, 1, 3]

        def next_slot():
            wb = SLOTS[st["nS"] % 3]
            st["nS"] += 1
            return wb

        def bg_banks():
            wb = st.get("free_wb", 0)
            return self.ps[wb * 2], self.ps[wb * 2 + 1]

        def emit_S(pi):
            s, h, qc, j = pairs[pi]
            TC = s.TC
            kvh = h // (NH // NKV)
            if j == 0:
                nq = st.get("nq", 0)
                st["nq"] = nq + 1
                qt = qt3[nq % 2]
                self.ld(qt[:, :TC], s.Qs[h, :, qc * TC:(qc + 1) * TC])
                st[("qt", s.name, h, qc)] = (qt, nq)
            qt, nq = st[("qt", s.name, h, qc)]
            wb = next_slot()
            st[("wb", pi)] = wb
            for a in range(2):
                kt = 2 * j + a
                p = self.ps[wb * 2 + a]
                self.mm(p[:, :TC], KT[:, kvh, kt * 128:(kt + 1) * 128], qt[:, :TC], True, True)

        def emit_rest(pi):
            s, h, qc, j = pairs[pi]
            TC = s.TC
            kvh = h // (NH // NKV)
            n_kt = s.n_keys // 128
            qt, nq = st[("qt", s.name, h, qc)]
            po = self.ps[4]
            pl = self.ps[5]
            wb = st[("wb", pi)]
            pw = self.psw[wb]
            pin = View(pw.h[:].rearrange("p (a n) -> p a n", a=2)[:, :, :TC], (self.ps[wb * 2].res, self.ps[wb * 2 + 1].res))
            pt = pt3[pi % 4]
            self.act(pt[:, :, :TC], pin, AF.Exp, bias=SHIFT, scale=SCALE)
            due = list(pend)
            del pend[:]
            for a in range(2):
                kt = 2 * j + a
                self.mm(po[:, :TC], V[:, kt, kvh * HD:(kvh + 1) * HD], pt[:, a, :TC], kt == 0, kt == n_kt - 1)
            for f_ in due:
                f_()
            npairs = n_kt // 2
            g0 = (j // GRP) * GRP
            gsz = min(GRP, npairs - g0)
            jj = j - g0
            ai = (st.get("na", 0)) % 2
            acc2, accf = acc2s[ai], accfs[ai]
            if gsz == 1:
                self.tt("dve", accf[:, :TC], pt[:, 0, :TC], pt[:, 1, :TC], ALU.add)
            elif jj == 0:
                st["ptprev"] = pt
            elif jj == 1:
                self.tt("dve", acc2[:, :, :TC], st["ptprev"][:, :, :TC], pt[:, :, :TC], ALU.add)
            else:
                self.tt("dve", acc2[:, :, :TC], acc2[:, :, :TC], pt[:, :, :TC], ALU.add)
            if jj == gsz - 1:
                if gsz > 1:
                    self.tt("dve", accf[:, :TC], acc2[:, 0, :TC], acc2[:, 1, :TC], ALU.add)
                first, lastg = (g0 == 0), (g0 + gsz == npairs)

                def emit_L(accf=accf, TC=TC, first=first, lastg=lastg):
                    self.mm(pl[:, :TC], self.onesb[:], accf[:, :TC], first, lastg)
                if lastg:
                    for f_ in pend:
                        f_()
                    del pend[:]
                    emit_L()
                else:
                    pend.append(emit_L)
                st["na"] = st.get("na", 0) + 1
            if j == n_kt // 2 - 1:
                rl, ao = rl2[nq % 2], ao2[nq % 2]
                osb = osb2[nq % 2]
                self.cp("act", osb[:, :TC], po[:, :TC])
                self.act(rl[:, :TC], pl[:, :TC], AF.Ln)
                self.act(rl[:, :TC], rl[:, :TC], AF.Exp, scale=-1.0)
                self.tt("dve", ao[:, :TC], osb[:, :TC], rl[:, :TC], ALU.mult)
                self.stq(s.AT[h * HD:(h + 1) * HD, qc * TC:(qc + 1) * TC], ao[:, :TC])

        bg = None
        if l == 0:
            items = []
            for ll in range(self.DEPTH):
                for (src, dst, K_, N_) in [("w_in", "wb_in", D, NPROJ), ("w_o_attn", "wb_oa", D, D), ("w_o_hyena", "wb_oh", HW, D),
                                           ("w_out", "wb_out", D, D), ("w_gate_up", "wb_gu", D, 2 * DFF), ("w_down", "wb_dn", DFF, D)]:
                    if not (src == "w_in" and ll == 0):
                        items.append((src, dst, K_, N_, ll))

            def chain():
                for x_ in self.spectrum_gen([jb for (_, _, jb) in self.jobs], bg_banks):
                    yield
                for x_ in self.cast_gen(items):
                    yield
            bg = chain()
        import os
        if bg is not None and os.environ.get("BG_FIRST"):
            for x_ in bg:
                pass
            bg = None
        nextS = 0
        for pi in range(len(pairs)):
            look = 2
            while nextS <= pi + look and nextS < len(pairs):
                emit_S(nextS)
                nextS += 1
            emit_rest(pi)
            st["free_wb"] = st[("wb", pi)]
            if bg is not None and pi % 3 == 2:
                if next(bg, "done") == "done":
                    bg = None
        if bg is not None:
            for x_ in bg:
                pass
        fw.end_phase()
        fw.end_phase()

    def ug_phase(self, l, last):
        fw, S = self.fw, self.S
        fw.begin_phase()
        NW = NPROJ - 1536
        w = fw.sb("wug", [128, KC, NW], BF16)
        for kc in range(KC):
            self.ld(w[:, kc, :], S["wb_in"][l, kc * 128:(kc + 1) * 128, 1536:NPROJ])
        RT = self.rms_tiles(512)
        xs2 = [fw.sb("xs%d" % i, [128, KC, 512], F32) for i in range(2)]
        hT2 = [fw.sb("hT%d" % i, [128, KC, 512], BF16) for i in range(2)]
        us2 = [fw.sb("us%d" % i, [128, 4, 512], F32) for i in range(2)]
        gs2 = [fw.sb("gs%d" % i, [128, 4, 512], BF16) for i in range(2)]
        mc, A1 = self.modcol[l], self.A1[l]
        n = 0
        nu = 0
        ng = 0
        for s in ([self.lat] if last else [self.lat, self.cx]):
            TC = s.TC
            for ch in range(s.T // TC):
                t0 = ch * TC
                xs, hT = xs2[n % 2], hT2[n % 2]
                n += 1
                self.ld(xs[:, :, :TC], s.XT[:, t0:t0 + TC].rearrange("(c p) t -> p c t", p=128))
                self.rms_mod(xs, TC, lambda c, j: A1[:, c, j:j + 1], lambda c, j: mc[:, c, j:j + 1], s.j, hT, RT)
                for o4 in range(3):
                    us = us2[nu % 2]
                    nu += 1
                    for oi in range(4):
                        oc = o4 * 4 + oi
                        p = self.nps(0, 6)
                        for kc in range(KC):
                            self.mm(p[:, :TC], w[:, kc, oc * 128:(oc + 1) * 128], hT[:, kc, :TC], kc == 0, kc == KC - 1)
                        self.cp("act" if oi % 2 == 0 else "dve", us[:, oi, :TC], p[:, :TC])
                    self.stq(s.U[o4 * 512:(o4 + 1) * 512, t0:t0 + TC].rearrange("(c p) t -> p c t", p=128), us[:, :, :TC])
                for o4 in range(4):
                    gs = gs2[ng % 2]
                    ng += 1
                    for oi in range(4):
                        oc = 12 + o4 * 4 + oi
                        p = self.nps(0, 6)
                        for kc in range(KC):
                            self.mm(p[:, :TC], w[:, kc, oc * 128:(oc + 1) * 128], hT[:, kc, :TC], kc == 0, kc == KC - 1)
                        self.act(gs[:, oi, :TC], p[:, :TC], AF.Sigmoid)
                    self.stq(s.G[o4 * 512:(o4 + 1) * 512, t0:t0 + TC].rearrange("(c p) t -> p c t", p=128), gs[:, :, :TC])
        fw.end_phase()

    def conv3(self, s, base_chunk, cc, t0, TCH, ub, out):
        T = s.T
        row0 = (base_chunk + cc) * 128
        lo, hi = t0 - 1, t0 + TCH + 1
        a = 0
        if lo < 0:
            self.memset("pool", ub[:, 0:1], 0.0)
            lo, a = 0, 1
        b = TCH + 2
        if hi > T:
            self.memset("pool", ub[:, TCH + 1:TCH + 2], 0.0)
            hi, b = T, TCH + 1
        self.ld(ub[:, a:b], s.U[row0:row0 + 128, lo:hi])
        k = base_chunk + cc
        cw, cb = self.cw, self.cb
        self.act(out[:, :TCH], ub[:, 1:TCH + 1], AF.Identity, bias=cb[:, k:k + 1], scale=cw[:, 1, k:k + 1])
        self.stt("dve", out[:, :TCH], ub[:, 0:TCH], cw[:, 0, k:k + 1], out[:, :TCH], ALU.mult, ALU.add)
        self.stt("dve", out[:, :TCH], ub[:, 2:TCH + 2], cw[:, 2, k:k + 1], out[:, :TCH], ALU.mult, ALU.add)

    def hyena_a(self, s, l):
        fw = self.fw
        fw.begin_phase()
        TCH = min(2048, s.T)
        ub2 = [fw.sb("ub%d" % i, [128, TCH + 2], F32) for i in range(4)]
        cx1 = [fw.sb("cx1_%d" % i, [128, TCH], F32) for i in range(2)]
        cv = [fw.sb("cv_%d" % i, [128, TCH], F32) for i in range(2)]
        vb = [fw.sb("vb_%d" % i, [128, TCH], BF16) for i in range(2)]
        n = 0
        for cc in range(4):
            for tch in range(s.T // TCH):
                t0 = tch * TCH
                i2 = n % 2
                self.conv3(s, 4, cc, t0, TCH, ub2[(2 * n) % 4], cx1[i2])
                self.conv3(s, 8, cc, t0, TCH, ub2[(2 * n + 1) % 4], cv[i2])
                n += 1
                self.tt("dve", cv[i2][:], cv[i2][:], cx1[i2][:], ALU.mult)
                self.cp("act", vb[i2][:], cv[i2][:])
                self.stq(s.VV[cc * 128:(cc + 1) * 128, t0:t0 + TCH], cv[i2][:])
                self.stq(s.VB[cc * 128:(cc + 1) * 128, t0:t0 + TCH], vb[i2][:])
        fw.end_phase()

    def hyena_c(self, s, l):
        fw = self.fw
        fw.begin_phase()
        TCH = min(2048, s.T)
        ub2 = [fw.sb("ub%d" % i, [128, TCH + 2], F32) for i in range(2)]
        cx0 = [fw.sb("cx0_%d" % i, [128, TCH], F32) for i in range(2)]
        yr = [fw.sb("yr_%d" % i, [128, TCH], F32) for i in range(2)]
        vv = [fw.sb("vv_%d" % i, [128, TCH], F32) for i in range(2)]
        hy = [fw.sb("hy_%d" % i, [128, TCH], BF16) for i in range(2)]
        n = 0
        for cc in range(4):
            for tch in range(s.T // TCH):
                t0 = tch * TCH
                i2 = n % 2
                n += 1
                self.conv3(s, 0, cc, t0, TCH, ub2[i2], cx0[i2])
                self.ld(yr[i2][:], s.YB[cc * 128:(cc + 1) * 128, t0:t0 + TCH])
                self.ld(vv[i2][:], s.VV[cc * 128:(cc + 1) * 128, t0:t0 + TCH])
                self.stt("dve", yr[i2][:], vv[i2][:], self.hb[:, cc:cc + 1], yr[i2][:], ALU.mult, ALU.add)
                self.tt("pool", hy[i2][:], yr[i2][:], cx0[i2][:], ALU.mult)
                self.stq(s.HY[cc * 128:(cc + 1) * 128, t0:t0 + TCH], hy[i2][:])
        fw.end_phase()

    def cmul(self, out, pin, tre, tim, conj, tmp1, tmp2, e_re="dve"):
        self.tt("dve", tmp1[:], pin, tre, ALU.mult)
        self.tt("dve", tmp2[:], pin, tim, ALU.mult)
        if not conj:
            self.tt(e_re, out[:, :, 0, :], tmp1[:, :, 0, :], tmp2[:, :, 1, :], ALU.subtract)
            self.tt("pool", out[:, :, 1, :], tmp2[:, :, 0, :], tmp1[:, :, 1, :], ALU.add)
        else:
            self.tt(e_re, out[:, :, 0, :], tmp1[:, :, 0, :], tmp2[:, :, 1, :], ALU.add)
            self.tt("pool", out[:, :, 1, :], tmp1[:, :, 1, :], tmp2[:, :, 0, :], ALU.subtract)

    def p4(self, p):
        return p.v(p.h[:].rearrange("p (c r k) -> p c r k", c=2, r=2))

    def tw_b(self, idx):
        return self.tw.v(self.tw.h[:, idx, :].unsqueeze(1).unsqueeze(1).broadcast_to([128, 2, 2, 128]))

    def fft_s1(self, v2, nK, c0, Bt, tmp1, tmp2, pA):
        ft = self.ft
        for c in range(2):
            self.mm(pA[:, c * 256:(c + 1) * 256], v2[0:nK, c0 + c, :],
                    ft.v(ft.h[0:nK, 0:2, :]), c == 0, c == 1, skip=True)
        self.cmul(Bt, self.p4(pA), self.tw_b(0), self.tw_b(1), False, tmp1, tmp2)

    def fft_s2(self, Bt, pX):
        ft = self.ft
        self.mm(pX[:], ft[:, 0, :], Bt[:], True, False, skip=True)
        pX4 = self.p4(pX)
        self.mm(View(pX4.ap[:, :, 0, :], pX.res), ft[:, 3, :], Bt[:, :, 1, :], False, False, skip=True)
        self.mm(View(pX4.ap[:, :, 1, :], pX.res), ft[:, 1, :], Bt[:, :, 0, :], False, True, skip=True)

    def fft_conv(self, s):
        fw = self.fw
        fw.begin_phase()
        T = s.T
        nK = T // 128
        ft = self.ft
        v2s = [fw.sb("v2_%d" % i, [max(nK, 2), 128, 128], BF16) for i in range(2)]
        y2 = fw.sb("y2", [max(nK, 2), 128, 128], F32)
        Bt = [fw.sb("Bt%d" % i, [128, 2, 2, 128], BF16) for i in range(2)]
        Yt = [fw.sb("Yt%d" % i, [128, 2, 2, 128], BF16) for i in range(2)]
        Ut = [fw.sb("Ut%d" % i, [128, 2, 2, 128], BF16) for i in range(2)]
        kf = [fw.sb("kf%d" % i, [128, 2, 2, 128], F32) for i in range(3)]
        tA = [[fw.sb("cmA%d_%d" % (k, i), [128, 2, 2, 128], F32) for i in range(2)] for k in range(3)]
        tB = [[fw.sb("cmB%d_%d" % (k, i), [128, 2, 2, 128], F32) for i in range(2)] for k in range(3)]
        NG = 64
        for cc in range(4):
            v2 = v2s[cc % 2]
            for q4 in range(4):
                self.ld(v2[0:nK, q4 * 32:(q4 + 1) * 32, :],
                        s.VB[cc * 128 + q4 * 32: cc * 128 + (q4 + 1) * 32, :].rearrange("ch (n1 n2) -> n1 ch n2", n2=128))
            for it in range(NG + 3):
                g = it
                if 0 <= g < NG:
                    i2 = g % 2
                    self.ld(kf[g % 3][:], s.KF[:, cc * 128 + g * 2: cc * 128 + g * 2 + 2, :, :])
                    self.fft_s1(v2, nK, g * 2, Bt[i2], tA[0][i2], tB[0][i2], self.ps[i2])
                g = it - 1
                if 0 <= g < NG:
                    i2 = g % 2
                    kfi = kf[g % 3]
                    pX = self.ps[2 + i2]
                    self.fft_s2(Bt[i2], pX)
                    kre = kfi.v(kfi.h[:, :, 0:1, :].broadcast_to([128, 2, 2, 128]))
                    kim = kfi.v(kfi.h[:, :, 1:2, :].broadcast_to([128, 2, 2, 128]))
                    self.cmul(Yt[i2], self.p4(pX), kre, kim, False, tA[1][i2], tB[1][i2], e_re="pool")
                g = it - 2
                if 0 <= g < NG:
                    i2 = g % 2
                    pU = self.ps[4 + i2]
                    for c in range(2):
                        self.mm(pU[:, c * 256:(c + 1) * 256], Yt[i2][:, c, 0, :], ft.v(ft.h[:, 2:4, :]), c == 0, False, skip=True)
                        self.mm(pU[:, c * 256:(c + 1) * 256], Yt[i2][:, c, 1, :], ft.v(ft.h[:, 1:3, :]), False, c == 1, skip=True)
                    self.cmul(Ut[i2], self.p4(pU), self.tw_b(0), self.tw_b(1), True, tA[2][i2], tB[2][i2], e_re="pool")
                g = it - 3
                if 0 <= g < NG:
                    i2 = g % 2
                    pY = self.ps[6 + i2]
                    pYv = View(pY.h[0:nK, 0:256].rearrange("p (c k) -> p c k", c=2), pY.res)
                    self.mm(pYv, ft[:, 0, 0:nK], Ut[i2][:, :, 0, :], True, False, skip=True)
                    self.mm(pYv, ft[:, 1, 0:nK], Ut[i2][:, :, 1, :], False, True, skip=True)
                    self.cp("act", y2[0:nK, g * 2:g * 2 + 2, :], pYv)
            for q4 in range(4):
                self.stq(s.YB[cc * 128 + q4 * 32: cc * 128 + (q4 + 1) * 32, :].rearrange("ch (m1 m2) -> m1 ch m2", m2=128),
                         y2[0:nK, q4 * 32:(q4 + 1) * 32, :])
        fw.end_phase()

    def mlp_gen(self, s, l, KB, pbank):
        fw, I = self.fw, self.I
        L = s.T
        TC = min(512, L)
        w1 = fw.sb("fw1", [FEMB, FHID], F32)
        w2 = fw.sb("fw2", [FHID, FHID], F32)
        w3 = fw.sb("fw3", [FHID, FHID], F32)
        w4 = fw.sb("fw4", [FHID, 2 * HW], F32)
        self.ld(w1[:], I["filt_w1"][l])
        self.ld(w2[:], I["filt_w2"][l])
        self.ld(w3[:], I["filt_w3"][l])
        self.ld(w4[:], I["filt_w4"][l])
        fq = fw.sb("fq", [FHID, 3], F32)
        fb = fw.sb("fb", [FHID, 3], F32)
        self.ld(fq[:], I["filt_freq"][l].rearrange("i p -> p i"), allow_slow_non_contiguous=True)
        for i, nm in enumerate(["filt_b1", "filt_b2", "filt_b3"]):
            self.ld(fb[:, i:i + 1], I[nm][l].rearrange("(p o) -> p o", o=1), allow_slow_non_contiguous=True)
        fsc = fw.sb("fsc", [FHID, 3], F32)
        fbc = fw.sb("fbc", [FHID, 3], F32)
        self.ts("dve", fsc[:], fq[:], 1.0 / 3.0, 0.0, ALU.mult, ALU.add)
        self.tt("dve", fbc[:], fsc[:], fb[:], ALU.mult)
        zt = self.zt
        z0, z1 = L, NFFT - L + 1
        for cc in range(4):
            p0 = z0
            while p0 < z1:
                n = min(2048, z1 - p0)
                self.stq(KB[cc * 128:(cc + 1) * 128, p0:p0 + n], zt[:, :n], allow_slow_non_contiguous=True)
                p0 += n
            yield
        zin = [fw.sb("zin%d" % i, [FEMB, TC], F32) for i in range(2)]
        hs = fw.sb("hs", [FHID, TC], F32)
        s2 = fw.sb("s2", [FHID, TC], F32)
        hh = [fw.sb("hh%d" % k, [FHID, TC], F32) for k in range(3)]
        dct = [fw.sb("dct%d" % i, [128, TC], F32) for i in range(2)]
        kr = [fw.sb("kr%d" % i, [128, TC], BF16) for i in range(2)]
        n = 0
        nd = 0
        p = pbank
        for d_ in range(2):
            for ch in range(L // TC):
                t0 = ch * TC
                i2 = n % 2
                n += 1
                self.ld(zin[i2][:], s.z[d_, :, t0:t0 + TC])
                cur = zin[i2]
                curK = FEMB
                for k, wk in enumerate([w1, w2, w3]):
                    self.mm(p[0:FHID, :TC], wk[0:curK, :], cur[0:curK, :], True, True)
                    self.act(hs[:], p[0:FHID, :TC], AF.Sin, bias=fbc[:, k:k + 1], scale=fsc[:, k:k + 1])
                    self.tt("dve", s2[:], hs[:], hs[:], ALU.mult)
                    self.ts("dve", s2[:], s2[:], -4.0, 3.0, ALU.mult, ALU.add)
                    self.tt("dve", hh[k][:], s2[:], hs[:], ALU.mult)
                    cur = hh[k]
                    curK = FHID
                    yield
                for oc in range(4):
                    self.mm(p[:, :TC], w4[:, d_ * HW + oc * 128: d_ * HW + (oc + 1) * 128], cur[:], True, True)
                    dc, krr = dct[nd % 2], kr[nd % 2]
                    nd += 1
                    self.ld(dc[:], s.dec[d_, oc * 128:(oc + 1) * 128, t0:t0 + TC])
                    self.tt("dve", krr[:], p[:, :TC], dc[:], ALU.mult)
                    if d_ == 0:
                        self.stq(KB[oc * 128:(oc + 1) * 128, t0:t0 + TC], krr[:])
                    else:
                        pos = NFFT - L + 1 + t0
                        nn = TC if t0 + TC < L else TC - 1
                        self.stq(KB[oc * 128:(oc + 1) * 128, pos:pos + nn], krr[:, :nn])
                    yield

    def spectrum_gen(self, jobs, get_banks):
        fw, S = self.fw, self.S
        k2s = [fw.sb("k2_%d" % i, [128, 32, 128], BF16) for i in range(2)]
        Bt = [fw.sb("Bt%d" % i, [128, 2, 2, 128], BF16) for i in range(2)]
        tmp1 = [fw.sb("cm1_%d" % i, [128, 2, 2, 128], F32) for i in range(2)]
        tmp2 = [fw.sb("cm2_%d" % i, [128, 2, 2, 128], F32) for i in range(2)]
        xo = [fw.sb("xo%d" % i, [128, 2, 2, 128], F32) for i in range(3)]
        nq = 0
        for jb in jobs:
            KB, KF = S["KB%d" % jb], S["KF%d" % jb]
            for q in range(16):
                k2 = k2s[nq % 2]
                nq += 1
                self.ld(k2[:], KB[q * 32:(q + 1) * 32, :].rearrange("ch (n1 n2) -> n1 ch n2", n2=128))
                for it in range(17):
                    pA, pX = get_banks()
                    g = it
                    if 0 <= g < 16:
                        i2 = g % 2
                        self.fft_s1(k2, 128, g * 2, Bt[i2], tmp1[i2], tmp2[i2], pA)
                    g = it - 1
                    if 0 <= g < 16:
                        i2 = g % 2
                        xoi = xo[g % 3]
                        self.fft_s2(Bt[i2], pX)
                        self.act(xoi[:], self.p4(pX), AF.Identity, scale=1.0 / NFFT)
                        self.stq(KF[:, q * 32 + g * 2: q * 32 + g * 2 + 2, :, :], xoi[:])
                    yield

    def merge_phase(self, l, streams):
        fw, S = self.fw, self.S
        fw.begin_phase()
        woa = fw.sb("woa", [128, KC, D], BF16)
        woh = fw.sb("woh", [128, 4, D], BF16)
        wout = fw.sb("wout", [128, KC, D], BF16)
        self.load_w(woa, S["wb_oa"][l], KC)
        self.load_w(woh, S["wb_oh"][l], 4)
        self.load_w(wout, S["wb_out"][l], KC)
        xs2 = [fw.sb("xs%d" % i, [128, KC, 512], F32) for i in range(2)]
        at2 = [fw.sb("at%d" % i, [128, KC, 512], BF16) for i in range(2)]
        hy2 = [fw.sb("hy%d" % i, [128, 4, 512], BF16) for i in range(2)]
        g2 = [fw.sb("g%d" % i, [128, 16, 512], BF16) for i in range(2)]
        mg = fw.sb("mg", [128, KC, 512], BF16)
        m1 = [fw.sb("m1_%d" % i, [128, 512], F32) for i in range(2)]
        m2 = [fw.sb("m2_%d" % i, [128, 512], F32) for i in range(2)]
        mc = self.modcol[l]
        n = 0
        no = 0
        for s in streams:
            TC = s.TC
            for ch in range(s.T // TC):
                t0 = ch * TC
                xs, at, hy, g = xs2[n % 2], at2[n % 2], hy2[n % 2], g2[n % 2]
                n += 1
                self.ld(xs[:, :, :TC], s.XT[:, t0:t0 + TC].rearrange("(c p) t -> p c t", p=128))
                self.ld(at[:, :, :TC], s.AT[:, t0:t0 + TC].rearrange("(c p) t -> p c t", p=128))
                self.ld(hy[:, :, :TC], s.HY[:, t0:t0 + TC].rearrange("(c p) t -> p c t", p=128))
                self.ld(g[:, :, :TC], s.G[:, t0:t0 + TC].rearrange("(c p) t -> p c t", p=128))
                for oc in range(KC):
                    i2 = no % 2
                    no += 1
                    pa, pb = self.ps[i2], self.ps[2 + i2]
                    for kc in range(KC):
                        self.mm(pa[:, :TC], woa[:, kc, oc * 128:(oc + 1) * 128], at[:, kc, :TC], kc == 0, kc == KC - 1)
                    for kc in range(4):
                        self.mm(pb[:, :TC], woh[:, kc, oc * 128:(oc + 1) * 128], hy[:, kc, :TC], kc == 0, kc == 3)
                    self.tt("dve", m1[i2][:, :TC], pa[:, :TC], g[:, oc, :TC], ALU.mult)
                    self.tt("dve", m2[i2][:, :TC], pb[:, :TC], g[:, 8 + oc, :TC], ALU.mult)
                    self.tt("pool", mg[:, oc, :TC], m1[i2][:, :TC], m2[i2][:, :TC], ALU.add)
                for oc in range(KC):
                    p = self.ps[4 + oc % 4]
                    for kc in range(KC):
                        self.mm(p[:, :TC], wout[:, kc, oc * 128:(oc + 1) * 128], mg[:, kc, :TC], kc == 0, kc == KC - 1)
                    self.stt("dve", xs[:, oc, :TC], p[:, :TC], mc[:, 16 + oc, s.j:s.j + 1], xs[:, oc, :TC], ALU.mult, ALU.add)
                self.stq(s.XT[:, t0:t0 + TC].rearrange("(c p) t -> p c t", p=128), xs[:, :, :TC])
        fw.end_phase()

    def ffn_phase(self, l, streams):
        fw, S = self.fw, self.S
        fw.begin_phase()
        TC = 256
        wgu = fw.sb("wgu", [128, KC, 2 * DFF], BF16)
        wdn = fw.sb("wdn", [128, FC, D], BF16)
        self.load_w(wgu, S["wb_gu"][l], KC)
        self.load_w(wdn, S["wb_dn"][l], FC)
        RT = self.rms_tiles(TC)
        xs2 = [fw.sb("xs%d" % i, [128, KC, TC], F32) for i in range(2)]
        h2 = fw.sb("h2", [128, KC, TC], BF16)
        sg = [fw.sb("sg%d" % i, [128, TC], F32) for i in range(2)]
        sT = fw.sb("sT", [128, FC, TC], BF16)
        mc, A2 = self.modcol[l], self.A2[l]
        n = 0
        nj = 0
        for s in streams:
            for ch in range(s.T // TC):
                t0 = ch * TC
                xs = xs2[n % 2]
                n += 1
                self.ld(xs[:], s.XT[:, t0:t0 + TC].rearrange("(c p) t -> p c t", p=128))
                self.rms_mod(xs, TC, lambda c, j: A2[:, c, j:j + 1], lambda c, j: mc[:, 24 + c, j:j + 1], s.j, h2, RT)
                for j2 in range(FC):
                    i2 = nj % 2
                    nj += 1
                    pg, pu = self.ps[i2], self.ps[2 + i2]
                    for kc in range(KC):
                        self.mm(pg[:, :TC], wgu[:, kc, j2 * 128:(j2 + 1) * 128], h2[:, kc, :], kc == 0, kc == KC - 1)
                    for kc in range(KC):
                        self.mm(pu[:, :TC], wgu[:, kc, DFF + j2 * 128: DFF + (j2 + 1) * 128], h2[:, kc, :], kc == 0, kc == KC - 1)
                    self.act(sg[i2][:], pg[:, :TC], AF.Silu)
                    self.tt("dve", sT[:, j2, :], sg[i2][:], pu[:, :TC], ALU.mult)
                for oc in range(KC):
                    p = self.ps[4 + oc % 3]
                    for j2 in range(FC):
                        self.mm(p[:, :TC], wdn[:, j2, oc * 128:(oc + 1) * 128], sT[:, j2, :], j2 == 0, j2 == FC - 1)
                    self.stt("dve", xs[:, oc, :], p[:, :TC], mc[:, 40 + oc, s.j:s.j + 1], xs[:, oc, :], ALU.mult, ALU.add)
                self.stq(s.XT[:, t0:t0 + TC].rearrange("(c p) t -> p c t", p=128), xs[:])
        fw.end_phase()

    def final_phase(self):
        fw = self.fw
        s = self.lat
        fw.begin_phase()
        TC = 512
        RT = self.rms_tiles(TC)
        xs2 = [fw.sb("xs%d" % i, [128, KC, TC], F32) for i in range(2)]
        yT = fw.sb("yT", [128, KC, TC], F32)
        yt2 = [fw.sb("ytok%d" % i, [128, 4, D], F32) for i in range(2)]
        nf = self.nfin
        for ch in range(s.T // TC):
            t0 = ch * TC
            xs, yt = xs2[ch % 2], yt2[ch % 2]
            self.ld(xs[:], s.XT[:, t0:t0 + TC].rearrange("(c p) t -> p c t", p=128))
            self.rms_mod(xs, TC, lambda c, j: nf[:, c:c + 1], None, 0, yT, RT)
            for j in range(4):
                for half in range(2):
                    p = self.nps(0, 4)
                    for c4 in range(4):
                        c = half * 4 + c4
                        self.tr(p[:, c4 * 128:(c4 + 1) * 128], yT[:, c, j * 128:(j + 1) * 128], self.ident[:])
                    self.cp("act" if half == 0 else "dve", yt[:, j, half * 512:(half + 1) * 512], p[:])
            self.stq(self.out[t0:t0 + TC, :].rearrange("(j p) f -> p j f", p=128), yt[:])
        fw.end_phase()


def host_consts(SEQ):
    K = {}
    K["k_ident"] = np.eye(128, dtype=np.float32)
    r = np.zeros((128, 128), np.float32)
    for i in range(64):
        r[2 * i + 1, 2 * i] = -1.0
        r[2 * i, 2 * i + 1] = 1.0
    K["k_rmat"] = r
    a = np.arange(128, dtype=np.float64)
    ang = -2.0 * np.pi * np.outer(a, a) / 128.0
    Fr, Fi = np.cos(ang), np.sin(ang)
    K["k_ft"] = np.ascontiguousarray(np.stack([Fr, Fi, Fr, -Fi], axis=1)).astype(np.float32)
    angt = -2.0 * np.pi * np.outer(a, a) / NFFT
    K["k_tw"] = np.ascontiguousarray(np.stack([np.cos(angt), np.sin(angt)], axis=1)).astype(np.float32)
    GRID_W = 64
    rows = SEQ // GRID_W
    row = np.repeat(np.arange(rows, dtype=np.float32), GRID_W)
    col = np.tile(np.arange(GRID_W, dtype=np.float32), rows)
    inv_freq = (np.float32(10000.0) ** (-np.arange(0, 64, 2, dtype=np.float32) / np.float32(64))).astype(np.float32)
    angr = np.concatenate([row[:, None] * inv_freq, col[:, None] * inv_freq], axis=-1).astype(np.float32)
    cs, sn = np.cos(angr), np.sin(angr)
    K["k_ropec"] = np.ascontiguousarray(np.repeat(cs, 2, axis=1).T).astype(np.float32)
    K["k_ropes"] = np.ascontiguousarray(np.repeat(sn, 2, axis=1).T).astype(np.float32)

    def ztab(L):
        t = np.linspace(0.0, 1.0, L, dtype=np.float32)[:, None]
        w = (np.float32(2.0 * math.pi / L) * np.arange(L, dtype=np.float32))[:, None]
        f = np.linspace(1e-4, 15.0, 16, dtype=np.float32)[None, :]
        z = np.concatenate([t, np.cos(f * w), -np.sin(f * w)], axis=-1).astype(np.float32)
        max_decay = math.log(1e-2) / 0.3
        min_decay = math.log(1e-2) / 1.5
        deltas = np.abs(np.linspace(min_decay, max_decay, HW, dtype=np.float32))
        dec = np.exp(-t * deltas).astype(np.float32)
        zz = np.stack([z.T, z[::-1].T], axis=0)
        dd = np.stack([dec.T, dec[::-1].T], axis=0)
        return np.ascontiguousarray(zz).astype(np.float32), np.ascontiguousarray(dd).astype(np.float32)

    K["k_z_lat"], K["k_dec_lat"] = ztab(SEQ)
    K["k_z_ctx"], K["k_dec_ctx"] = ztab(CTX)
    return K


_NC_CACHE = {}


def run_cores(inputs, n_cores, dbg=None):
    x = np.asarray(inputs["x"], dtype=np.float32)
    SEQ = x.shape[1]
    DEPTH = np.asarray(inputs["w_mod"]).shape[0]
    key = (SEQ, DEPTH, tuple(sorted(dbg or [])))
    if key not in _NC_CACHE:
        nc = bass.Bass("TRN2", target_bir_lowering=False)
        b = Builder(nc, SEQ, DEPTH, dbg=dbg)
        b.build()
        _NC_CACHE[key] = nc
    nc = _NC_CACHE[key]
    K = host_consts(SEQ)
    shared = {k: np.ascontiguousarray(np.asarray(v, dtype=np.float32)) for k, v in inputs.items()
              if k not in ("x", "c", "ctx")}
    shared.update(K)
    in_maps = []
    for b_ in range(n_cores):
        m = dict(shared)
        m["x"] = np.ascontiguousarray(x[b_])
        m["c"] = np.ascontiguousarray(np.asarray(inputs["c"], dtype=np.float32)[b_])
        m["ctx"] = np.ascontiguousarray(np.asarray(inputs["ctx"], dtype=np.float32)[b_])
        in_maps.append(m)
    res = run_bass_kernel_spmd(nc, in_maps, core_ids=list(range(n_cores)))
    return res


def kernel(**inputs):
    res = run_cores(inputs, 8)
    out = np.stack([np.asarray(r["out"], dtype=np.float32) for r in res.results], axis=0)
    return out
```

```python
import math
from contextlib import ExitStack
import numpy as np
import concourse.bass as bass
import concourse.mybir as mybir
from concourse.bass_utils import run_bass_kernel_spmd

F32 = mybir.dt.float32
BF16 = mybir.dt.bfloat16
AF = mybir.ActivationFunctionType
ALU = mybir.AluOpType

D = 1024
KC = 8
NH = 8
NKV = 2
HD = 128
HW = 512
NPROJ = 5120
DFF = 2816
FC = 22
NMOD = 6
CTX = 256
NFFT = 16384
EPS = 1e-6
FEMB = 33
FHID = 64


class Res:
    __slots__ = ("name", "w", "r")

    def __init__(self, name=""):
        self.name = name
        self.w = None
        self.r = []


class View:
    __slots__ = ("ap", "res")

    def __init__(self, ap, res):
        self.ap = ap
        self.res = res


class Tl:
    def __init__(self, h, name):
        self.h = h
        self.res = Res(name)

    def __getitem__(self, k):
        return View(self.h[k], self.res)

    def v(self, ap):
        return View(ap, self.res)


def DV(ap):
    return View(ap, None)


class Eng:
    def __init__(self, name):
        self.name = name
        self.q = []
        self.sem = None
        self.cnt = 0
        self.seen = {}
        self.dsems = []
        self.dnext = 0


class FW:
    def __init__(self, nc, ndma=8):
        self.nc = nc
        self.st = ExitStack()
        self.E = {}
        for nm in ["pe", "act", "dve", "pool", "sp"]:
            e = Eng(nm)
            e.sem = self.st.enter_context(nc.semaphore("cs_" + nm))
            self.E[nm] = e
        for nm in ["sp", "pool", "act"]:
            e = self.E[nm]
            for i in range(ndma):
                s = self.st.enter_context(nc.semaphore("ds_%s%d" % (nm, i)))
                e.dsems.append([s, 0])
        self.n_ops = 0
        self.uid = 0
        self.phase_st = None
        self.stack = []

    def _alloc(self, name, shape, dt, psum):
        self.uid += 1
        nm = "%s_%d" % (name, self.uid)
        st = self.phase_st if self.phase_st is not None else self.st
        if psum:
            h = st.enter_context(self.nc.psum_tensor(nm, list(shape), dt))
        else:
            h = st.enter_context(self.nc.sbuf_tensor(nm, list(shape), dt))
        return Tl(h, nm)

    def sb(self, name, shape, dt):
        return self._alloc(name, shape, dt, False)

    def ps(self, name, shape, dt):
        return self._alloc(name, shape, dt, True)

    def _wait(self, e, deps):
        need = {}
        for d in deps:
            if d is None:
                continue
            sem, val, owner = d
            if owner == e.name and e.name == "pe":
                continue
            k = id(sem)
            if e.seen.get(k, 0) >= val:
                continue
            if k not in need or need[k][1] < val:
                need[k] = (sem, val)
        for k, (sem, val) in need.items():
            e.seen[k] = val
            e.q.append(lambda h, sem=sem, val=val: h.wait_ge(sem, val))

    def _deps(self, reads, writes):
        deps = []
        for r in reads:
            deps.append(r.w)
        for w in writes:
            deps.append(w.w)
            deps.extend(w.r)
        return deps

    def _commit(self, tok, reads, writes):
        for r in reads:
            r.r.append(tok)
        for w in writes:
            w.w = tok
            w.r = []

    def op(self, eng, fn, reads=(), writes=()):
        e = self.E[eng]
        self._wait(e, self._deps(reads, writes))
        e.cnt += 1
        sem = e.sem
        e.q.append(lambda h, fn=fn, sem=sem: fn(h).then_inc(sem, 1))
        tok = (sem, e.cnt, e.name)
        self._commit(tok, reads, writes)
        self.n_ops += 1
        return tok

    def dma(self, eng, out, in_, reads=(), writes=(), **kw):
        e = self.E[eng]
        self._wait(e, self._deps(reads, writes))
        slot = e.dsems[e.dnext]
        e.dnext = (e.dnext + 1) % len(e.dsems)
        sem, val = slot
        if val > 0 and e.seen.get(id(sem), 0) < val:
            e.seen[id(sem)] = val
            e.q.append(lambda h, sem=sem, val=val: h.wait_ge(sem, val))
        slot[1] = val + 16
        e.q.append(lambda h, out=out, in_=in_, sem=sem, kw=kw:
                   h.dma_start(out=out, in_=in_, **kw).then_inc(sem, 16))
        tok = (sem, val + 16, "dma_" + e.name)
        self._commit(tok, reads, writes)
        self.n_ops += 1
        return tok

    def barrier(self):
        toks = []
        for e in self.E.values():
            if e.cnt > 0:
                toks.append((e.sem, e.cnt, e.name))
            for sem, val in e.dsems:
                if val > 0:
                    toks.append((sem, val, "dma_" + e.name))
        for e in self.E.values():
            need = {}
            for sem, val, owner in toks:
                if owner == e.name:
                    continue
                k = id(sem)
                if e.seen.get(k, 0) >= val:
                    continue
                need[k] = (sem, val)
            for k, (sem, val) in need.items():
                e.seen[k] = val
                e.q.append(lambda h, sem=sem, val=val: h.wait_ge(sem, val))

    def begin_phase(self):
        self.barrier()
        self.stack.append(self.phase_st)
        self.phase_st = ExitStack()

    def end_phase(self):
        self.barrier()
        self.phase_st.close()
        self.phase_st = self.stack.pop()

    def finish(self):
        self.barrier()
        nc = self.nc
        E = self.E
        with nc.Block() as block:
            @block.tensor
            def _(h):
                for f in E["pe"].q:
                    f(h)

            @block.scalar
            def _(h):
                for f in E["act"].q:
                    f(h)

            @block.vector
            def _(h):
                for f in E["dve"].q:
                    f(h)

            @block.gpsimd
            def _(h):
                for f in E["pool"].q:
                    f(h)

            @block.sync
            def _(h):
                for f in E["sp"].q:
                    f(h)
        self.st.close()


def _flat(vs):
    out = []
    for v in vs:
        if isinstance(v, View) and v.res is not None:
            if isinstance(v.res, (tuple, list)):
                out.extend(v.res)
            else:
                out.append(v.res)
    return out


def _rw(ins, outs):
    return _flat(ins), _flat(outs)


def _a(x):
    return x.ap if isinstance(x, View) else x


class Stream:
    pass


class Builder:
    def __init__(self, nc, SEQ, DEPTH, dbg=None):
        self.nc = nc
        self.SEQ = SEQ
        self.DEPTH = DEPTH
        self.fw = FW(nc)
        self.dbg = dbg or {}
        self.rr = 0

    def mm(self, out, lhsT, rhs, start, stop, skip=False):
        r, w = _rw([lhsT, rhs], [out])
        o, a, b = out.ap, lhsT.ap, rhs.ap
        self.fw.op("pe", lambda h: h.matmul(o, a, b, start=start, stop=stop, skip_group_check=skip), r, w)

    def tr(self, out, in_, ident):
        r, w = _rw([in_, ident], [out])
        o, a, b = out.ap, in_.ap, ident.ap
        self.fw.op("pe", lambda h: h.transpose(o, a, b), r, w)

    def act(self, out, in_, func, bias=None, scale=None):
        r, w = _rw([in_, bias, scale], [out])
        kw = {}
        if bias is not None:
            kw["bias"] = _a(bias)
        if scale is not None:
            kw["scale"] = _a(scale)
        o, a = out.ap, in_.ap
        self.fw.op("act", lambda h: h.activation(o, a, func, **kw), r, w)

    def tt(self, eng, out, in0, in1, op):
        r, w = _rw([in0, in1], [out])
        o, a, b = out.ap, in0.ap, in1.ap
        self.fw.op(eng, lambda h: h.tensor_tensor(o, a, b, op), r, w)

    def stt(self, eng, out, in0, scalar, in1, op0, op1):
        r, w = _rw([in0, scalar, in1], [out])
        o, a, s, b = out.ap, in0.ap, _a(scalar), in1.ap
        self.fw.op(eng, lambda h: h.scalar_tensor_tensor(o, a, s, b, op0, op1), r, w)

    def ts(self, eng, out, in0, s1, s2, op0, op1):
        r, w = _rw([in0, s1, s2], [out])
        o, a, x1, x2 = out.ap, in0.ap, _a(s1), _a(s2)
        self.fw.op(eng, lambda h: h.tensor_scalar(o, a, x1, x2, op0, op1), r, w)

    def cp(self, eng, out, in_):
        r, w = _rw([in_], [out])
        o, a = out.ap, in_.ap
        if eng == "act":
            self.fw.op("act", lambda h: h.activation(o, a, AF.Copy), r, w)
        else:
            self.fw.op(eng, lambda h: h.tensor_copy(o, a), r, w)

    def recip(self, out, in_):
        r, w = _rw([in_], [out])
        o, a = out.ap, in_.ap
        self.fw.op("dve", lambda h: h.reciprocal(o, a), r, w)

    def memset(self, eng, out, val):
        r, w = _rw([], [out])
        o = out.ap
        self.fw.op(eng, lambda h: h.memset(o, val), r, w)

    def dma(self, eng, out, in_, **kw):
        r, w = _rw([in_], [out])
        return self.fw.dma(eng, out.ap, in_.ap, r, w, **kw)

    def ld(self, out, in_ap, **kw):
        self.dma("sp", out, DV(in_ap), **kw)

    def stq(self, out_ap, in_, **kw):
        return self.dma("pool", DV(out_ap), in_, **kw)

    def dram_in(self, name, shape, dt=F32):
        return self.nc.dram_tensor(name, list(shape), dt, kind="ExternalInput").ap()

    def dram_scratch(self, name, shape, dt):
        kind = "ExternalOutput" if name in self.dbg else "Internal"
        return self.nc.dram_tensor(name, list(shape), dt, kind=kind).ap()

    def build(self):
        nc, fw, SEQ, DEPTH = self.nc, self.fw, self.SEQ, self.DEPTH
        I = {}
        I["x"] = self.dram_in("x", [SEQ, D])
        I["c"] = self.dram_in("c", [D])
        I["ctx"] = self.dram_in("ctx", [CTX, D])
        I["c_ctx"] = self.dram_in("c_ctx", [D])
        I["w_mod"] = self.dram_in("w_mod", [DEPTH, D, NMOD * D])
        I["b_mod"] = self.dram_in("b_mod", [DEPTH, NMOD * D])
        I["norm_mix"] = self.dram_in("norm_mix", [DEPTH, D])
        I["w_in"] = self.dram_in("w_in", [DEPTH, D, NPROJ])
        I["q_norm"] = self.dram_in("q_norm", [DEPTH, HD])
        I["k_norm"] = self.dram_in("k_norm", [DEPTH, HD])
        I["conv_w"] = self.dram_in("conv_w", [DEPTH, 3, 3 * HW])
        I["conv_b"] = self.dram_in("conv_b", [DEPTH, 3 * HW])
        I["filt_w1"] = self.dram_in("filt_w1", [DEPTH, FEMB, FHID])
        I["filt_b1"] = self.dram_in("filt_b1", [DEPTH, FHID])
        I["filt_w2"] = self.dram_in("filt_w2", [DEPTH, FHID, FHID])
        I["filt_b2"] = self.dram_in("filt_b2", [DEPTH, FHID])
        I["filt_w3"] = self.dram_in("filt_w3", [DEPTH, FHID, FHID])
        I["filt_b3"] = self.dram_in("filt_b3", [DEPTH, FHID])
        I["filt_w4"] = self.dram_in("filt_w4", [DEPTH, FHID, 2 * HW])
        I["filt_freq"] = self.dram_in("filt_freq", [DEPTH, 3, FHID])
        I["hyena_bias"] = self.dram_in("hyena_bias", [DEPTH, HW])
        I["w_o_attn"] = self.dram_in("w_o_attn", [DEPTH, D, D])
        I["w_o_hyena"] = self.dram_in("w_o_hyena", [DEPTH, HW, D])
        I["w_out"] = self.dram_in("w_out", [DEPTH, D, D])
        I["norm_ffn"] = self.dram_in("norm_ffn", [DEPTH, D])
        I["w_gate_up"] = self.dram_in("w_gate_up", [DEPTH, D, 2 * DFF])
        I["w_down"] = self.dram_in("w_down", [DEPTH, DFF, D])
        I["norm_final"] = self.dram_in("norm_final", [D])
        I["k_ident"] = self.dram_in("k_ident", [128, 128])
        I["k_rmat"] = self.dram_in("k_rmat", [128, 128])
        I["k_ft"] = self.dram_in("k_ft", [128, 4, 128])
        I["k_tw"] = self.dram_in("k_tw", [128, 2, 128])
        I["k_ropec"] = self.dram_in("k_ropec", [128, SEQ])
        I["k_ropes"] = self.dram_in("k_ropes", [128, SEQ])
        I["k_z_lat"] = self.dram_in("k_z_lat", [2, FEMB, SEQ])
        I["k_z_ctx"] = self.dram_in("k_z_ctx", [2, FEMB, CTX])
        I["k_dec_lat"] = self.dram_in("k_dec_lat", [2, HW, SEQ])
        I["k_dec_ctx"] = self.dram_in("k_dec_ctx", [2, HW, CTX])
        self.I = I
        self.out = nc.dram_tensor("out", [SEQ, D], F32, kind="ExternalOutput").ap()

        S = {}
        sc = self.dram_scratch
        S["wb_in"] = sc("wb_in", [DEPTH, D, NPROJ], BF16)
        S["wb_oa"] = sc("wb_oa", [DEPTH, D, D], BF16)
        S["wb_oh"] = sc("wb_oh", [DEPTH, HW, D], BF16)
        S["wb_out"] = sc("wb_out", [DEPTH, D, D], BF16)
        S["wb_gu"] = sc("wb_gu", [DEPTH, D, 2 * DFF], BF16)
        S["wb_dn"] = sc("wb_dn", [DEPTH, DFF, D], BF16)
        for jb in range(3):
            S["KB%d" % jb] = sc("KB%d" % jb, [HW, NFFT], BF16)
            S["KF%d" % jb] = sc("KF%d" % jb, [128, HW, 2, 128], F32)
        self.S = S

        def mkstream(name, T, j, rope, key_off):
            s = Stream()
            s.name, s.T, s.j, s.rope, s.key_off = name, T, j, rope, key_off
            s.TC = min(512, T)
            s.XT = sc("XT_" + name, [D, T], F32)
            s.Qs = sc("Qs_" + name, [NH, HD, T], BF16)
            s.AT = sc("AT_" + name, [D, T], BF16)
            s.U = sc("U_" + name, [3 * HW, T], F32)
            s.G = sc("G_" + name, [2 * D, T], BF16)
            s.VV = sc("VV_" + name, [HW, T], F32)
            s.VB = sc("VB_" + name, [HW, T], BF16)
            s.YB = sc("YB_" + name, [HW, T], F32)
            s.HY = sc("HY_" + name, [HW, T], BF16)
            return s

        self.lat = mkstream("lat", SEQ, 0, True, CTX)
        self.cx = mkstream("ctx", CTX, 1, False, 0)
        self.lat.z, self.lat.dec = I["k_z_lat"], I["k_dec_lat"]
        self.cx.z, self.cx.dec = I["k_z_ctx"], I["k_dec_ctx"]
        self.lat.n_keys = CTX + SEQ
        self.cx.n_keys = CTX

        self.ident = fw.sb("ident", [128, 128], F32)
        self.onesb = fw.sb("onesb", [128, 128], BF16)
        self.onesf = fw.sb("onesf", [128, 128], F32)
        self.epsc = fw.sb("epsc", [128, 1], F32)
        self.rmat = fw.sb("rmat", [128, 128], BF16)
        self.ft = fw.sb("ft", [128, 4, 128], BF16)
        self.tw = fw.sb("tw", [128, 2, 128], F32)
        self.modcol = [fw.sb("modcol%d" % l, [128, 6 * KC, 2], F32) for l in range(DEPTH)]
        self.A1 = [fw.sb("A1_%d" % l, [128, KC, 2], F32) for l in range(DEPTH)]
        self.A2 = [fw.sb("A2_%d" % l, [128, KC, 2], F32) for l in range(DEPTH)]
        self.nfin = fw.sb("nfin", [128, KC], F32)
        self.psw = [fw.ps("psw%d" % i, [128, 1024], F32) for i in range(4)]
        self.ps = []
        for i in range(4):
            for hf in range(2):
                t = Tl(self.psw[i].h[:, hf * 512:(hf + 1) * 512], "psw%d_%d" % (i, hf))
                self.ps.append(t)

        self.jobs = [(self.lat, 0, 0), (self.cx, 0, 1)] + ([(self.lat, 1, 2)] if DEPTH > 1 else [])
        self.prologue()
        for l in range(DEPTH):
            last = (l == DEPTH - 1)
            self.layer_cols(l)
            self.lat.KF = S["KF0"] if l == 0 else S["KF2"]
            self.cx.KF = S["KF1"]
            self.attn_phase(l, last)
            self.ug_phase(l, last)
            streams = [self.lat] if last else [self.lat, self.cx]
            for s in streams:
                self.fft_conv(s)
                self.hyena_c(s, l)
            self.merge_phase(l, streams)
            self.ffn_phase(l, streams)
        self.final_phase()
        fw.barrier()
        self.cols_st.close()
        fw.finish()

    def nps(self, lo=0, hi=4):
        p = self.ps[lo + (self.rr % (hi - lo))]
        self.rr += 1
        return p

    def colvec(self, dst, src_ap):
        self.ld(dst, src_ap.rearrange("(c p) -> p c", p=128), allow_slow_non_contiguous=True)

    def prologue(self):
        fw, I, S = self.fw, self.I, self.S
        DEPTH, SEQ = self.DEPTH, self.SEQ
        fw.begin_phase()
        self.ld(self.ident[:], I["k_ident"])
        tmpf = fw.sb("tmpf", [128, 4, 128], F32)
        self.ld(tmpf[:, 0, :], I["k_rmat"])
        self.cp("dve", self.rmat[:], tmpf[:, 0, :])
        tmpf2 = fw.sb("tmpf2", [128, 4, 128], F32)
        self.ld(tmpf2[:], I["k_ft"])
        self.cp("dve", self.ft[:], tmpf2[:])
        self.ld(self.tw[:], I["k_tw"])
        self.memset("dve", self.onesb[:], 1.0)
        self.memset("dve", self.onesf[:], 1.0)
        self.memset("dve", self.epsc[:], EPS)
        self.colvec(self.nfin[:], I["norm_final"])
        for _ in self.cast_gen([("w_in", "wb_in", D, NPROJ, 0)]):
            pass
        ccol = fw.sb("ccol", [128, KC, 2], F32)
        scol = fw.sb("scol", [128, KC, 2], F32)
        self.colvec(ccol[:, :, 0], I["c"])
        self.colvec(ccol[:, :, 1], I["c_ctx"])
        self.act(scol[:], ccol[:], AF.Silu)
        bcol = fw.sb("bcol", [128, 6 * KC], F32)
        wm = [fw.sb("wm%d" % i, [128, KC, 512], F32) for i in range(2)]
        pm = self.ps[7]
        n = 0
        for l in range(DEPTH):
            for q4 in range(4):
                self.ld(bcol[:, q4 * 12:(q4 + 1) * 12],
                        I["b_mod"][l, q4 * 1536:(q4 + 1) * 1536].rearrange("(c p) -> p c", p=128),
                        allow_slow_non_contiguous=True)
            for cb in range(12):
                w = wm[n % 2]
                n += 1
                self.ld(w[:], I["w_mod"][l, :, cb * 512:(cb + 1) * 512].rearrange("(kc p) n -> p kc n", p=128))
                for f4 in range(4):
                    f = cb * 4 + f4
                    for kc in range(KC):
                        self.mm(pm[:, f * 2:f * 2 + 2], w[:, kc, f4 * 128:(f4 + 1) * 128], scol[:, kc, :],
                                kc == 0, kc == KC - 1, skip=True)
            pv = pm.v(pm.h[:, 0:96].rearrange("p (f j) -> p f j", j=2))
            self.tt("dve", self.modcol[l][:], pv, bcol.v(bcol.h[:].unsqueeze(2).broadcast_to([128, 48, 2])), ALU.add)
        fw.end_phase()
        fw.begin_phase()
        self.zt = fw.sb("zt", [128, 2048], BF16)
        self.memset("pool", self.zt[:], 0.0)
        xtok = [fw.sb("xtok%d" % i, [128, 4, D], F32) for i in range(2)]
        xts = [fw.sb("xts%d" % i, [128, KC, 512], F32) for i in range(2)]

        xtok_c = [fw.sb("xtokc", [128, 2, D], F32)]
        xts_c = [fw.sb("xtsc", [128, KC, 256], F32)]

        def xt_gen(src, s, xtok, xts):
            TC = s.TC
            nj = TC // 128
            for ch in range(s.T // TC):
                t0 = ch * TC
                xt, xs = xtok[ch % len(xtok)], xts[ch % len(xts)]
                self.ld(xt[:, :nj, :], src[t0:t0 + TC, :].rearrange("(j p) f -> p j f", p=128))
                for c in range(KC):
                    p = self.nps(0, 4)
                    for j in range(nj):
                        self.tr(p[:, j * 128:(j + 1) * 128], xt[:, j, c * 128:(c + 1) * 128], self.ident[:])
                    self.cp("act" if c % 2 == 0 else "dve", xs[:, c, :TC], p[:, :TC])
                    if c % 2 == 1:
                        yield
                self.stq(s.XT[:, t0:t0 + TC].rearrange("(c p) t -> p c t", p=128), xs[:, :, :TC])
                yield

        gens = [xt_gen(I["x"], self.lat, xtok, xts), xt_gen(I["ctx"], self.cx, xtok_c, xts_c)]
        for (st_, l_, jb) in self.jobs:
            gens.append(self.mlp_gen(st_, l_, S["KB%d" % jb], self.ps[4 + jb]))
        while gens:
            for g in list(gens):
                try:
                    next(g)
                except StopIteration:
                    gens.remove(g)
        fw.end_phase()

    def cast_gen(self, items):
        fw, I, S = self.fw, self.I, self.S
        stg = [fw.sb("stg%d" % i, [128, 2048], F32) for i in range(3)]
        stb = [fw.sb("stb%d" % i, [128, 2048], BF16) for i in range(3)]
        n = 0
        for (src, dst, K, N, l) in items:
            for rb in range(K // 128):
                for c0 in range(0, N, 2048):
                    cw = min(2048, N - c0)
                    a, b_ = stg[n % 3], stb[n % 3]
                    self.ld(a[:, :cw], I[src][l, rb * 128:(rb + 1) * 128, c0:c0 + cw])
                    self.cp("pool", b_[:, :cw], a[:, :cw])
                    self.stq(S[dst][l, rb * 128:(rb + 1) * 128, c0:c0 + cw], b_[:, :cw])
                    n += 1
                    yield

    def layer_cols(self, l):
        fw, I = self.fw, self.I
        if l > 0:
            fw.barrier()
            self.cols_st.close()
        self.cols_st = ExitStack()
        assert fw.phase_st is None
        fw.phase_st = self.cols_st
        nm = fw.sb("nmcol", [128, KC], F32)
        nf = fw.sb("nfcol", [128, KC], F32)
        self.colvec(nm[:], I["norm_mix"][l])
        self.colvec(nf[:], I["norm_ffn"][l])
        mc = self.modcol[l]
        self.stt("dve", self.A1[l][:], mc[:, 8:16, :], 1.0, nm.v(nm.h[:].unsqueeze(2).broadcast_to([128, KC, 2])), ALU.add, ALU.mult)
        self.stt("dve", self.A2[l][:], mc[:, 32:40, :], 1.0, nf.v(nf.h[:].unsqueeze(2).broadcast_to([128, KC, 2])), ALU.add, ALU.mult)
        self.gq = fw.sb("gq", [128, 1], F32)
        self.gk = fw.sb("gk", [128, 1], F32)
        self.ld(self.gq[:], I["q_norm"][l].rearrange("(p o) -> p o", o=1), allow_slow_non_contiguous=True)
        self.ld(self.gk[:], I["k_norm"][l].rearrange("(p o) -> p o", o=1), allow_slow_non_contiguous=True)
        grow = fw.sb("grow", [1, 2, 128], F32)
        self.ld(grow[:, 0, :], I["q_norm"][l].rearrange("(o n) -> o n", o=1))
        self.ld(grow[:, 1, :], I["k_norm"][l].rearrange("(o n) -> o n", o=1))
        gmax = fw.sb("gmax", [1, 2], F32)
        r, w = _rw([grow[:]], [gmax[:]])
        go, gi = gmax.h[:], grow.h[:]
        fw.op("dve", lambda h: h.tensor_reduce(go, gi, mybir.AxisListType.X, ALU.max, apply_absolute_value=True), r, w)
        nb = fw.sb("nb", [1, 2], F32)
        self.stt("dve", nb[:, 0:1], gmax[:, 0:1], -math.sqrt(128.0), gmax[:, 1:2], ALU.mult, ALU.mult)
        self.stt("dve", nb[:, 1:2], gmax[:, 0:1], -math.sqrt(128.0), gmax[:, 1:2], ALU.mult, ALU.mult)
        pn = self.ps[6]
        self.mm(pn[:, 0:2], self.onesf[0:1, :], nb[:], True, True)
        self.negB = fw.sb("negB", [128, 1], F32)
        self.cp("dve", self.negB[:], pn[:, 0:1])
        self.cw = fw.sb("cwcol", [128, 3, 12], F32)
        for tap in range(3):
            self.colvec(self.cw[:, tap, :], I["conv_w"][l, tap])
        self.cb = fw.sb("cbcol", [128, 12], F32)
        self.colvec(self.cb[:], I["conv_b"][l])
        self.hb = fw.sb("hbcol", [128, 4], F32)
        self.colvec(self.hb[:], I["hyena_bias"][l])
        fw.phase_st = None

    def rms_mod(self, xs, TC, A, Bm, j, out, T_):
        sq, sd, rstd, tmp = T_["sq"], T_["sd"], T_["rstd"], T_["tmp"]
        self.act(sq[:, :, :TC], xs[:, :, :TC], AF.Square)
        pS = self.ps[7]
        for c in range(KC):
            self.mm(pS[:, :TC], self.onesb[:], sq[:, c, :TC], c == 0, c == KC - 1)
        self.act(sd[:, :TC], pS[:, :TC], AF.Ln, bias=self.epsc[:], scale=1.0 / D)
        self.act(rstd[:, :TC], sd[:, :TC], AF.Exp, scale=-0.5)
        for c in range(KC):
            if Bm is None:
                self.stt("dve", out[:, c, :TC], xs[:, c, :TC], A(c, j), rstd[:, :TC], ALU.mult, ALU.mult)
            else:
                t = tmp[c % 2]
                self.stt("dve", t[:, :TC], xs[:, c, :TC], A(c, j), rstd[:, :TC], ALU.mult, ALU.mult)
                self.act(out[:, c, :TC], t[:, :TC], AF.Identity, bias=Bm(c, j))

    def rms_tiles(self, TC):
        fw = self.fw
        return {"sq": fw.sb("sq", [128, KC, TC], BF16), "sd": fw.sb("sd", [128, TC], F32),
                "rstd": fw.sb("rstd", [128, TC], F32), "tmp": [fw.sb("rtmp%d" % i, [128, TC], F32) for i in range(2)]}

    def load_w(self, dst, src, kchunks):
        for kc in range(kchunks):
            self.ld(dst[:, kc, :], src[kc * 128:(kc + 1) * 128, :])

    def attn_phase(self, l, last):
        fw, I, S = self.fw, self.I, self.S
        lat, cx = self.lat, self.cx
        fw.begin_phase()
        NK = lat.n_keys
        KT = fw.sb("KT", [128, NKV, NK], BF16)
        V = fw.sb("V", [128, NK // 128, NKV * HD], BF16)
        fw.begin_phase()
        wq = fw.sb("wqkv", [128, KC, 1536], BF16)
        for kc in range(KC):
            self.ld(wq[:, kc, :], S["wb_in"][l, kc * 128:(kc + 1) * 128, 0:1536])
        RT = self.rms_tiles(512)
        xs2 = [fw.sb("xs%d" % i, [128, KC, 512], F32) for i in range(2)]
        hT2 = [fw.sb("hT%d" % i, [128, KC, 512], BF16) for i in range(2)]
        rc2 = [fw.sb("rc%d" % i, [128, 512], F32) for i in range(2)]
        rs2 = [fw.sb("rs%d" % i, [128, 512], F32) for i in range(2)]
        sqh = [fw.sb("sqh%d" % i, [128, 512], BF16) for i in range(2)]
        qg = [fw.sb("qg%d" % i, [128, 512], BF16) for i in range(2)]
        sdh = [fw.sb("sdh%d" % i, [128, 512], F32) for i in range(2)]
        rsh = [fw.sb("rsh%d" % i, [128, 512], F32) for i in range(2)]
        t1 = [fw.sb("t1_%d" % i, [128, 512], F32) for i in range(2)]
        t2 = [fw.sb("t2_%d" % i, [128, 512], F32) for i in range(2)]
        qo = [fw.sb("qo%d" % i, [128, 512], BF16) for i in range(3)]
        mc = self.modcol[l]
        A1 = self.A1[l]
        hn = 0
        nchunk = 0
        for s in [cx, lat]:
            want_q = (s is lat) or (not last)
            TC = s.TC
            for ch in range(s.T // TC):
                t0 = ch * TC
                xs, hT = xs2[nchunk % 2], hT2[nchunk % 2]
                rc, rs = rc2[nchunk % 2], rs2[nchunk % 2]
                nchunk += 1
                self.ld(xs[:, :, :TC], s.XT[:, t0:t0 + TC].rearrange("(c p) t -> p c t", p=128))
                if s.rope:
                    self.ld(rc[:, :TC], I["k_ropec"][:, t0:t0 + TC])
                    self.ld(rs[:, :TC], I["k_ropes"][:, t0:t0 + TC])
                self.rms_mod(xs, TC, lambda c, j: A1[:, c, j:j + 1], lambda c, j: mc[:, c, j:j + 1], s.j, hT, RT)
                heads = ([("q", j) for j in range(NH)] if want_q else []) + [("k", 0), ("k", 1)]
                for (kind, j) in heads:
                    col0 = j * 128 if kind == "q" else D + j * 128
                    gcol = self.gq if kind == "q" else self.gk
                    i2 = hn % 2
                    hn += 1
                    p = self.nps(0, 4)
                    for kc in range(KC):
                        self.mm(p[:, :TC], wq[:, kc, col0:col0 + 128], hT[:, kc, :TC], kc == 0, kc == KC - 1)
                    self.act(sqh[i2][:, :TC], p[:, :TC], AF.Square)
                    self.act(qg[i2][:, :TC], p[:, :TC], AF.Identity, scale=gcol[:])
                    pa = self.ps[4 + i2]
                    self.mm(pa[:, :TC], self.onesb[:], sqh[i2][:, :TC], True, True)
                    self.act(sdh[i2][:, :TC], pa[:, :TC], AF.Ln, bias=self.epsc[:], scale=1.0 / HD)
                    self.act(rsh[i2][:, :TC], sdh[i2][:, :TC], AF.Exp, scale=-0.5)
                    if kind == "q":
                        qoi = qo[hn % 3]
                        dest = qoi[:, :TC]
                    else:
                        dest = KT[:, j, s.key_off + t0: s.key_off + t0 + TC]
                    if s.rope:
                        pb = self.ps[6]
                        self.mm(pb[:, :TC], self.rmat[:], qg[i2][:, :TC], True, True)
                        self.tt("dve", t1[i2][:, :TC], qg[i2][:, :TC], rc[:, :TC], ALU.mult)
                        self.tt("dve", t2[i2][:, :TC], pb[:, :TC], rs[:, :TC], ALU.mult)
                        self.tt("pool", t1[i2][:, :TC], t1[i2][:, :TC], t2[i2][:, :TC], ALU.add)
                        self.tt("dve", dest, t1[i2][:, :TC], rsh[i2][:, :TC], ALU.mult)
                    else:
                        self.tt("dve", dest, qg[i2][:, :TC], rsh[i2][:, :TC], ALU.mult)
                    if kind == "q":
                        self.stq(s.Qs[j, :, t0:t0 + TC], qoi[:, :TC])
                for tsub in range(TC // 128):
                    p = self.nps(0, 4)
                    for kc in range(KC):
                        self.mm(p[:, 0:256], hT[:, kc, tsub * 128:(tsub + 1) * 128], wq[:, kc, 1280:1536], kc == 0, kc == KC - 1)
                    self.cp("act", V[:, (s.key_off + t0) // 128 + tsub, :], p[:, 0:256])
        fw.end_phase()
        fw.begin_phase()
        qt3 = [fw.sb("qt%d" % i, [128, 512], BF16) for i in range(2)]
        pt3 = [fw.sb("pt%d" % i, [128, 2, 512], BF16) for i in range(4)]
        rl2 = [fw.sb("rl%d" % i, [128, 512], F32) for i in range(2)]
        ao2 = [fw.sb("ao%d" % i, [128, 512], BF16) for i in range(2)]
        SCALE = HD ** -0.5
        GRP = 8
        osb2 = [fw.sb("osb%d" % i, [128, 512], F32) for i in range(2)]
        acc2s = [fw.sb("acc2_%d" % i, [128, 2, 512], BF16) for i in range(2)]
        accfs = [fw.sb("accf_%d" % i, [128, 512], BF16) for i in range(2)]
        SHIFT = -8.0
        pairs = []
        for s in ([lat] if last else [lat, cx]):
            for h in range(NH):
                for qc in range(s.T // s.TC):
                    for j in range(s.n_keys // 256):
                        pairs.append((s, h, qc, j))
        st = {"nS": 0}
        pend = []
        SLOTS = [0, 1, 3]

        def next_slot():
            wb = SLOTS[st["nS"] % 3]
            st["nS"] += 1
            return wb

        def bg_banks():
            wb = st.get("free_wb", 0)
            return self.ps[wb * 2], self.ps[wb * 2 + 1]

        def emit_S(pi):
            s, h, qc, j = pairs[pi]
            TC = s.TC
            kvh = h // (NH // NKV)
            if j == 0:
                nq = st.get("nq", 0)
                st["nq"] = nq + 1
                qt = qt3[nq % 2]
                self.ld(qt[:, :TC], s.Qs[h, :, qc * TC:(qc + 1) * TC])
                st[("qt", s.name, h, qc)] = (qt, nq)
            qt, nq = st[("qt", s.name, h, qc)]
            wb = next_slot()
            st[("wb", pi)] = wb
            for a in range(2):
                kt = 2 * j + a
                p = self.ps[wb * 2 + a]
                self.mm(p[:, :TC], KT[:, kvh, kt * 128:(kt + 1) * 128], qt[:, :TC], True, True)

        def emit_rest(pi):
            s, h, qc, j = pairs[pi]
            TC = s.TC
            kvh = h // (NH // NKV)
            n_kt = s.n_keys // 128
            qt, nq = st[("qt", s.name, h, qc)]
            po = self.ps[4]
            pl = self.ps[5]
            wb = st[("wb", pi)]
            pw = self.psw[wb]
            pin = View(pw.h[:].rearrange("p (a n) -> p a n", a=2)[:, :, :TC], (self.ps[wb * 2].res, self.ps[wb * 2 + 1].res))
            pt = pt3[pi % 4]
            self.act(pt[:, :, :TC], pin, AF.Exp, bias=SHIFT, scale=SCALE)
            due = list(pend)
            del pend[:]
            for a in range(2):
                kt = 2 * j + a
                self.mm(po[:, :TC], V[:, kt, kvh * HD:(kvh + 1) * HD], pt[:, a, :TC], kt == 0, kt == n_kt - 1)
            for f_ in due:
                f_()
            npairs = n_kt // 2
            g0 = (j // GRP) * GRP
            gsz = min(GRP, npairs - g0)
            jj = j - g0
            ai = (st.get("na", 0)) % 2
            acc2, accf = acc2s[ai], accfs[ai]
            if gsz == 1:
                self.tt("dve", accf[:, :TC], pt[:, 0, :TC], pt[:, 1, :TC], ALU.add)
            elif jj == 0:
                st["ptprev"] = pt
            elif jj == 1:
                self.tt("dve", acc2[:, :, :TC], st["ptprev"][:, :, :TC], pt[:, :, :TC], ALU.add)
            else:
                self.tt("dve", acc2[:, :, :TC], acc2[:, :, :TC], pt[:, :, :TC], ALU.add)
            if jj == gsz - 1:
                if gsz > 1:
                    self.tt("dve", accf[:, :TC], acc2[:, 0, :TC], acc2[:, 1, :TC], ALU.add)
                first, lastg = (g0 == 0), (g0 + gsz == npairs)

                def emit_L(accf=accf, TC=TC, first=first, lastg=lastg):
                    self.mm(pl[:, :TC], self.onesb[:], accf[:, :TC], first, lastg)
                if lastg:
                    for f_ in pend:
                        f_()
                    del pend[:]
                    emit_L()
                else:
                    pend.append(emit_L)
                st["na"] = st.get("na", 0) + 1
            if j == n_kt // 2 - 1:
                rl, ao = rl2[nq % 2], ao2[nq % 2]
                osb = osb2[nq % 2]
                self.cp("act", osb[:, :TC], po[:, :TC])
                self.act(rl[:, :TC], pl[:, :TC], AF.Ln)
                self.act(rl[:, :TC], rl[:, :TC], AF.Exp, scale=-1.0)
                self.tt("dve", ao[:, :TC], osb[:, :TC], rl[:, :TC], ALU.mult)
                self.stq(s.AT[h * HD:(h + 1) * HD, qc * TC:(qc + 1) * TC], ao[:, :TC])

        bg = None
        if l == 0:
            items = []
            for ll in range(self.DEPTH):
                for (src, dst, K_, N_) in [("w_in", "wb_in", D, NPROJ), ("w_o_attn", "wb_oa", D, D), ("w_o_hyena", "wb_oh", HW, D),
                                           ("w_out", "wb_out", D, D), ("w_gate_up", "wb_gu", D, 2 * DFF), ("w_down", "wb_dn", DFF, D)]:
                    if not (src == "w_in" and ll == 0):
                        items.append((src, dst, K_, N_, ll))

            def chain():
                for x_ in self.spectrum_gen([jb for (_, _, jb) in self.jobs], bg_banks):
                    yield
                for x_ in self.cast_gen(items):
                    yield
            bg = chain()
        import os
        if bg is not None and os.environ.get("BG_FIRST"):
            for x_ in bg:
                pass
            bg = None
        nextS = 0
        for pi in range(len(pairs)):
            look = 2
            while nextS <= pi + look and nextS < len(pairs):
                emit_S(nextS)
                nextS += 1
            emit_rest(pi)
            st["free_wb"] = st[("wb", pi)]
            if bg is not None and pi % 3 == 2:
                if next(bg, "done") == "done":
                    bg = None
        if bg is not None:
            for x_ in bg:
                pass
        fw.end_phase()
        fw.end_phase()

    def ug_phase(self, l, last):
        fw, S = self.fw, self.S
        fw.begin_phase()
        NW = NPROJ - 1536
        w = fw.sb("wug", [128, KC, NW], BF16)
        for kc in range(KC):
            self.ld(w[:, kc, :], S["wb_in"][l, kc * 128:(kc + 1) * 128, 1536:NPROJ])
        RT = self.rms_tiles(512)
        xs2 = [fw.sb("xs%d" % i, [128, KC, 512], F32) for i in range(2)]
        hT2 = [fw.sb("hT%d" % i, [128, KC, 512], BF16) for i in range(2)]
        us2 = [fw.sb("us%d" % i, [128, 4, 512], F32) for i in range(2)]
        gs2 = [fw.sb("gs%d" % i, [128, 4, 512], BF16) for i in range(2)]
        mc, A1 = self.modcol[l], self.A1[l]
        n = 0
        nu = 0
        ng = 0
        HTCH = 1024
        hub = [fw.sb("hub%d" % i, [128, HTCH + 2], F32) for i in range(4)]
        hcx1 = [fw.sb("hcx1_%d" % i, [128, HTCH], F32) for i in range(2)]
        hcv = [fw.sb("hcv_%d" % i, [128, HTCH], F32) for i in range(2)]
        hvb = [fw.sb("hvb_%d" % i, [128, HTCH], BF16) for i in range(2)]
        hn = [0]

        def hyena_a_piece(s, tch, TCH, toks):
            t0 = tch * TCH
            for cc in range(4):
                k = hn[0]
                hn[0] += 1
                i2 = k % 2
                self.conv3(s, 4, cc, t0, TCH, hub[(2 * k) % 4], hcx1[i2], ldeng="pool", toks=toks)
                self.conv3(s, 8, cc, t0, TCH, hub[(2 * k + 1) % 4], hcv[i2], ldeng="pool", toks=toks)
                self.tt("dve", hcv[i2][:, :TCH], hcv[i2][:, :TCH], hcx1[i2][:, :TCH], ALU.mult)
                self.cp("act", hvb[i2][:, :TCH], hcv[i2][:, :TCH])
                self.stq(s.VV[cc * 128:(cc + 1) * 128, t0:t0 + TCH], hcv[i2][:, :TCH])
                self.stq(s.VB[cc * 128:(cc + 1) * 128, t0:t0 + TCH], hvb[i2][:, :TCH])

        for s in ([self.lat] if last else [self.lat, self.cx]):
            TC = s.TC
            TCH = min(HTCH, s.T)
            utoks = []
            done_tch = 0
            for ch in range(s.T // TC):
                t0 = ch * TC
                xs, hT = xs2[n % 2], hT2[n % 2]
                n += 1
                self.ld(xs[:, :, :TC], s.XT[:, t0:t0 + TC].rearrange("(c p) t -> p c t", p=128))
                self.rms_mod(xs, TC, lambda c, j: A1[:, c, j:j + 1], lambda c, j: mc[:, c, j:j + 1], s.j, hT, RT)
                for o4 in range(3):
                    us = us2[nu % 2]
                    nu += 1
                    for oi in range(4):
                        oc = o4 * 4 + oi
                        p = self.nps(0, 6)
                        for kc in range(KC):
                            self.mm(p[:, :TC], w[:, kc, oc * 128:(oc + 1) * 128], hT[:, kc, :TC], kc == 0, kc == KC - 1)
                        self.cp("act" if oi % 2 == 0 else "dve", us[:, oi, :TC], p[:, :TC])
                    utoks.append(self.stq(s.U[o4 * 512:(o4 + 1) * 512, t0:t0 + TC].rearrange("(c p) t -> p c t", p=128), us[:, :, :TC]))
                for o4 in range(4):
                    gs = gs2[ng % 2]
                    ng += 1
                    for oi in range(4):
                        oc = 12 + o4 * 4 + oi
                        p = self.nps(0, 6)
                        for kc in range(KC):
                            self.mm(p[:, :TC], w[:, kc, oc * 128:(oc + 1) * 128], hT[:, kc, :TC], kc == 0, kc == KC - 1)
                        self.act(gs[:, oi, :TC], p[:, :TC], AF.Sigmoid)
                    self.stq(s.G[o4 * 512:(o4 + 1) * 512, t0:t0 + TC].rearrange("(c p) t -> p c t", p=128), gs[:, :, :TC])
                while done_tch < s.T // TCH and min(s.T - 1, (done_tch + 1) * TCH) // TC <= ch:
                    hyena_a_piece(s, done_tch, TCH, list(utoks))
                    done_tch += 1
        fw.end_phase()

    def conv3(self, s, base_chunk, cc, t0, TCH, ub, out, ldeng="sp", toks=None):
        T = s.T
        row0 = (base_chunk + cc) * 128
        lo, hi = t0 - 1, t0 + TCH + 1
        a = 0
        if lo < 0:
            self.memset("pool", ub[:, 0:1], 0.0)
            lo, a = 0, 1
        b = TCH + 2
        if hi > T:
            self.memset("pool", ub[:, TCH + 1:TCH + 2], 0.0)
            hi, b = T, TCH + 1
        if toks:
            self.fw._wait(self.fw.E[ldeng], toks)
        self.dma(ldeng, ub[:, a:b], DV(s.U[row0:row0 + 128, lo:hi]))
        k = base_chunk + cc
        cw, cb = self.cw, self.cb
        self.act(out[:, :TCH], ub[:, 1:TCH + 1], AF.Identity, bias=cb[:, k:k + 1], scale=cw[:, 1, k:k + 1])
        self.stt("dve", out[:, :TCH], ub[:, 0:TCH], cw[:, 0, k:k + 1], out[:, :TCH], ALU.mult, ALU.add)
        self.stt("dve", out[:, :TCH], ub[:, 2:TCH + 2], cw[:, 2, k:k + 1], out[:, :TCH], ALU.mult, ALU.add)

    def hyena_a(self, s, l):
        fw = self.fw
        fw.begin_phase()
        TCH = min(2048, s.T)
        ub2 = [fw.sb("ub%d" % i, [128, TCH + 2], F32) for i in range(4)]
        cx1 = [fw.sb("cx1_%d" % i, [128, TCH], F32) for i in range(2)]
        cv = [fw.sb("cv_%d" % i, [128, TCH], F32) for i in range(2)]
        vb = [fw.sb("vb_%d" % i, [128, TCH], BF16) for i in range(2)]
        n = 0
        for cc in range(4):
            for tch in range(s.T // TCH):
                t0 = tch * TCH
                i2 = n % 2
                self.conv3(s, 4, cc, t0, TCH, ub2[(2 * n) % 4], cx1[i2])
                self.conv3(s, 8, cc, t0, TCH, ub2[(2 * n + 1) % 4], cv[i2])
                n += 1
                self.tt("dve", cv[i2][:], cv[i2][:], cx1[i2][:], ALU.mult)
                self.cp("act", vb[i2][:], cv[i2][:])
                self.stq(s.VV[cc * 128:(cc + 1) * 128, t0:t0 + TCH], cv[i2][:])
                self.stq(s.VB[cc * 128:(cc + 1) * 128, t0:t0 + TCH], vb[i2][:])
        fw.end_phase()

    def hyena_c(self, s, l):
        fw = self.fw
        fw.begin_phase()
        TCH = min(2048, s.T)
        ub2 = [fw.sb("ub%d" % i, [128, TCH + 2], F32) for i in range(2)]
        cx0 = [fw.sb("cx0_%d" % i, [128, TCH], F32) for i in range(2)]
        yr = [fw.sb("yr_%d" % i, [128, TCH], F32) for i in range(2)]
        vv = [fw.sb("vv_%d" % i, [128, TCH], F32) for i in range(2)]
        hy = [fw.sb("hy_%d" % i, [128, TCH], BF16) for i in range(2)]
        n = 0
        for cc in range(4):
            for tch in range(s.T // TCH):
                t0 = tch * TCH
                i2 = n % 2
                n += 1
                self.conv3(s, 0, cc, t0, TCH, ub2[i2], cx0[i2])
                self.ld(yr[i2][:], s.YB[cc * 128:(cc + 1) * 128, t0:t0 + TCH])
                self.ld(vv[i2][:], s.VV[cc * 128:(cc + 1) * 128, t0:t0 + TCH])
                self.stt("dve", yr[i2][:], vv[i2][:], self.hb[:, cc:cc + 1], yr[i2][:], ALU.mult, ALU.add)
                self.tt("pool", hy[i2][:], yr[i2][:], cx0[i2][:], ALU.mult)
                self.stq(s.HY[cc * 128:(cc + 1) * 128, t0:t0 + TCH], hy[i2][:])
        fw.end_phase()

    def cmul(self, out, pin, tre, tim, conj, tmp1, tmp2, e_re="dve"):
        self.tt("dve", tmp1[:], pin, tre, ALU.mult)
        self.tt("dve", tmp2[:], pin, tim, ALU.mult)
        if not conj:
            self.tt(e_re, out[:, :, 0, :], tmp1[:, :, 0, :], tmp2[:, :, 1, :], ALU.subtract)
            self.tt("pool", out[:, :, 1, :], tmp2[:, :, 0, :], tmp1[:, :, 1, :], ALU.add)
        else:
            self.tt(e_re, out[:, :, 0, :], tmp1[:, :, 0, :], tmp2[:, :, 1, :], ALU.add)
            self.tt("pool", out[:, :, 1, :], tmp1[:, :, 1, :], tmp2[:, :, 0, :], ALU.subtract)

    def p4(self, p):
        return p.v(p.h[:].rearrange("p (c r k) -> p c r k", c=2, r=2))

    def tw_b(self, idx):
        return self.tw.v(self.tw.h[:, idx, :].unsqueeze(1).unsqueeze(1).broadcast_to([128, 2, 2, 128]))

    def fft_s1(self, v2, nK, c0, Bt, tmp1, tmp2, pA):
        ft = self.ft
        for c in range(2):
            self.mm(pA[:, c * 256:(c + 1) * 256], v2[0:nK, c0 + c, :],
                    ft.v(ft.h[0:nK, 0:2, :]), c == 0, c == 1, skip=True)
        self.cmul(Bt, self.p4(pA), self.tw_b(0), self.tw_b(1), False, tmp1, tmp2)

    def fft_s2(self, Bt, pX):
        ft = self.ft
        self.mm(pX[:], ft[:, 0, :], Bt[:], True, False, skip=True)
        pX4 = self.p4(pX)
        self.mm(View(pX4.ap[:, :, 0, :], pX.res), ft[:, 3, :], Bt[:, :, 1, :], False, False, skip=True)
        self.mm(View(pX4.ap[:, :, 1, :], pX.res), ft[:, 1, :], Bt[:, :, 0, :], False, True, skip=True)

    def fft_conv(self, s):
        fw = self.fw
        fw.begin_phase()
        T = s.T
        nK = T // 128
        ft = self.ft
        v2s = [fw.sb("v2_%d" % i, [max(nK, 2), 128, 128], BF16) for i in range(2)]
        y2 = fw.sb("y2", [max(nK, 2), 128, 128], F32)
        Bt = [fw.sb("Bt%d" % i, [128, 2, 2, 128], BF16) for i in range(2)]
        Yt = [fw.sb("Yt%d" % i, [128, 2, 2, 128], BF16) for i in range(2)]
        Ut = [fw.sb("Ut%d" % i, [128, 2, 2, 128], BF16) for i in range(2)]
        kf = [fw.sb("kf%d" % i, [128, 2, 2, 128], F32) for i in range(3)]
        tA = [[fw.sb("cmA%d_%d" % (k, i), [128, 2, 2, 128], F32) for i in range(2)] for k in range(3)]
        tB = [[fw.sb("cmB%d_%d" % (k, i), [128, 2, 2, 128], F32) for i in range(2)] for k in range(3)]
        NG = 64
        for cc in range(4):
            v2 = v2s[cc % 2]
            for q4 in range(4):
                self.ld(v2[0:nK, q4 * 32:(q4 + 1) * 32, :],
                        s.VB[cc * 128 + q4 * 32: cc * 128 + (q4 + 1) * 32, :].rearrange("ch (n1 n2) -> n1 ch n2", n2=128))
            for it in range(NG + 3):
                g = it
                if 0 <= g < NG:
                    i2 = g % 2
                    self.ld(kf[g % 3][:], s.KF[:, cc * 128 + g * 2: cc * 128 + g * 2 + 2, :, :])
                    self.fft_s1(v2, nK, g * 2, Bt[i2], tA[0][i2], tB[0][i2], self.ps[i2])
                g = it - 1
                if 0 <= g < NG:
                    i2 = g % 2
                    kfi = kf[g % 3]
                    pX = self.ps[2 + i2]
                    self.fft_s2(Bt[i2], pX)
                    kre = kfi.v(kfi.h[:, :, 0:1, :].broadcast_to([128, 2, 2, 128]))
                    kim = kfi.v(kfi.h[:, :, 1:2, :].broadcast_to([128, 2, 2, 128]))
                    self.cmul(Yt[i2], self.p4(pX), kre, kim, False, tA[1][i2], tB[1][i2], e_re="pool")
                g = it - 2
                if 0 <= g < NG:
                    i2 = g % 2
                    pU = self.ps[4 + i2]
                    for c in range(2):
                        self.mm(pU[:, c * 256:(c + 1) * 256], Yt[i2][:, c, 0, :], ft.v(ft.h[:, 2:4, :]), c == 0, False, skip=True)
                        self.mm(pU[:, c * 256:(c + 1) * 256], Yt[i2][:, c, 1, :], ft.v(ft.h[:, 1:3, :]), False, c == 1, skip=True)
                    self.cmul(Ut[i2], self.p4(pU), self.tw_b(0), self.tw_b(1), True, tA[2][i2], tB[2][i2], e_re="pool")
                g = it - 3
                if 0 <= g < NG:
                    i2 = g % 2
                    pY = self.ps[6 + i2]
                    pYv = View(pY.h[0:nK, 0:256].rearrange("p (c k) -> p c k", c=2), pY.res)
                    self.mm(pYv, ft[:, 0, 0:nK], Ut[i2][:, :, 0, :], True, False, skip=True)
                    self.mm(pYv, ft[:, 1, 0:nK], Ut[i2][:, :, 1, :], False, True, skip=True)
                    self.cp("act", y2[0:nK, g * 2:g * 2 + 2, :], pYv)
            for q4 in range(4):
                self.stq(s.YB[cc * 128 + q4 * 32: cc * 128 + (q4 + 1) * 32, :].rearrange("ch (m1 m2) -> m1 ch m2", m2=128),
                         y2[0:nK, q4 * 32:(q4 + 1) * 32, :])
        fw.end_phase()

    def mlp_gen(self, s, l, KB, pbank):
        fw, I = self.fw, self.I
        L = s.T
        TC = min(512, L)
        w1 = fw.sb("fw1", [FEMB, FHID], F32)
        w2 = fw.sb("fw2", [FHID, FHID], F32)
        w3 = fw.sb("fw3", [FHID, FHID], F32)
        w4 = fw.sb("fw4", [FHID, 2 * HW], F32)
        self.ld(w1[:], I["filt_w1"][l])
        self.ld(w2[:], I["filt_w2"][l])
        self.ld(w3[:], I["filt_w3"][l])
        self.ld(w4[:], I["filt_w4"][l])
        fq = fw.sb("fq", [FHID, 3], F32)
        fb = fw.sb("fb", [FHID, 3], F32)
        self.ld(fq[:], I["filt_freq"][l].rearrange("i p -> p i"), allow_slow_non_contiguous=True)
        for i, nm in enumerate(["filt_b1", "filt_b2", "filt_b3"]):
            self.ld(fb[:, i:i + 1], I[nm][l].rearrange("(p o) -> p o", o=1), allow_slow_non_contiguous=True)
        fsc = fw.sb("fsc", [FHID, 3], F32)
        fbc = fw.sb("fbc", [FHID, 3], F32)
        self.ts("dve", fsc[:], fq[:], 1.0 / 3.0, 0.0, ALU.mult, ALU.add)
        self.tt("dve", fbc[:], fsc[:], fb[:], ALU.mult)
        zt = self.zt
        z0, z1 = L, NFFT - L + 1
        for cc in range(4):
            p0 = z0
            while p0 < z1:
                n = min(2048, z1 - p0)
                self.stq(KB[cc * 128:(cc + 1) * 128, p0:p0 + n], zt[:, :n], allow_slow_non_contiguous=True)
                p0 += n
            yield
        zin = [fw.sb("zin%d" % i, [FEMB, TC], F32) for i in range(2)]
        hs = fw.sb("hs", [FHID, TC], F32)
        s2 = fw.sb("s2", [FHID, TC], F32)
        hh = [fw.sb("hh%d" % k, [FHID, TC], F32) for k in range(3)]
        dct = [fw.sb("dct%d" % i, [128, TC], F32) for i in range(2)]
        kr = [fw.sb("kr%d" % i, [128, TC], BF16) for i in range(2)]
        n = 0
        nd = 0
        p = pbank
        for d_ in range(2):
            for ch in range(L // TC):
                t0 = ch * TC
                i2 = n % 2
                n += 1
                self.ld(zin[i2][:], s.z[d_, :, t0:t0 + TC])
                cur = zin[i2]
                curK = FEMB
                for k, wk in enumerate([w1, w2, w3]):
                    self.mm(p[0:FHID, :TC], wk[0:curK, :], cur[0:curK, :], True, True)
                    self.act(hs[:], p[0:FHID, :TC], AF.Sin, bias=fbc[:, k:k + 1], scale=fsc[:, k:k + 1])
                    self.tt("dve", s2[:], hs[:], hs[:], ALU.mult)
                    self.ts("dve", s2[:], s2[:], -4.0, 3.0, ALU.mult, ALU.add)
                    self.tt("dve", hh[k][:], s2[:], hs[:], ALU.mult)
                    cur = hh[k]
                    curK = FHID
                    yield
                for oc in range(4):
                    self.mm(p[:, :TC], w4[:, d_ * HW + oc * 128: d_ * HW + (oc + 1) * 128], cur[:], True, True)
                    dc, krr = dct[nd % 2], kr[nd % 2]
                    nd += 1
                    self.ld(dc[:], s.dec[d_, oc * 128:(oc + 1) * 128, t0:t0 + TC])
                    self.tt("dve", krr[:], p[:, :TC], dc[:], ALU.mult)
                    if d_ == 0:
                        self.stq(KB[oc * 128:(oc + 1) * 128, t0:t0 + TC], krr[:])
                    else:
                        pos = NFFT - L + 1 + t0
                        nn = TC if t0 + TC < L else TC - 1
                        self.stq(KB[oc * 128:(oc + 1) * 128, pos:pos + nn], krr[:, :nn])
                    yield

    def spectrum_gen(self, jobs, get_banks):
        fw, S = self.fw, self.S
        k2s = [fw.sb("k2_%d" % i, [128, 32, 128], BF16) for i in range(2)]
        Bt = [fw.sb("Bt%d" % i, [128, 2, 2, 128], BF16) for i in range(2)]
        tmp1 = [fw.sb("cm1_%d" % i, [128, 2, 2, 128], F32) for i in range(2)]
        tmp2 = [fw.sb("cm2_%d" % i, [128, 2, 2, 128], F32) for i in range(2)]
        xo = [fw.sb("xo%d" % i, [128, 2, 2, 128], F32) for i in range(3)]
        nq = 0
        for jb in jobs:
            KB, KF = S["KB%d" % jb], S["KF%d" % jb]
            for q in range(16):
                k2 = k2s[nq % 2]
                nq += 1
                self.ld(k2[:], KB[q * 32:(q + 1) * 32, :].rearrange("ch (n1 n2) -> n1 ch n2", n2=128))
                for it in range(17):
                    pA, pX = get_banks()
                    g = it
                    if 0 <= g < 16:
                        i2 = g % 2
                        self.fft_s1(k2, 128, g * 2, Bt[i2], tmp1[i2], tmp2[i2], pA)
                    g = it - 1
                    if 0 <= g < 16:
                        i2 = g % 2
                        xoi = xo[g % 3]
                        self.fft_s2(Bt[i2], pX)
                        self.act(xoi[:], self.p4(pX), AF.Identity, scale=1.0 / NFFT)
                        self.stq(KF[:, q * 32 + g * 2: q * 32 + g * 2 + 2, :, :], xoi[:])
                    yield

    def merge_phase(self, l, streams):
        fw, S = self.fw, self.S
        fw.begin_phase()
        woa = fw.sb("woa", [128, KC, D], BF16)
        woh = fw.sb("woh", [128, 4, D], BF16)
        wout = fw.sb("wout", [128, KC, D], BF16)
        self.load_w(woa, S["wb_oa"][l], KC)
        self.load_w(woh, S["wb_oh"][l], 4)
        self.load_w(wout, S["wb_out"][l], KC)
        xs2 = [fw.sb("xs%d" % i, [128, KC, 512], F32) for i in range(2)]
        at2 = [fw.sb("at%d" % i, [128, KC, 512], BF16) for i in range(2)]
        hy2 = [fw.sb("hy%d" % i, [128, 4, 512], BF16) for i in range(2)]
        g2 = [fw.sb("g%d" % i, [128, 16, 512], BF16) for i in range(2)]
        mg = fw.sb("mg", [128, KC, 512], BF16)
        m1 = [fw.sb("m1_%d" % i, [128, 512], F32) for i in range(2)]
        m2 = [fw.sb("m2_%d" % i, [128, 512], F32) for i in range(2)]
        mc = self.modcol[l]
        n = 0
        no = 0
        for s in streams:
            TC = s.TC
            for ch in range(s.T // TC):
                t0 = ch * TC
                xs, at, hy, g = xs2[n % 2], at2[n % 2], hy2[n % 2], g2[n % 2]
                n += 1
                self.ld(xs[:, :, :TC], s.XT[:, t0:t0 + TC].rearrange("(c p) t -> p c t", p=128))
                self.ld(at[:, :, :TC], s.AT[:, t0:t0 + TC].rearrange("(c p) t -> p c t", p=128))
                self.ld(hy[:, :, :TC], s.HY[:, t0:t0 + TC].rearrange("(c p) t -> p c t", p=128))
                self.ld(g[:, :, :TC], s.G[:, t0:t0 + TC].rearrange("(c p) t -> p c t", p=128))
                for oc in range(KC):
                    i2 = no % 2
                    no += 1
                    pa, pb = self.ps[i2], self.ps[2 + i2]
                    for kc in range(KC):
                        self.mm(pa[:, :TC], woa[:, kc, oc * 128:(oc + 1) * 128], at[:, kc, :TC], kc == 0, kc == KC - 1)
                    for kc in range(4):
                        self.mm(pb[:, :TC], woh[:, kc, oc * 128:(oc + 1) * 128], hy[:, kc, :TC], kc == 0, kc == 3)
                    self.tt("dve", m1[i2][:, :TC], pa[:, :TC], g[:, oc, :TC], ALU.mult)
                    self.tt("dve", m2[i2][:, :TC], pb[:, :TC], g[:, 8 + oc, :TC], ALU.mult)
                    self.tt("pool", mg[:, oc, :TC], m1[i2][:, :TC], m2[i2][:, :TC], ALU.add)
                for oc in range(KC):
                    p = self.ps[4 + oc % 4]
                    for kc in range(KC):
                        self.mm(p[:, :TC], wout[:, kc, oc * 128:(oc + 1) * 128], mg[:, kc, :TC], kc == 0, kc == KC - 1)
                    self.stt("dve", xs[:, oc, :TC], p[:, :TC], mc[:, 16 + oc, s.j:s.j + 1], xs[:, oc, :TC], ALU.mult, ALU.add)
                self.stq(s.XT[:, t0:t0 + TC].rearrange("(c p) t -> p c t", p=128), xs[:, :, :TC])
        fw.end_phase()

    def ffn_phase(self, l, streams):
        fw, S = self.fw, self.S
        fw.begin_phase()
        TC = 256
        wgu = fw.sb("wgu", [128, KC, 2 * DFF], BF16)
        wdn = fw.sb("wdn", [128, FC, D], BF16)
        self.load_w(wgu, S["wb_gu"][l], KC)
        self.load_w(wdn, S["wb_dn"][l], FC)
        RT = self.rms_tiles(TC)
        xs2 = [fw.sb("xs%d" % i, [128, KC, TC], F32) for i in range(2)]
        h2 = fw.sb("h2", [128, KC, TC], BF16)
        sg = [fw.sb("sg%d" % i, [128, TC], F32) for i in range(2)]
        sT = fw.sb("sT", [128, FC, TC], BF16)
        mc, A2 = self.modcol[l], self.A2[l]
        n = 0
        nj = 0
        for s in streams:
            for ch in range(s.T // TC):
                t0 = ch * TC
                xs = xs2[n % 2]
                n += 1
                self.ld(xs[:], s.XT[:, t0:t0 + TC].rearrange("(c p) t -> p c t", p=128))
                self.rms_mod(xs, TC, lambda c, j: A2[:, c, j:j + 1], lambda c, j: mc[:, 24 + c, j:j + 1], s.j, h2, RT)
                for j2 in range(FC):
                    i2 = nj % 2
                    nj += 1
                    pg, pu = self.ps[i2], self.ps[2 + i2]
                    for kc in range(KC):
                        self.mm(pg[:, :TC], wgu[:, kc, j2 * 128:(j2 + 1) * 128], h2[:, kc, :], kc == 0, kc == KC - 1)
                    for kc in range(KC):
                        self.mm(pu[:, :TC], wgu[:, kc, DFF + j2 * 128: DFF + (j2 + 1) * 128], h2[:, kc, :], kc == 0, kc == KC - 1)
                    self.act(sg[i2][:], pg[:, :TC], AF.Silu)
                    self.tt("dve", sT[:, j2, :], sg[i2][:], pu[:, :TC], ALU.mult)
                for oc in range(KC):
                    p = self.ps[4 + oc % 3]
                    for j2 in range(FC):
                        self.mm(p[:, :TC], wdn[:, j2, oc * 128:(oc + 1) * 128], sT[:, j2, :], j2 == 0, j2 == FC - 1)
                    self.stt("dve", xs[:, oc, :], p[:, :TC], mc[:, 40 + oc, s.j:s.j + 1], xs[:, oc, :], ALU.mult, ALU.add)
                self.stq(s.XT[:, t0:t0 + TC].rearrange("(c p) t -> p c t", p=128), xs[:])
        fw.end_phase()

    def final_phase(self):
        fw = self.fw
        s = self.lat
        fw.begin_phase()
        TC = 512
        RT = self.rms_tiles(TC)
        xs2 = [fw.sb("xs%d" % i, [128, KC, TC], F32) for i in range(2)]
        yT = fw.sb("yT", [128, KC, TC], F32)
        yt2 = [fw.sb("ytok%d" % i, [128, 4, D], F32) for i in range(2)]
        nf = self.nfin
        for ch in range(s.T // TC):
            t0 = ch * TC
            xs, yt = xs2[ch % 2], yt2[ch % 2]
            self.ld(xs[:], s.XT[:, t0:t0 + TC].rearrange("(c p) t -> p c t", p=128))
            self.rms_mod(xs, TC, lambda c, j: nf[:, c:c + 1], None, 0, yT, RT)
            for j in range(4):
                for half in range(2):
                    p = self.nps(0, 4)
                    for c4 in range(4):
                        c = half * 4 + c4
                        self.tr(p[:, c4 * 128:(c4 + 1) * 128], yT[:, c, j * 128:(j + 1) * 128], self.ident[:])
                    self.cp("act" if half == 0 else "dve", yt[:, j, half * 512:(half + 1) * 512], p[:])
            self.stq(self.out[t0:t0 + TC, :].rearrange("(j p) f -> p j f", p=128), yt[:])
        fw.end_phase()


def host_consts(SEQ):
    K = {}
    K["k_ident"] = np.eye(128, dtype=np.float32)
    r = np.zeros((128, 128), np.float32)
    for i in range(64):
        r[2 * i + 1, 2 * i] = -1.0
        r[2 * i, 2 * i + 1] = 1.0
    K["k_rmat"] = r
    a = np.arange(128, dtype=np.float64)
    ang = -2.0 * np.pi * np.outer(a, a) / 128.0
    Fr, Fi = np.cos(ang), np.sin(ang)
    K["k_ft"] = np.ascontiguousarray(np.stack([Fr, Fi, Fr, -Fi], axis=1)).astype(np.float32)
    angt = -2.0 * np.pi * np.outer(a, a) / NFFT
    K["k_tw"] = np.ascontiguousarray(np.stack([np.cos(angt), np.sin(angt)], axis=1)).astype(np.float32)
    GRID_W = 64
    rows = SEQ // GRID_W
    row = np.repeat(np.arange(rows, dtype=np.float32), GRID_W)
    col = np.tile(np.arange(GRID_W, dtype=np.float32), rows)
    inv_freq = (np.float32(10000.0) ** (-np.arange(0, 64, 2, dtype=np.float32) / np.float32(64))).astype(np.float32)
    angr = np.concatenate([row[:, None] * inv_freq, col[:, None] * inv_freq], axis=-1).astype(np.float32)
    cs, sn = np.cos(angr), np.sin(angr)
    K["k_ropec"] = np.ascontiguousarray(np.repeat(cs, 2, axis=1).T).astype(np.float32)
    K["k_ropes"] = np.ascontiguousarray(np.repeat(sn, 2, axis=1).T).astype(np.float32)

    def ztab(L):
        t = np.linspace(0.0, 1.0, L, dtype=np.float32)[:, None]
        w = (np.float32(2.0 * math.pi / L) * np.arange(L, dtype=np.float32))[:, None]
        f = np.linspace(1e-4, 15.0, 16, dtype=np.float32)[None, :]
        z = np.concatenate([t, np.cos(f * w), -np.sin(f * w)], axis=-1).astype(np.float32)
        max_decay = math.log(1e-2) / 0.3
        min_decay = math.log(1e-2) / 1.5
        deltas = np.abs(np.linspace(min_decay, max_decay, HW, dtype=np.float32))
        dec = np.exp(-t * deltas).astype(np.float32)
        zz = np.stack([z.T, z[::-1].T], axis=0)
        dd = np.stack([dec.T, dec[::-1].T], axis=0)
        return np.ascontiguousarray(zz).astype(np.float32), np.ascontiguousarray(dd).astype(np.float32)

    K["k_z_lat"], K["k_dec_lat"] = ztab(SEQ)
    K["k_z_ctx"], K["k_dec_ctx"] = ztab(CTX)
    return K


_NC_CACHE = {}


def run_cores(inputs, n_cores, dbg=None):
    x = np.asarray(inputs["x"], dtype=np.float32)
    SEQ = x.shape[1]
    DEPTH = np.asarray(inputs["w_mod"]).shape[0]
    key = (SEQ, DEPTH, tuple(sorted(dbg or [])))
    if key not in _NC_CACHE:
        nc = bass.Bass("TRN2", target_bir_lowering=False)
        b = Builder(nc, SEQ, DEPTH, dbg=dbg)
        b.build()
        _NC_CACHE[key] = nc
    nc = _NC_CACHE[key]
    K = host_consts(SEQ)
    shared = {k: np.ascontiguousarray(np.asarray(v, dtype=np.float32)) for k, v in inputs.items()
              if k not in ("x", "c", "ctx")}
    shared.update(K)
    in_maps = []
    for b_ in range(n_cores):
        m = dict(shared)
        m["x"] = np.ascontiguousarray(x[b_])
        m["c"] = np.ascontiguousarray(np.asarray(inputs["c"], dtype=np.float32)[b_])
        m["ctx"] = np.ascontiguousarray(np.asarray(inputs["ctx"], dtype=np.float32)[b_])
        in_maps.append(m)
    res = run_bass_kernel_spmd(nc, in_maps, core_ids=list(range(n_cores)))
    return res


def kernel(**inputs):
    res = run_cores(inputs, 8)
    out = np.stack([np.asarray(r["out"], dtype=np.float32) for r in res.results], axis=0)
    return out
```

```python
import math
from contextlib import ExitStack
import numpy as np
import concourse.bass as bass
import concourse.mybir as mybir
from concourse.bass_utils import run_bass_kernel_spmd

F32 = mybir.dt.float32
BF16 = mybir.dt.bfloat16
AF = mybir.ActivationFunctionType
ALU = mybir.AluOpType

D = 1024
KC = 8
NH = 8
NKV = 2
HD = 128
HW = 512
NPROJ = 5120
DFF = 2816
FC = 22
NMOD = 6
CTX = 256
NFFT = 16384
EPS = 1e-6
FEMB = 33
FHID = 64


class Res:
    __slots__ = ("name", "w", "r")

    def __init__(self, name=""):
        self.name = name
        self.w = None
        self.r = []


class View:
    __slots__ = ("ap", "res")

    def __init__(self, ap, res):
        self.ap = ap
        self.res = res


class Tl:
    def __init__(self, h, name):
        self.h = h
        self.res = Res(name)

    def __getitem__(self, k):
        return View(self.h[k], self.res)

    def v(self, ap):
        return View(ap, self.res)


def DV(ap):
    return View(ap, None)


class Eng:
    def __init__(self, name):
        self.name = name
        self.q = []
        self.sem = None
        self.cnt = 0
        self.seen = {}
        self.dsems = []
        self.dnext = 0


class FW:
    def __init__(self, nc, ndma=8):
        self.nc = nc
        self.st = ExitStack()
        self.E = {}
        for nm in ["pe", "act", "dve", "pool", "sp"]:
            e = Eng(nm)
            e.sem = self.st.enter_context(nc.semaphore("cs_" + nm))
            self.E[nm] = e
        for nm in ["sp", "pool", "act"]:
            e = self.E[nm]
            for i in range(ndma):
                s = self.st.enter_context(nc.semaphore("ds_%s%d" % (nm, i)))
                e.dsems.append([s, 0])
        self.n_ops = 0
        self.uid = 0
        self.phase_st = None
        self.stack = []

    def _alloc(self, name, shape, dt, psum):
        self.uid += 1
        nm = "%s_%d" % (name, self.uid)
        st = self.phase_st if self.phase_st is not None else self.st
        if psum:
            h = st.enter_context(self.nc.psum_tensor(nm, list(shape), dt))
        else:
            h = st.enter_context(self.nc.sbuf_tensor(nm, list(shape), dt))
        return Tl(h, nm)

    def sb(self, name, shape, dt):
        return self._alloc(name, shape, dt, False)

    def ps(self, name, shape, dt):
        return self._alloc(name, shape, dt, True)

    def _wait(self, e, deps):
        need = {}
        for d in deps:
            if d is None:
                continue
            sem, val, owner = d
            if owner == e.name and e.name == "pe":
                continue
            k = id(sem)
            if e.seen.get(k, 0) >= val:
                continue
            if k not in need or need[k][1] < val:
                need[k] = (sem, val)
        for k, (sem, val) in need.items():
            e.seen[k] = val
            e.q.append(lambda h, sem=sem, val=val: h.wait_ge(sem, val))

    def _deps(self, reads, writes):
        deps = []
        for r in reads:
            deps.append(r.w)
        for w in writes:
            deps.append(w.w)
            deps.extend(w.r)
        return deps

    def _commit(self, tok, reads, writes):
        for r in reads:
            r.r.append(tok)
        for w in writes:
            w.w = tok
            w.r = []

    def op(self, eng, fn, reads=(), writes=()):
        e = self.E[eng]
        self._wait(e, self._deps(reads, writes))
        e.cnt += 1
        sem = e.sem
        e.q.append(lambda h, fn=fn, sem=sem: fn(h).then_inc(sem, 1))
        tok = (sem, e.cnt, e.name)
        self._commit(tok, reads, writes)
        self.n_ops += 1
        return tok

    def dma(self, eng, out, in_, reads=(), writes=(), **kw):
        e = self.E[eng]
        self._wait(e, self._deps(reads, writes))
        slot = e.dsems[e.dnext]
        e.dnext = (e.dnext + 1) % len(e.dsems)
        sem, val = slot
        if val > 0 and e.seen.get(id(sem), 0) < val:
            e.seen[id(sem)] = val
            e.q.append(lambda h, sem=sem, val=val: h.wait_ge(sem, val))
        slot[1] = val + 16
        e.q.append(lambda h, out=out, in_=in_, sem=sem, kw=kw:
                   h.dma_start(out=out, in_=in_, **kw).then_inc(sem, 16))
        tok = (sem, val + 16, "dma_" + e.name)
        self._commit(tok, reads, writes)
        self.n_ops += 1
        return tok

    def barrier(self):
        toks = []
        for e in self.E.values():
            if e.cnt > 0:
                toks.append((e.sem, e.cnt, e.name))
            for sem, val in e.dsems:
                if val > 0:
                    toks.append((sem, val, "dma_" + e.name))
        for e in self.E.values():
            need = {}
            for sem, val, owner in toks:
                if owner == e.name:
                    continue
                k = id(sem)
                if e.seen.get(k, 0) >= val:
                    continue
                need[k] = (sem, val)
            for k, (sem, val) in need.items():
                e.seen[k] = val
                e.q.append(lambda h, sem=sem, val=val: h.wait_ge(sem, val))

    def begin_phase(self):
        self.barrier()
        self.stack.append(self.phase_st)
        self.phase_st = ExitStack()

    def end_phase(self):
        self.barrier()
        self.phase_st.close()
        self.phase_st = self.stack.pop()

    def finish(self):
        self.barrier()
        nc = self.nc
        E = self.E
        with nc.Block() as block:
            @block.tensor
            def _(h):
                for f in E["pe"].q:
                    f(h)

            @block.scalar
            def _(h):
                for f in E["act"].q:
                    f(h)

            @block.vector
            def _(h):
                for f in E["dve"].q:
                    f(h)

            @block.gpsimd
            def _(h):
                for f in E["pool"].q:
                    f(h)

            @block.sync
            def _(h):
                for f in E["sp"].q:
                    f(h)
        self.st.close()


def _flat(vs):
    out = []
    for v in vs:
        if isinstance(v, View) and v.res is not None:
            if isinstance(v.res, (tuple, list)):
                out.extend(v.res)
            else:
                out.append(v.res)
    return out


def _rw(ins, outs):
    return _flat(ins), _flat(outs)


def _a(x):
    return x.ap if isinstance(x, View) else x


class Stream:
    pass


class Builder:
    def __init__(self, nc, SEQ, DEPTH, dbg=None):
        self.nc = nc
        self.SEQ = SEQ
        self.DEPTH = DEPTH
        self.fw = FW(nc)
        self.dbg = dbg or {}
        self.rr = 0

    def mm(self, out, lhsT, rhs, start, stop, skip=False):
        r, w = _rw([lhsT, rhs], [out])
        o, a, b = out.ap, lhsT.ap, rhs.ap
        self.fw.op("pe", lambda h: h.matmul(o, a, b, start=start, stop=stop, skip_group_check=skip), r, w)

    def tr(self, out, in_, ident):
        r, w = _rw([in_, ident], [out])
        o, a, b = out.ap, in_.ap, ident.ap
        self.fw.op("pe", lambda h: h.transpose(o, a, b), r, w)

    def act(self, out, in_, func, bias=None, scale=None):
        r, w = _rw([in_, bias, scale], [out])
        kw = {}
        if bias is not None:
            kw["bias"] = _a(bias)
        if scale is not None:
            kw["scale"] = _a(scale)
        o, a = out.ap, in_.ap
        self.fw.op("act", lambda h: h.activation(o, a, func, **kw), r, w)

    def tt(self, eng, out, in0, in1, op):
        r, w = _rw([in0, in1], [out])
        o, a, b = out.ap, in0.ap, in1.ap
        self.fw.op(eng, lambda h: h.tensor_tensor(o, a, b, op), r, w)

    def stt(self, eng, out, in0, scalar, in1, op0, op1):
        r, w = _rw([in0, scalar, in1], [out])
        o, a, s, b = out.ap, in0.ap, _a(scalar), in1.ap
        self.fw.op(eng, lambda h: h.scalar_tensor_tensor(o, a, s, b, op0, op1), r, w)

    def ts(self, eng, out, in0, s1, s2, op0, op1):
        r, w = _rw([in0, s1, s2], [out])
        o, a, x1, x2 = out.ap, in0.ap, _a(s1), _a(s2)
        self.fw.op(eng, lambda h: h.tensor_scalar(o, a, x1, x2, op0, op1), r, w)

    def cp(self, eng, out, in_):
        r, w = _rw([in_], [out])
        o, a = out.ap, in_.ap
        if eng == "act":
            self.fw.op("act", lambda h: h.activation(o, a, AF.Copy), r, w)
        else:
            self.fw.op(eng, lambda h: h.tensor_copy(o, a), r, w)

    def recip(self, out, in_):
        r, w = _rw([in_], [out])
        o, a = out.ap, in_.ap
        self.fw.op("dve", lambda h: h.reciprocal(o, a), r, w)

    def memset(self, eng, out, val):
        r, w = _rw([], [out])
        o = out.ap
        self.fw.op(eng, lambda h: h.memset(o, val), r, w)

    def dma(self, eng, out, in_, **kw):
        r, w = _rw([in_], [out])
        return self.fw.dma(eng, out.ap, in_.ap, r, w, **kw)

    def ld(self, out, in_ap, **kw):
        self.dma("sp", out, DV(in_ap), **kw)

    def stq(self, out_ap, in_, **kw):
        self.dma("pool", DV(out_ap), in_, **kw)

    def dram_in(self, name, shape, dt=F32):
        return self.nc.dram_tensor(name, list(shape), dt, kind="ExternalInput").ap()

    def dram_scratch(self, name, shape, dt):
        kind = "ExternalOutput" if name in self.dbg else "Internal"
        return self.nc.dram_tensor(name, list(shape), dt, kind=kind).ap()

    def build(self):
        nc, fw, SEQ, DEPTH = self.nc, self.fw, self.SEQ, self.DEPTH
        I = {}
        I["x"] = self.dram_in("x", [SEQ, D])
        I["c"] = self.dram_in("c", [D])
        I["ctx"] = self.dram_in("ctx", [CTX, D])
        I["c_ctx"] = self.dram_in("c_ctx", [D])
        I["w_mod"] = self.dram_in("w_mod", [DEPTH, D, NMOD * D])
        I["b_mod"] = self.dram_in("b_mod", [DEPTH, NMOD * D])
        I["norm_mix"] = self.dram_in("norm_mix", [DEPTH, D])
        I["w_in"] = self.dram_in("w_in", [DEPTH, D, NPROJ])
        I["q_norm"] = self.dram_in("q_norm", [DEPTH, HD])
        I["k_norm"] = self.dram_in("k_norm", [DEPTH, HD])
        I["conv_w"] = self.dram_in("conv_w", [DEPTH, 3, 3 * HW])
        I["conv_b"] = self.dram_in("conv_b", [DEPTH, 3 * HW])
        I["filt_w1"] = self.dram_in("filt_w1", [DEPTH, FEMB, FHID])
        I["filt_b1"] = self.dram_in("filt_b1", [DEPTH, FHID])
        I["filt_w2"] = self.dram_in("filt_w2", [DEPTH, FHID, FHID])
        I["filt_b2"] = self.dram_in("filt_b2", [DEPTH, FHID])
        I["filt_w3"] = self.dram_in("filt_w3", [DEPTH, FHID, FHID])
        I["filt_b3"] = self.dram_in("filt_b3", [DEPTH, FHID])
        I["filt_w4"] = self.dram_in("filt_w4", [DEPTH, FHID, 2 * HW])
        I["filt_freq"] = self.dram_in("filt_freq", [DEPTH, 3, FHID])
        I["hyena_bias"] = self.dram_in("hyena_bias", [DEPTH, HW])
        I["w_o_attn"] = self.dram_in("w_o_attn", [DEPTH, D, D])
        I["w_o_hyena"] = self.dram_in("w_o_hyena", [DEPTH, HW, D])
        I["w_out"] = self.dram_in("w_out", [DEPTH, D, D])
        I["norm_ffn"] = self.dram_in("norm_ffn", [DEPTH, D])
        I["w_gate_up"] = self.dram_in("w_gate_up", [DEPTH, D, 2 * DFF])
        I["w_down"] = self.dram_in("w_down", [DEPTH, DFF, D])
        I["norm_final"] = self.dram_in("norm_final", [D])
        I["k_ident"] = self.dram_in("k_ident", [128, 128])
        I["k_rmat"] = self.dram_in("k_rmat", [128, 128])
        I["k_ft"] = self.dram_in("k_ft", [128, 4, 128])
        I["k_tw"] = self.dram_in("k_tw", [128, 2, 128])
        I["k_ropec"] = self.dram_in("k_ropec", [128, SEQ])
        I["k_ropes"] = self.dram_in("k_ropes", [128, SEQ])
        I["k_z_lat"] = self.dram_in("k_z_lat", [2, FEMB, SEQ])
        I["k_z_ctx"] = self.dram_in("k_z_ctx", [2, FEMB, CTX])
        I["k_dec_lat"] = self.dram_in("k_dec_lat", [2, HW, SEQ])
        I["k_dec_ctx"] = self.dram_in("k_dec_ctx", [2, HW, CTX])
        self.I = I
        self.out = nc.dram_tensor("out", [SEQ, D], F32, kind="ExternalOutput").ap()

        S = {}
        sc = self.dram_scratch
        S["wb_in"] = sc("wb_in", [DEPTH, D, NPROJ], BF16)
        S["wb_oa"] = sc("wb_oa", [DEPTH, D, D], BF16)
        S["wb_oh"] = sc("wb_oh", [DEPTH, HW, D], BF16)
        S["wb_out"] = sc("wb_out", [DEPTH, D, D], BF16)
        S["wb_gu"] = sc("wb_gu", [DEPTH, D, 2 * DFF], BF16)
        S["wb_dn"] = sc("wb_dn", [DEPTH, DFF, D], BF16)
        for jb in range(3):
            S["KB%d" % jb] = sc("KB%d" % jb, [HW, NFFT], BF16)
            S["KF%d" % jb] = sc("KF%d" % jb, [128, HW, 2, 128], F32)
        self.S = S

        def mkstream(name, T, j, rope, key_off):
            s = Stream()
            s.name, s.T, s.j, s.rope, s.key_off = name, T, j, rope, key_off
            s.TC = min(512, T)
            s.XT = sc("XT_" + name, [D, T], F32)
            s.Qs = sc("Qs_" + name, [NH, HD, T], BF16)
            s.AT = sc("AT_" + name, [D, T], BF16)
            s.U = sc("U_" + name, [3 * HW, T], F32)
            s.G = sc("G_" + name, [2 * D, T], BF16)
            s.VV = sc("VV_" + name, [HW, T], F32)
            s.VB = sc("VB_" + name, [HW, T], BF16)
            s.YB = sc("YB_" + name, [HW, T], F32)
            s.HY = sc("HY_" + name, [HW, T], BF16)
            return s

        self.lat = mkstream("lat", SEQ, 0, True, CTX)
        self.cx = mkstream("ctx", CTX, 1, False, 0)
        self.lat.z, self.lat.dec = I["k_z_lat"], I["k_dec_lat"]
        self.cx.z, self.cx.dec = I["k_z_ctx"], I["k_dec_ctx"]
        self.lat.n_keys = CTX + SEQ
        self.cx.n_keys = CTX

        self.ident = fw.sb("ident", [128, 128], F32)
        self.onesb = fw.sb("onesb", [128, 128], BF16)
        self.onesf = fw.sb("onesf", [128, 128], F32)
        self.epsc = fw.sb("epsc", [128, 1], F32)
        self.rmat = fw.sb("rmat", [128, 128], BF16)
        self.ft = fw.sb("ft", [128, 4, 128], BF16)
        self.tw = fw.sb("tw", [128, 2, 128], F32)
        self.modcol = [fw.sb("modcol%d" % l, [128, 6 * KC, 2], F32) for l in range(DEPTH)]
        self.A1 = [fw.sb("A1_%d" % l, [128, KC, 2], F32) for l in range(DEPTH)]
        self.A2 = [fw.sb("A2_%d" % l, [128, KC, 2], F32) for l in range(DEPTH)]
        self.nfin = fw.sb("nfin", [128, KC], F32)
        self.psw = [fw.ps("psw%d" % i, [128, 1024], F32) for i in range(4)]
        self.ps = []
        for i in range(4):
            for hf in range(2):
                t = Tl(self.psw[i].h[:, hf * 512:(hf + 1) * 512], "psw%d_%d" % (i, hf))
                self.ps.append(t)

        self.jobs = [(self.lat, 0, 0), (self.cx, 0, 1)] + ([(self.lat, 1, 2)] if DEPTH > 1 else [])
        self.prologue()
        for l in range(DEPTH):
            last = (l == DEPTH - 1)
            self.layer_cols(l)
            self.lat.KF = S["KF0"] if l == 0 else S["KF2"]
            self.cx.KF = S["KF1"]
            self.attn_phase(l, last)
            self.ug_phase(l, last)
            streams = [self.lat] if last else [self.lat, self.cx]
            for s in streams:
                self.hyena_a(s, l)
                self.fft_conv(s)
                self.hyena_c(s, l)
            self.merge_phase(l, streams)
            self.ffn_phase(l, streams)
        self.final_phase()
        fw.barrier()
        self.cols_st.close()
        fw.finish()

    def nps(self, lo=0, hi=4):
        p = self.ps[lo + (self.rr % (hi - lo))]
        self.rr += 1
        return p

    def colvec(self, dst, src_ap):
        self.ld(dst, src_ap.rearrange("(c p) -> p c", p=128), allow_slow_non_contiguous=True)

    def prologue(self):
        fw, I, S = self.fw, self.I, self.S
        DEPTH, SEQ = self.DEPTH, self.SEQ
        fw.begin_phase()
        self.ld(self.ident[:], I["k_ident"])
        tmpf = fw.sb("tmpf", [128, 4, 128], F32)
        self.ld(tmpf[:, 0, :], I["k_rmat"])
        self.cp("dve", self.rmat[:], tmpf[:, 0, :])
        tmpf2 = fw.sb("tmpf2", [128, 4, 128], F32)
        self.ld(tmpf2[:], I["k_ft"])
        self.cp("dve", self.ft[:], tmpf2[:])
        self.ld(self.tw[:], I["k_tw"])
        self.memset("dve", self.onesb[:], 1.0)
        self.memset("dve", self.onesf[:], 1.0)
        self.memset("dve", self.epsc[:], EPS)
        self.colvec(self.nfin[:], I["norm_final"])
        for _ in self.cast_gen([("w_in", "wb_in", D, NPROJ, 0)]):
            pass
        ccol = fw.sb("ccol", [128, KC, 2], F32)
        scol = fw.sb("scol", [128, KC, 2], F32)
        self.colvec(ccol[:, :, 0], I["c"])
        self.colvec(ccol[:, :, 1], I["c_ctx"])
        self.act(scol[:], ccol[:], AF.Silu)
        bcol = fw.sb("bcol", [128, 6 * KC], F32)
        wm = [fw.sb("wm%d" % i, [128, KC, 512], F32) for i in range(2)]
        pm = self.ps[7]
        n = 0
        for l in range(DEPTH):
            for q4 in range(4):
                self.ld(bcol[:, q4 * 12:(q4 + 1) * 12],
                        I["b_mod"][l, q4 * 1536:(q4 + 1) * 1536].rearrange("(c p) -> p c", p=128),
                        allow_slow_non_contiguous=True)
            for cb in range(12):
                w = wm[n % 2]
                n += 1
                self.ld(w[:], I["w_mod"][l, :, cb * 512:(cb + 1) * 512].rearrange("(kc p) n -> p kc n", p=128))
                for f4 in range(4):
                    f = cb * 4 + f4
                    for kc in range(KC):
                        self.mm(pm[:, f * 2:f * 2 + 2], w[:, kc, f4 * 128:(f4 + 1) * 128], scol[:, kc, :],
                                kc == 0, kc == KC - 1, skip=True)
            pv = pm.v(pm.h[:, 0:96].rearrange("p (f j) -> p f j", j=2))
            self.tt("dve", self.modcol[l][:], pv, bcol.v(bcol.h[:].unsqueeze(2).broadcast_to([128, 48, 2])), ALU.add)
        fw.end_phase()
        fw.begin_phase()
        self.zt = fw.sb("zt", [128, 2048], BF16)
        self.memset("pool", self.zt[:], 0.0)
        xtok = [fw.sb("xtok%d" % i, [128, 4, D], F32) for i in range(2)]
        xts = [fw.sb("xts%d" % i, [128, KC, 512], F32) for i in range(2)]

        xtok_c = [fw.sb("xtokc", [128, 2, D], F32)]
        xts_c = [fw.sb("xtsc", [128, KC, 256], F32)]

        def xt_gen(src, s, xtok, xts):
            TC = s.TC
            nj = TC // 128
            for ch in range(s.T // TC):
                t0 = ch * TC
                xt, xs = xtok[ch % len(xtok)], xts[ch % len(xts)]
                self.ld(xt[:, :nj, :], src[t0:t0 + TC, :].rearrange("(j p) f -> p j f", p=128))
                for c in range(KC):
                    p = self.nps(0, 4)
                    for j in range(nj):
                        self.tr(p[:, j * 128:(j + 1) * 128], xt[:, j, c * 128:(c + 1) * 128], self.ident[:])
                    self.cp("act" if c % 2 == 0 else "dve", xs[:, c, :TC], p[:, :TC])
                    if c % 2 == 1:
                        yield
                self.stq(s.XT[:, t0:t0 + TC].rearrange("(c p) t -> p c t", p=128), xs[:, :, :TC])
                yield

        gens = [xt_gen(I["x"], self.lat, xtok, xts), xt_gen(I["ctx"], self.cx, xtok_c, xts_c)]
        for (st_, l_, jb) in self.jobs:
            gens.append(self.mlp_gen(st_, l_, S["KB%d" % jb], self.ps[4 + jb]))
        while gens:
            for g in list(gens):
                try:
                    next(g)
                except StopIteration:
                    gens.remove(g)
        fw.end_phase()

    def cast_gen(self, items):
        fw, I, S = self.fw, self.I, self.S
        stg = [fw.sb("stg%d" % i, [128, 2048], F32) for i in range(3)]
        stb = [fw.sb("stb%d" % i, [128, 2048], BF16) for i in range(3)]
        n = 0
        for (src, dst, K, N, l) in items:
            for rb in range(K // 128):
                for c0 in range(0, N, 2048):
                    cw = min(2048, N - c0)
                    a, b_ = stg[n % 3], stb[n % 3]
                    self.ld(a[:, :cw], I[src][l, rb * 128:(rb + 1) * 128, c0:c0 + cw])
                    self.cp("pool", b_[:, :cw], a[:, :cw])
                    self.stq(S[dst][l, rb * 128:(rb + 1) * 128, c0:c0 + cw], b_[:, :cw])
                    n += 1
                    yield

    def layer_cols(self, l):
        fw, I = self.fw, self.I
        if l > 0:
            fw.barrier()
            self.cols_st.close()
        self.cols_st = ExitStack()
        assert fw.phase_st is None
        fw.phase_st = self.cols_st
        nm = fw.sb("nmcol", [128, KC], F32)
        nf = fw.sb("nfcol", [128, KC], F32)
        self.colvec(nm[:], I["norm_mix"][l])
        self.colvec(nf[:], I["norm_ffn"][l])
        mc = self.modcol[l]
        self.stt("dve", self.A1[l][:], mc[:, 8:16, :], 1.0, nm.v(nm.h[:].unsqueeze(2).broadcast_to([128, KC, 2])), ALU.add, ALU.mult)
        self.stt("dve", self.A2[l][:], mc[:, 32:40, :], 1.0, nf.v(nf.h[:].unsqueeze(2).broadcast_to([128, KC, 2])), ALU.add, ALU.mult)
        self.gq = fw.sb("gq", [128, 1], F32)
        self.gk = fw.sb("gk", [128, 1], F32)
        self.ld(self.gq[:], I["q_norm"][l].rearrange("(p o) -> p o", o=1), allow_slow_non_contiguous=True)
        self.ld(self.gk[:], I["k_norm"][l].rearrange("(p o) -> p o", o=1), allow_slow_non_contiguous=True)
        grow = fw.sb("grow", [1, 2, 128], F32)
        self.ld(grow[:, 0, :], I["q_norm"][l].rearrange("(o n) -> o n", o=1))
        self.ld(grow[:, 1, :], I["k_norm"][l].rearrange("(o n) -> o n", o=1))
        gmax = fw.sb("gmax", [1, 2], F32)
        r, w = _rw([grow[:]], [gmax[:]])
        go, gi = gmax.h[:], grow.h[:]
        fw.op("dve", lambda h: h.tensor_reduce(go, gi, mybir.AxisListType.X, ALU.max, apply_absolute_value=True), r, w)
        nb = fw.sb("nb", [1, 2], F32)
        self.stt("dve", nb[:, 0:1], gmax[:, 0:1], -math.sqrt(128.0), gmax[:, 1:2], ALU.mult, ALU.mult)
        self.stt("dve", nb[:, 1:2], gmax[:, 0:1], -math.sqrt(128.0), gmax[:, 1:2], ALU.mult, ALU.mult)
        pn = self.ps[6]
        self.mm(pn[:, 0:2], self.onesf[0:1, :], nb[:], True, True)
        self.negB = fw.sb("negB", [128, 1], F32)
        self.cp("dve", self.negB[:], pn[:, 0:1])
        self.cw = fw.sb("cwcol", [128, 3, 12], F32)
        for tap in range(3):
            self.colvec(self.cw[:, tap, :], I["conv_w"][l, tap])
        self.cb = fw.sb("cbcol", [128, 12], F32)
        self.colvec(self.cb[:], I["conv_b"][l])
        self.hb = fw.sb("hbcol", [128, 4], F32)
        self.colvec(self.hb[:], I["hyena_bias"][l])
        fw.phase_st = None

    def rms_mod(self, xs, TC, A, Bm, j, out, T_):
        sq, sd, rstd, tmp = T_["sq"], T_["sd"], T_["rstd"], T_["tmp"]
        self.act(sq[:, :, :TC], xs[:, :, :TC], AF.Square)
        pS = self.ps[7]
        for c in range(KC):
            self.mm(pS[:, :TC], self.onesb[:], sq[:, c, :TC], c == 0, c == KC - 1)
        self.act(sd[:, :TC], pS[:, :TC], AF.Ln, bias=self.epsc[:], scale=1.0 / D)
        self.act(rstd[:, :TC], sd[:, :TC], AF.Exp, scale=-0.5)
        for c in range(KC):
            if Bm is None:
                self.stt("dve", out[:, c, :TC], xs[:, c, :TC], A(c, j), rstd[:, :TC], ALU.mult, ALU.mult)
            else:
                t = tmp[c % 2]
                self.stt("dve", t[:, :TC], xs[:, c, :TC], A(c, j), rstd[:, :TC], ALU.mult, ALU.mult)
                self.act(out[:, c, :TC], t[:, :TC], AF.Identity, bias=Bm(c, j))

    def rms_tiles(self, TC):
        fw = self.fw
        return {"sq": fw.sb("sq", [128, KC, TC], BF16), "sd": fw.sb("sd", [128, TC], F32),
                "rstd": fw.sb("rstd", [128, TC], F32), "tmp": [fw.sb("rtmp%d" % i, [128, TC], F32) for i in range(2)]}

    def load_w(self, dst, src, kchunks):
        for kc in range(kchunks):
            self.ld(dst[:, kc, :], src[kc * 128:(kc + 1) * 128, :])

    def attn_phase(self, l, last):
        fw, I, S = self.fw, self.I, self.S
        lat, cx = self.lat, self.cx
        fw.begin_phase()
        NK = lat.n_keys
        KT = fw.sb("KT", [128, NKV, NK], BF16)
        V = fw.sb("V", [128, NK // 128, NKV * HD], BF16)
        fw.begin_phase()
        wq = fw.sb("wqkv", [128, KC, 1536], BF16)
        for kc in range(KC):
            self.ld(wq[:, kc, :], S["wb_in"][l, kc * 128:(kc + 1) * 128, 0:1536])
        RT = self.rms_tiles(512)
        xs2 = [fw.sb("xs%d" % i, [128, KC, 512], F32) for i in range(2)]
        hT2 = [fw.sb("hT%d" % i, [128, KC, 512], BF16) for i in range(2)]
        rc2 = [fw.sb("rc%d" % i, [128, 512], F32) for i in range(2)]
        rs2 = [fw.sb("rs%d" % i, [128, 512], F32) for i in range(2)]
        sqh = [fw.sb("sqh%d" % i, [128, 512], BF16) for i in range(2)]
        qg = [fw.sb("qg%d" % i, [128, 512], BF16) for i in range(2)]
        sdh = [fw.sb("sdh%d" % i, [128, 512], F32) for i in range(2)]
        rsh = [fw.sb("rsh%d" % i, [128, 512], F32) for i in range(2)]
        t1 = [fw.sb("t1_%d" % i, [128, 512], F32) for i in range(2)]
        t2 = [fw.sb("t2_%d" % i, [128, 512], F32) for i in range(2)]
        qo = [fw.sb("qo%d" % i, [128, 512], BF16) for i in range(3)]
        mc = self.modcol[l]
        A1 = self.A1[l]
        hn = 0
        nchunk = 0
        for s in [cx, lat]:
            want_q = (s is lat) or (not last)
            TC = s.TC
            for ch in range(s.T // TC):
                t0 = ch * TC
                xs, hT = xs2[nchunk % 2], hT2[nchunk % 2]
                rc, rs = rc2[nchunk % 2], rs2[nchunk % 2]
                nchunk += 1
                self.ld(xs[:, :, :TC], s.XT[:, t0:t0 + TC].rearrange("(c p) t -> p c t", p=128))
                if s.rope:
                    self.ld(rc[:, :TC], I["k_ropec"][:, t0:t0 + TC])
                    self.ld(rs[:, :TC], I["k_ropes"][:, t0:t0 + TC])
                self.rms_mod(xs, TC, lambda c, j: A1[:, c, j:j + 1], lambda c, j: mc[:, c, j:j + 1], s.j, hT, RT)
                heads = ([("q", j) for j in range(NH)] if want_q else []) + [("k", 0), ("k", 1)]
                for (kind, j) in heads:
                    col0 = j * 128 if kind == "q" else D + j * 128
                    gcol = self.gq if kind == "q" else self.gk
                    i2 = hn % 2
                    hn += 1
                    p = self.nps(0, 4)
                    for kc in range(KC):
                        self.mm(p[:, :TC], wq[:, kc, col0:col0 + 128], hT[:, kc, :TC], kc == 0, kc == KC - 1)
                    self.act(sqh[i2][:, :TC], p[:, :TC], AF.Square)
                    self.act(qg[i2][:, :TC], p[:, :TC], AF.Identity, scale=gcol[:])
                    pa = self.ps[4 + i2]
                    self.mm(pa[:, :TC], self.onesb[:], sqh[i2][:, :TC], True, True)
                    self.act(sdh[i2][:, :TC], pa[:, :TC], AF.Ln, bias=self.epsc[:], scale=1.0 / HD)
                    self.act(rsh[i2][:, :TC], sdh[i2][:, :TC], AF.Exp, scale=-0.5)
                    if kind == "q":
                        qoi = qo[hn % 3]
                        dest = qoi[:, :TC]
                    else:
                        dest = KT[:, j, s.key_off + t0: s.key_off + t0 + TC]
                    if s.rope:
                        pb = self.ps[6]
                        self.mm(pb[:, :TC], self.rmat[:], qg[i2][:, :TC], True, True)
                        self.tt("dve", t1[i2][:, :TC], qg[i2][:, :TC], rc[:, :TC], ALU.mult)
                        self.tt("dve", t2[i2][:, :TC], pb[:, :TC], rs[:, :TC], ALU.mult)
                        self.tt("pool", t1[i2][:, :TC], t1[i2][:, :TC], t2[i2][:, :TC], ALU.add)
                        self.tt("dve", dest, t1[i2][:, :TC], rsh[i2][:, :TC], ALU.mult)
                    else:
                        self.tt("dve", dest, qg[i2][:, :TC], rsh[i2][:, :TC], ALU.mult)
                    if kind == "q":
                        self.stq(s.Qs[j, :, t0:t0 + TC], qoi[:, :TC])
                for tsub in range(TC // 128):
                    p = self.nps(0, 4)
                    for kc in range(KC):
                        self.mm(p[:, 0:256], hT[:, kc, tsub * 128:(tsub + 1) * 128], wq[:, kc, 1280:1536], kc == 0, kc == KC - 1)
                    self.cp("act", V[:, (s.key_off + t0) // 128 + tsub, :], p[:, 0:256])
        fw.end_phase()
        fw.begin_phase()
        qt3 = [fw.sb("qt%d" % i, [128, 512], BF16) for i in range(2)]
        pt3 = [fw.sb("pt%d" % i, [128, 2, 512], BF16) for i in range(4)]
        rl2 = [fw.sb("rl%d" % i, [128, 512], F32) for i in range(2)]
        ao2 = [fw.sb("ao%d" % i, [128, 512], BF16) for i in range(2)]
        SCALE = HD ** -0.5
        GRP = 8
        osb2 = [fw.sb("osb%d" % i, [128, 512], F32) for i in range(2)]
        acc2s = [fw.sb("acc2_%d" % i, [128, 2, 512], BF16) for i in range(2)]
        accfs = [fw.sb("accf_%d" % i, [128, 512], BF16) for i in range(2)]
        SHIFT = -8.0
        pairs = []
        for s in ([lat] if last else [lat, cx]):
            for h in range(NH):
                for qc in range(s.T // s.TC):
                    for j in range(s.n_keys // 256):
                        pairs.append((s, h, qc, j))
        st = {}
        pend = []

        def emit_S(pi):
            s, h, qc, j = pairs[pi]
            TC = s.TC
            kvh = h // (NH // NKV)
            if j == 0:
                nq = st.get("nq", 0)
                st["nq"] = nq + 1
                qt = qt3[nq % 2]
                self.ld(qt[:, :TC], s.Qs[h, :, qc * TC:(qc + 1) * TC])
                st[("qt", s.name, h, qc)] = (qt, nq)
            qt, nq = st[("qt", s.name, h, qc)]
            bufs = [0, 1] if st["bg_on"] else [0, 1, 3]
            wb = bufs[st["nS"] % len(bufs)]
            st["nS"] += 1
            st[("wb", pi)] = wb
            for a in range(2):
                kt = 2 * j + a
                p = self.ps[wb * 2 + a]
                self.mm(p[:, :TC], KT[:, kvh, kt * 128:(kt + 1) * 128], qt[:, :TC], True, True)

        def emit_rest(pi):
            s, h, qc, j = pairs[pi]
            TC = s.TC
            kvh = h // (NH // NKV)
            n_kt = s.n_keys // 128
            qt, nq = st[("qt", s.name, h, qc)]
            po = self.ps[4]
            pl = self.ps[5]
            wb = st[("wb", pi)]
            pw = self.psw[wb]
            pin = View(pw.h[:].rearrange("p (a n) -> p a n", a=2)[:, :, :TC], (self.ps[wb * 2].res, self.ps[wb * 2 + 1].res))
            pt = pt3[pi % 4]
            self.act(pt[:, :, :TC], pin, AF.Exp, bias=SHIFT, scale=SCALE)
            due = list(pend)
            del pend[:]
            for a in range(2):
                kt = 2 * j + a
                self.mm(po[:, :TC], V[:, kt, kvh * HD:(kvh + 1) * HD], pt[:, a, :TC], kt == 0, kt == n_kt - 1)
            for f_ in due:
                f_()
            npairs = n_kt // 2
            g0 = (j // GRP) * GRP
            gsz = min(GRP, npairs - g0)
            jj = j - g0
            ai = (st.get("na", 0)) % 2
            acc2, accf = acc2s[ai], accfs[ai]
            if gsz == 1:
                self.tt("dve", accf[:, :TC], pt[:, 0, :TC], pt[:, 1, :TC], ALU.add)
            elif jj == 0:
                st["ptprev"] = pt
            elif jj == 1:
                self.tt("dve", acc2[:, :, :TC], st["ptprev"][:, :, :TC], pt[:, :, :TC], ALU.add)
            else:
                self.tt("dve", acc2[:, :, :TC], acc2[:, :, :TC], pt[:, :, :TC], ALU.add)
            if jj == gsz - 1:
                if gsz > 1:
                    self.tt("dve", accf[:, :TC], acc2[:, 0, :TC], acc2[:, 1, :TC], ALU.add)
                first, lastg = (g0 == 0), (g0 + gsz == npairs)

                def emit_L(accf=accf, TC=TC, first=first, lastg=lastg):
                    self.mm(pl[:, :TC], self.onesb[:], accf[:, :TC], first, lastg)
                if lastg:
                    for f_ in pend:
                        f_()
                    del pend[:]
                    emit_L()
                else:
                    pend.append(emit_L)
                st["na"] = st.get("na", 0) + 1
            if j == n_kt // 2 - 1:
                rl, ao = rl2[nq % 2], ao2[nq % 2]
                osb = osb2[nq % 2]
                self.cp("act", osb[:, :TC], po[:, :TC])
                self.act(rl[:, :TC], pl[:, :TC], AF.Ln)
                self.act(rl[:, :TC], rl[:, :TC], AF.Exp, scale=-1.0)
                self.tt("dve", ao[:, :TC], osb[:, :TC], rl[:, :TC], ALU.mult)
                self.stq(s.AT[h * HD:(h + 1) * HD, qc * TC:(qc + 1) * TC], ao[:, :TC])

        bg = None
        if l == 0:
            items = []
            for ll in range(self.DEPTH):
                for (src, dst, K_, N_) in [("w_in", "wb_in", D, NPROJ), ("w_o_attn", "wb_oa", D, D), ("w_o_hyena", "wb_oh", HW, D),
                                           ("w_out", "wb_out", D, D), ("w_gate_up", "wb_gu", D, 2 * DFF), ("w_down", "wb_dn", DFF, D)]:
                    if not (src == "w_in" and ll == 0):
                        items.append((src, dst, K_, N_, ll))

            def chain():
                for x_ in self.spectrum_gen([jb for (_, l_, jb) in self.jobs if l_ == 0], self.ps[6], self.ps[7]):
                    yield
                for x_ in self.cast_gen(items):
                    yield
            bg = chain()
        import os
        if bg is not None and os.environ.get("BG_FIRST"):
            for x_ in bg:
                pass
            bg = None
        st["nS"] = 0
        nextS = 0
        for pi in range(len(pairs)):
            st["bg_on"] = bg is not None
            look = 1 if bg is not None else 2
            while nextS <= pi + look and nextS < len(pairs):
                emit_S(nextS)
                nextS += 1
            emit_rest(pi)
            if bg is not None and pi % 3 == 2:
                if next(bg, "done") == "done":
                    bg = None
        if bg is not None:
            for x_ in bg:
                pass
        fw.end_phase()
        fw.end_phase()

    def ug_phase(self, l, last):
        fw, S = self.fw, self.S
        fw.begin_phase()
        NW = NPROJ - 1536
        w = fw.sb("wug", [128, KC, NW], BF16)
        for kc in range(KC):
            self.ld(w[:, kc, :], S["wb_in"][l, kc * 128:(kc + 1) * 128, 1536:NPROJ])
        RT = self.rms_tiles(512)
        xs2 = [fw.sb("xs%d" % i, [128, KC, 512], F32) for i in range(2)]
        hT2 = [fw.sb("hT%d" % i, [128, KC, 512], BF16) for i in range(2)]
        us2 = [fw.sb("us%d" % i, [128, 4, 512], F32) for i in range(2)]
        gs2 = [fw.sb("gs%d" % i, [128, 4, 512], BF16) for i in range(2)]
        mc, A1 = self.modcol[l], self.A1[l]
        n = 0
        nu = 0
        ng = 0
        for s in ([self.lat] if last else [self.lat, self.cx]):
            TC = s.TC
            for ch in range(s.T // TC):
                t0 = ch * TC
                xs, hT = xs2[n % 2], hT2[n % 2]
                n += 1
                self.ld(xs[:, :, :TC], s.XT[:, t0:t0 + TC].rearrange("(c p) t -> p c t", p=128))
                self.rms_mod(xs, TC, lambda c, j: A1[:, c, j:j + 1], lambda c, j: mc[:, c, j:j + 1], s.j, hT, RT)
                for o4 in range(3):
                    us = us2[nu % 2]
                    nu += 1
                    for oi in range(4):
                        oc = o4 * 4 + oi
                        p = self.nps(0, 6)
                        for kc in range(KC):
                            self.mm(p[:, :TC], w[:, kc, oc * 128:(oc + 1) * 128], hT[:, kc, :TC], kc == 0, kc == KC - 1)
                        self.cp("act" if oi % 2 == 0 else "dve", us[:, oi, :TC], p[:, :TC])
                    self.stq(s.U[o4 * 512:(o4 + 1) * 512, t0:t0 + TC].rearrange("(c p) t -> p c t", p=128), us[:, :, :TC])
                for o4 in range(4):
                    gs = gs2[ng % 2]
                    ng += 1
                    for oi in range(4):
                        oc = 12 + o4 * 4 + oi
                        p = self.nps(0, 6)
                        for kc in range(KC):
                            self.mm(p[:, :TC], w[:, kc, oc * 128:(oc + 1) * 128], hT[:, kc, :TC], kc == 0, kc == KC - 1)
                        self.act(gs[:, oi, :TC], p[:, :TC], AF.Sigmoid)
                    self.stq(s.G[o4 * 512:(o4 + 1) * 512, t0:t0 + TC].rearrange("(c p) t -> p c t", p=128), gs[:, :, :TC])
        fw.end_phase()

    def conv3(self, s, base_chunk, cc, t0, TCH, ub, out):
        T = s.T
        row0 = (base_chunk + cc) * 128
        lo, hi = t0 - 1, t0 + TCH + 1
        a = 0
        if lo < 0:
            self.memset("pool", ub[:, 0:1], 0.0)
            lo, a = 0, 1
        b = TCH + 2
        if hi > T:
            self.memset("pool", ub[:, TCH + 1:TCH + 2], 0.0)
            hi, b = T, TCH + 1
        self.ld(ub[:, a:b], s.U[row0:row0 + 128, lo:hi])
        k = base_chunk + cc
        cw, cb = self.cw, self.cb
        self.act(out[:, :TCH], ub[:, 1:TCH + 1], AF.Identity, bias=cb[:, k:k + 1], scale=cw[:, 1, k:k + 1])
        self.stt("dve", out[:, :TCH], ub[:, 0:TCH], cw[:, 0, k:k + 1], out[:, :TCH], ALU.mult, ALU.add)
        self.stt("dve", out[:, :TCH], ub[:, 2:TCH + 2], cw[:, 2, k:k + 1], out[:, :TCH], ALU.mult, ALU.add)

    def hyena_a(self, s, l):
        fw = self.fw
        fw.begin_phase()
        TCH = min(2048, s.T)
        ub2 = [fw.sb("ub%d" % i, [128, TCH + 2], F32) for i in range(4)]
        cx1 = [fw.sb("cx1_%d" % i, [128, TCH], F32) for i in range(2)]
        cv = [fw.sb("cv_%d" % i, [128, TCH], F32) for i in range(2)]
        vb = [fw.sb("vb_%d" % i, [128, TCH], BF16) for i in range(2)]
        n = 0
        for cc in range(4):
            for tch in range(s.T // TCH):
                t0 = tch * TCH
                i2 = n % 2
                self.conv3(s, 4, cc, t0, TCH, ub2[(2 * n) % 4], cx1[i2])
                self.conv3(s, 8, cc, t0, TCH, ub2[(2 * n + 1) % 4], cv[i2])
                n += 1
                self.tt("dve", cv[i2][:], cv[i2][:], cx1[i2][:], ALU.mult)
                self.cp("act", vb[i2][:], cv[i2][:])
                self.stq(s.VV[cc * 128:(cc + 1) * 128, t0:t0 + TCH], cv[i2][:])
                self.stq(s.VB[cc * 128:(cc + 1) * 128, t0:t0 + TCH], vb[i2][:])
        fw.end_phase()

    def hyena_c(self, s, l):
        fw = self.fw
        fw.begin_phase()
        TCH = min(2048, s.T)
        ub2 = [fw.sb("ub%d" % i, [128, TCH + 2], F32) for i in range(2)]
        cx0 = [fw.sb("cx0_%d" % i, [128, TCH], F32) for i in range(2)]
        yr = [fw.sb("yr_%d" % i, [128, TCH], F32) for i in range(2)]
        vv = [fw.sb("vv_%d" % i, [128, TCH], F32) for i in range(2)]
        hy = [fw.sb("hy_%d" % i, [128, TCH], BF16) for i in range(2)]
        n = 0
        for cc in range(4):
            for tch in range(s.T // TCH):
                t0 = tch * TCH
                i2 = n % 2
                n += 1
                self.conv3(s, 0, cc, t0, TCH, ub2[i2], cx0[i2])
                self.ld(yr[i2][:], s.YB[cc * 128:(cc + 1) * 128, t0:t0 + TCH])
                self.ld(vv[i2][:], s.VV[cc * 128:(cc + 1) * 128, t0:t0 + TCH])
                self.stt("dve", yr[i2][:], vv[i2][:], self.hb[:, cc:cc + 1], yr[i2][:], ALU.mult, ALU.add)
                self.tt("pool", hy[i2][:], yr[i2][:], cx0[i2][:], ALU.mult)
                self.stq(s.HY[cc * 128:(cc + 1) * 128, t0:t0 + TCH], hy[i2][:])
        fw.end_phase()

    def cmul(self, out, pin, tre, tim, conj, tmp1, tmp2, e_re="dve"):
        self.tt("dve", tmp1[:], pin, tre, ALU.mult)
        self.tt("dve", tmp2[:], pin, tim, ALU.mult)
        if not conj:
            self.tt(e_re, out[:, :, 0, :], tmp1[:, :, 0, :], tmp2[:, :, 1, :], ALU.subtract)
            self.tt("pool", out[:, :, 1, :], tmp2[:, :, 0, :], tmp1[:, :, 1, :], ALU.add)
        else:
            self.tt(e_re, out[:, :, 0, :], tmp1[:, :, 0, :], tmp2[:, :, 1, :], ALU.add)
            self.tt("pool", out[:, :, 1, :], tmp1[:, :, 1, :], tmp2[:, :, 0, :], ALU.subtract)

    def p4(self, p):
        return p.v(p.h[:].rearrange("p (c r k) -> p c r k", c=2, r=2))

    def tw_b(self, idx):
        return self.tw.v(self.tw.h[:, idx, :].unsqueeze(1).unsqueeze(1).broadcast_to([128, 2, 2, 128]))

    def fft_s1(self, v2, nK, c0, Bt, tmp1, tmp2, pA):
        ft = self.ft
        for c in range(2):
            self.mm(pA[:, c * 256:(c + 1) * 256], v2[0:nK, c0 + c, :],
                    ft.v(ft.h[0:nK, 0:2, :]), c == 0, c == 1, skip=True)
        self.cmul(Bt, self.p4(pA), self.tw_b(0), self.tw_b(1), False, tmp1, tmp2)

    def fft_s2(self, Bt, pX):
        ft = self.ft
        self.mm(pX[:], ft[:, 0, :], Bt[:], True, False, skip=True)
        pX4 = self.p4(pX)
        self.mm(View(pX4.ap[:, :, 0, :], pX.res), ft[:, 3, :], Bt[:, :, 1, :], False, False, skip=True)
        self.mm(View(pX4.ap[:, :, 1, :], pX.res), ft[:, 1, :], Bt[:, :, 0, :], False, True, skip=True)

    def fft_conv(self, s):
        fw = self.fw
        fw.begin_phase()
        T = s.T
        nK = T // 128
        ft = self.ft
        v2s = [fw.sb("v2_%d" % i, [max(nK, 2), 128, 128], BF16) for i in range(2)]
        y2 = fw.sb("y2", [max(nK, 2), 128, 128], F32)
        Bt = [fw.sb("Bt%d" % i, [128, 2, 2, 128], BF16) for i in range(2)]
        Yt = [fw.sb("Yt%d" % i, [128, 2, 2, 128], BF16) for i in range(2)]
        Ut = [fw.sb("Ut%d" % i, [128, 2, 2, 128], BF16) for i in range(2)]
        kf = [fw.sb("kf%d" % i, [128, 2, 2, 128], F32) for i in range(3)]
        tA = [[fw.sb("cmA%d_%d" % (k, i), [128, 2, 2, 128], F32) for i in range(2)] for k in range(3)]
        tB = [[fw.sb("cmB%d_%d" % (k, i), [128, 2, 2, 128], F32) for i in range(2)] for k in range(3)]
        NG = 64
        for cc in range(4):
            v2 = v2s[cc % 2]
            for q4 in range(4):
                self.ld(v2[0:nK, q4 * 32:(q4 + 1) * 32, :],
                        s.VB[cc * 128 + q4 * 32: cc * 128 + (q4 + 1) * 32, :].rearrange("ch (n1 n2) -> n1 ch n2", n2=128))
            for it in range(NG + 3):
                g = it
                if 0 <= g < NG:
                    i2 = g % 2
                    self.ld(kf[g % 3][:], s.KF[:, cc * 128 + g * 2: cc * 128 + g * 2 + 2, :, :])
                    self.fft_s1(v2, nK, g * 2, Bt[i2], tA[0][i2], tB[0][i2], self.ps[i2])
                g = it - 1
                if 0 <= g < NG:
                    i2 = g % 2
                    kfi = kf[g % 3]
                    pX = self.ps[2 + i2]
                    self.fft_s2(Bt[i2], pX)
                    kre = kfi.v(kfi.h[:, :, 0:1, :].broadcast_to([128, 2, 2, 128]))
                    kim = kfi.v(kfi.h[:, :, 1:2, :].broadcast_to([128, 2, 2, 128]))
                    self.cmul(Yt[i2], self.p4(pX), kre, kim, False, tA[1][i2], tB[1][i2], e_re="pool")
                g = it - 2
                if 0 <= g < NG:
                    i2 = g % 2
                    pU = self.ps[4 + i2]
                    for c in range(2):
                        self.mm(pU[:, c * 256:(c + 1) * 256], Yt[i2][:, c, 0, :], ft.v(ft.h[:, 2:4, :]), c == 0, False, skip=True)
                        self.mm(pU[:, c * 256:(c + 1) * 256], Yt[i2][:, c, 1, :], ft.v(ft.h[:, 1:3, :]), False, c == 1, skip=True)
                    self.cmul(Ut[i2], self.p4(pU), self.tw_b(0), self.tw_b(1), True, tA[2][i2], tB[2][i2], e_re="pool")
                g = it - 3
                if 0 <= g < NG:
                    i2 = g % 2
                    pY = self.ps[6 + i2]
                    pYv = View(pY.h[0:nK, 0:256].rearrange("p (c k) -> p c k", c=2), pY.res)
                    self.mm(pYv, ft[:, 0, 0:nK], Ut[i2][:, :, 0, :], True, False, skip=True)
                    self.mm(pYv, ft[:, 1, 0:nK], Ut[i2][:, :, 1, :], False, True, skip=True)
                    self.cp("act", y2[0:nK, g * 2:g * 2 + 2, :], pYv)
            for q4 in range(4):
                self.stq(s.YB[cc * 128 + q4 * 32: cc * 128 + (q4 + 1) * 32, :].rearrange("ch (m1 m2) -> m1 ch m2", m2=128),
                         y2[0:nK, q4 * 32:(q4 + 1) * 32, :])
        fw.end_phase()

    def mlp_gen(self, s, l, KB, pbank):
        fw, I = self.fw, self.I
        L = s.T
        TC = min(512, L)
        w1 = fw.sb("fw1", [FEMB, FHID], F32)
        w2 = fw.sb("fw2", [FHID, FHID], F32)
        w3 = fw.sb("fw3", [FHID, FHID], F32)
        w4 = fw.sb("fw4", [FHID, 2 * HW], F32)
        self.ld(w1[:], I["filt_w1"][l])
        self.ld(w2[:], I["filt_w2"][l])
        self.ld(w3[:], I["filt_w3"][l])
        self.ld(w4[:], I["filt_w4"][l])
        fq = fw.sb("fq", [FHID, 3], F32)
        fb = fw.sb("fb", [FHID, 3], F32)
        self.ld(fq[:], I["filt_freq"][l].rearrange("i p -> p i"), allow_slow_non_contiguous=True)
        for i, nm in enumerate(["filt_b1", "filt_b2", "filt_b3"]):
            self.ld(fb[:, i:i + 1], I[nm][l].rearrange("(p o) -> p o", o=1), allow_slow_non_contiguous=True)
        fsc = fw.sb("fsc", [FHID, 3], F32)
        fbc = fw.sb("fbc", [FHID, 3], F32)
        self.ts("dve", fsc[:], fq[:], 1.0 / 3.0, 0.0, ALU.mult, ALU.add)
        self.tt("dve", fbc[:], fsc[:], fb[:], ALU.mult)
        zt = self.zt
        z0, z1 = L, NFFT - L + 1
        for cc in range(4):
            p0 = z0
            while p0 < z1:
                n = min(2048, z1 - p0)
                self.stq(KB[cc * 128:(cc + 1) * 128, p0:p0 + n], zt[:, :n], allow_slow_non_contiguous=True)
                p0 += n
            yield
        zin = [fw.sb("zin%d" % i, [FEMB, TC], F32) for i in range(2)]
        hs = fw.sb("hs", [FHID, TC], F32)
        s2 = fw.sb("s2", [FHID, TC], F32)
        hh = [fw.sb("hh%d" % k, [FHID, TC], F32) for k in range(3)]
        dct = [fw.sb("dct%d" % i, [128, TC], F32) for i in range(2)]
        kr = [fw.sb("kr%d" % i, [128, TC], BF16) for i in range(2)]
        n = 0
        nd = 0
        p = pbank
        for d_ in range(2):
            for ch in range(L // TC):
                t0 = ch * TC
                i2 = n % 2
                n += 1
                self.ld(zin[i2][:], s.z[d_, :, t0:t0 + TC])
                cur = zin[i2]
                curK = FEMB
                for k, wk in enumerate([w1, w2, w3]):
                    self.mm(p[0:FHID, :TC], wk[0:curK, :], cur[0:curK, :], True, True)
                    self.act(hs[:], p[0:FHID, :TC], AF.Sin, bias=fbc[:, k:k + 1], scale=fsc[:, k:k + 1])
                    self.tt("dve", s2[:], hs[:], hs[:], ALU.mult)
                    self.ts("dve", s2[:], s2[:], -4.0, 3.0, ALU.mult, ALU.add)
                    self.tt("dve", hh[k][:], s2[:], hs[:], ALU.mult)
                    cur = hh[k]
                    curK = FHID
                    yield
                for oc in range(4):
                    self.mm(p[:, :TC], w4[:, d_ * HW + oc * 128: d_ * HW + (oc + 1) * 128], cur[:], True, True)
                    dc, krr = dct[nd % 2], kr[nd % 2]
                    nd += 1
                    self.ld(dc[:], s.dec[d_, oc * 128:(oc + 1) * 128, t0:t0 + TC])
                    self.tt("dve", krr[:], p[:, :TC], dc[:], ALU.mult)
                    if d_ == 0:
                        self.stq(KB[oc * 128:(oc + 1) * 128, t0:t0 + TC], krr[:])
                    else:
                        pos = NFFT - L + 1 + t0
                        nn = TC if t0 + TC < L else TC - 1
                        self.stq(KB[oc * 128:(oc + 1) * 128, pos:pos + nn], krr[:, :nn])
                    yield

    def spectrum_gen(self, jobs, pA, pX):
        fw, S = self.fw, self.S
        k2s = [fw.sb("k2_%d" % i, [128, 32, 128], BF16) for i in range(2)]
        Bt = [fw.sb("Bt%d" % i, [128, 2, 2, 128], BF16) for i in range(2)]
        tmp1 = [fw.sb("cm1_%d" % i, [128, 2, 2, 128], F32) for i in range(2)]
        tmp2 = [fw.sb("cm2_%d" % i, [128, 2, 2, 128], F32) for i in range(2)]
        xo = [fw.sb("xo%d" % i, [128, 2, 2, 128], F32) for i in range(3)]
        nq = 0
        for jb in jobs:
            KB, KF = S["KB%d" % jb], S["KF%d" % jb]
            for q in range(16):
                k2 = k2s[nq % 2]
                nq += 1
                self.ld(k2[:], KB[q * 32:(q + 1) * 32, :].rearrange("ch (n1 n2) -> n1 ch n2", n2=128))
                for it in range(17):
                    g = it
                    if 0 <= g < 16:
                        i2 = g % 2
                        self.fft_s1(k2, 128, g * 2, Bt[i2], tmp1[i2], tmp2[i2], pA)
                    g = it - 1
                    if 0 <= g < 16:
                        i2 = g % 2
                        xoi = xo[g % 3]
                        self.fft_s2(Bt[i2], pX)
                        self.act(xoi[:], self.p4(pX), AF.Identity, scale=1.0 / NFFT)
                        self.stq(KF[:, q * 32 + g * 2: q * 32 + g * 2 + 2, :, :], xoi[:])
                    yield

    def merge_phase(self, l, streams):
        fw, S = self.fw, self.S
        fw.begin_phase()
        woa = fw.sb("woa", [128, KC, D], BF16)
        woh = fw.sb("woh", [128, 4, D], BF16)
        wout = fw.sb("wout", [128, KC, D], BF16)
        self.load_w(woa, S["wb_oa"][l], KC)
        self.load_w(woh, S["wb_oh"][l], 4)
        self.load_w(wout, S["wb_out"][l], KC)
        xs2 = [fw.sb("xs%d" % i, [128, KC, 512], F32) for i in range(2)]
        at2 = [fw.sb("at%d" % i, [128, KC, 512], BF16) for i in range(2)]
        hy2 = [fw.sb("hy%d" % i, [128, 4, 512], BF16) for i in range(2)]
        g2 = [fw.sb("g%d" % i, [128, 16, 512], BF16) for i in range(2)]
        mg = fw.sb("mg", [128, KC, 512], BF16)
        m1 = [fw.sb("m1_%d" % i, [128, 512], F32) for i in range(2)]
        m2 = [fw.sb("m2_%d" % i, [128, 512], F32) for i in range(2)]
        mc = self.modcol[l]
        n = 0
        no = 0
        nxt = [jb for (_, l_, jb) in self.jobs if l_ == l + 1]
        bg = self.spectrum_gen(nxt, self.ps[6], self.ps[7]) if nxt else None

        def bg_step():
            nonlocal bg
            if bg is not None and next(bg, "done") == "done":
                bg = None
        for s in streams:
            TC = s.TC
            for ch in range(s.T // TC):
                t0 = ch * TC
                xs, at, hy, g = xs2[n % 2], at2[n % 2], hy2[n % 2], g2[n % 2]
                n += 1
                self.ld(xs[:, :, :TC], s.XT[:, t0:t0 + TC].rearrange("(c p) t -> p c t", p=128))
                self.ld(at[:, :, :TC], s.AT[:, t0:t0 + TC].rearrange("(c p) t -> p c t", p=128))
                self.ld(hy[:, :, :TC], s.HY[:, t0:t0 + TC].rearrange("(c p) t -> p c t", p=128))
                self.ld(g[:, :, :TC], s.G[:, t0:t0 + TC].rearrange("(c p) t -> p c t", p=128))
                for oc in range(KC):
                    i2 = no % 2
                    no += 1
                    pa, pb = self.ps[i2], self.ps[2 + i2]
                    for kc in range(KC):
                        self.mm(pa[:, :TC], woa[:, kc, oc * 128:(oc + 1) * 128], at[:, kc, :TC], kc == 0, kc == KC - 1)
                    for kc in range(4):
                        self.mm(pb[:, :TC], woh[:, kc, oc * 128:(oc + 1) * 128], hy[:, kc, :TC], kc == 0, kc == 3)
                    self.tt("dve", m1[i2][:, :TC], pa[:, :TC], g[:, oc, :TC], ALU.mult)
                    self.tt("dve", m2[i2][:, :TC], pb[:, :TC], g[:, 8 + oc, :TC], ALU.mult)
                    self.tt("pool", mg[:, oc, :TC], m1[i2][:, :TC], m2[i2][:, :TC], ALU.add)
                    bg_step()
                for oc in range(KC):
                    p = self.ps[4 + oc % 2]
                    for kc in range(KC):
                        self.mm(p[:, :TC], wout[:, kc, oc * 128:(oc + 1) * 128], mg[:, kc, :TC], kc == 0, kc == KC - 1)
                    self.stt("dve", xs[:, oc, :TC], p[:, :TC], mc[:, 16 + oc, s.j:s.j + 1], xs[:, oc, :TC], ALU.mult, ALU.add)
                    bg_step()
                self.stq(s.XT[:, t0:t0 + TC].rearrange("(c p) t -> p c t", p=128), xs[:, :, :TC])
        if bg is not None:
            for x_ in bg:
                pass
        fw.end_phase()

    def ffn_phase(self, l, streams):
        fw, S = self.fw, self.S
        fw.begin_phase()
        TC = 256
        wgu = fw.sb("wgu", [128, KC, 2 * DFF], BF16)
        wdn = fw.sb("wdn", [128, FC, D], BF16)
        self.load_w(wgu, S["wb_gu"][l], KC)
        self.load_w(wdn, S["wb_dn"][l], FC)
        RT = self.rms_tiles(TC)
        xs2 = [fw.sb("xs%d" % i, [128, KC, TC], F32) for i in range(2)]
        h2 = fw.sb("h2", [128, KC, TC], BF16)
        sg = [fw.sb("sg%d" % i, [128, TC], F32) for i in range(2)]
        sT = fw.sb("sT", [128, FC, TC], BF16)
        mc, A2 = self.modcol[l], self.A2[l]
        n = 0
        nj = 0
        for s in streams:
            for ch in range(s.T // TC):
                t0 = ch * TC
                xs = xs2[n % 2]
                n += 1
                self.ld(xs[:], s.XT[:, t0:t0 + TC].rearrange("(c p) t -> p c t", p=128))
                self.rms_mod(xs, TC, lambda c, j: A2[:, c, j:j + 1], lambda c, j: mc[:, 24 + c, j:j + 1], s.j, h2, RT)
                for j2 in range(FC):
                    i2 = nj % 2
                    nj += 1
                    pg, pu = self.ps[i2], self.ps[2 + i2]
                    for kc in range(KC):
                        self.mm(pg[:, :TC], wgu[:, kc, j2 * 128:(j2 + 1) * 128], h2[:, kc, :], kc == 0, kc == KC - 1)
                    for kc in range(KC):
                        self.mm(pu[:, :TC], wgu[:, kc, DFF + j2 * 128: DFF + (j2 + 1) * 128], h2[:, kc, :], kc == 0, kc == KC - 1)
                    self.act(sg[i2][:], pg[:, :TC], AF.Silu)
                    self.tt("dve", sT[:, j2, :], sg[i2][:], pu[:, :TC], ALU.mult)
                for oc in range(KC):
                    p = self.ps[4 + oc % 3]
                    for j2 in range(FC):
                        self.mm(p[:, :TC], wdn[:, j2, oc * 128:(oc + 1) * 128], sT[:, j2, :], j2 == 0, j2 == FC - 1)
                    self.stt("dve", xs[:, oc, :], p[:, :TC], mc[:, 40 + oc, s.j:s.j + 1], xs[:, oc, :], ALU.mult, ALU.add)
                self.stq(s.XT[:, t0:t0 + TC].rearrange("(c p) t -> p c t", p=128), xs[:])
        fw.end_phase()

    def final_phase(self):
        fw = self.fw
        s = self.lat
        fw.begin_phase()
        TC = 512
        RT = self.rms_tiles(TC)
        xs2 = [fw.sb("xs%d" % i, [128, KC, TC], F32) for i in range(2)]
        yT = fw.sb("yT", [128, KC, TC], F32)
        yt2 = [fw.sb("ytok%d" % i, [128, 4, D], F32) for i in range(2)]
        nf = self.nfin
        for ch in range(s.T // TC):
            t0 = ch * TC
            xs, yt = xs2[ch % 2], yt2[ch % 2]
            self.ld(xs[:], s.XT[:, t0:t0 + TC].rearrange("(c p) t -> p c t", p=128))
            self.rms_mod(xs, TC, lambda c, j: nf[:, c:c + 1], None, 0, yT, RT)
            for j in range(4):
                for half in range(2):
                    p = self.nps(0, 4)
                    for c4 in range(4):
                        c = half * 4 + c4
                        self.tr(p[:, c4 * 128:(c4 + 1) * 128], yT[:, c, j * 128:(j + 1) * 128], self.ident[:])
                    self.cp("act" if half == 0 else "dve", yt[:, j, half * 512:(half + 1) * 512], p[:])
            self.stq(self.out[t0:t0 + TC, :].rearrange("(j p) f -> p j f", p=128), yt[:])
        fw.end_phase()


def host_consts(SEQ):
    K = {}
    K["k_ident"] = np.eye(128, dtype=np.float32)
    r = np.zeros((128, 128), np.float32)
    for i in range(64):
        r[2 * i + 1, 2 * i] = -1.0
        r[2 * i, 2 * i + 1] = 1.0
    K["k_rmat"] = r
    a = np.arange(128, dtype=np.float64)
    ang = -2.0 * np.pi * np.outer(a, a) / 128.0
    Fr, Fi = np.cos(ang), np.sin(ang)
    K["k_ft"] = np.ascontiguousarray(np.stack([Fr, Fi, Fr, -Fi], axis=1)).astype(np.float32)
    angt = -2.0 * np.pi * np.outer(a, a) / NFFT
    K["k_tw"] = np.ascontiguousarray(np.stack([np.cos(angt), np.sin(angt)], axis=1)).astype(np.float32)
    GRID_W = 64
    rows = SEQ // GRID_W
    row = np.repeat(np.arange(rows, dtype=np.float32), GRID_W)
    col = np.tile(np.arange(GRID_W, dtype=np.float32), rows)
    inv_freq = (np.float32(10000.0) ** (-np.arange(0, 64, 2, dtype=np.float32) / np.float32(64))).astype(np.float32)
    angr = np.concatenate([row[:, None] * inv_freq, col[:, None] * inv_freq], axis=-1).astype(np.float32)
    cs, sn = np.cos(angr), np.sin(angr)
    K["k_ropec"] = np.ascontiguousarray(np.repeat(cs, 2, axis=1).T).astype(np.float32)
    K["k_ropes"] = np.ascontiguousarray(np.repeat(sn, 2, axis=1).T).astype(np.float32)

    def ztab(L):
        t = np.linspace(0.0, 1.0, L, dtype=np.float32)[:, None]
        w = (np.float32(2.0 * math.pi / L) * np.arange(L, dtype=np.float32))[:, None]
        f = np.linspace(1e-4, 15.0, 16, dtype=np.float32)[None, :]
        z = np.concatenate([t, np.cos(f * w), -np.sin(f * w)], axis=-1).astype(np.float32)
        max_decay = math.log(1e-2) / 0.3
        min_decay = math.log(1e-2) / 1.5
        deltas = np.abs(np.linspace(min_decay, max_decay, HW, dtype=np.float32))
        dec = np.exp(-t * deltas).astype(np.float32)
        zz = np.stack([z.T, z[::-1].T], axis=0)
        dd = np.stack([dec.T, dec[::-1].T], axis=0)
        return np.ascontiguousarray(zz).astype(np.float32), np.ascontiguousarray(dd).astype(np.float32)

    K["k_z_lat"], K["k_dec_lat"] = ztab(SEQ)
    K["k_z_ctx"], K["k_dec_ctx"] = ztab(CTX)
    return K


_NC_CACHE = {}


def run_cores(inputs, n_cores, dbg=None):
    x = np.asarray(inputs["x"], dtype=np.float32)
    SEQ = x.shape[1]
    DEPTH = np.asarray(inputs["w_mod"]).shape[0]
    key = (SEQ, DEPTH, tuple(sorted(dbg or [])))
    if key not in _NC_CACHE:
        nc = bass.Bass("TRN2", target_bir_lowering=False)
        b = Builder(nc, SEQ, DEPTH, dbg=dbg)
        b.build()
        _NC_CACHE[key] = nc
    nc = _NC_CACHE[key]
    K = host_consts(SEQ)
    shared = {k: np.ascontiguousarray(np.asarray(v, dtype=np.float32)) for k, v in inputs.items()
              if k not in ("x", "c", "ctx")}
    shared.update(K)
    in_maps = []
    for b_ in range(n_cores):
        m = dict(shared)
        m["x"] = np.ascontiguousarray(x[b_])
        m["c"] = np.ascontiguousarray(np.asarray(inputs["c"], dtype=np.float32)[b_])
        m["ctx"] = np.ascontiguousarray(np.asarray(inputs["ctx"], dtype=np.float32)[b_])
        in_maps.append(m)
    res = run_bass_kernel_spmd(nc, in_maps, core_ids=list(range(n_cores)))
    return res


def kernel(**inputs):
    res = run_cores(inputs, 8)
    out = np.stack([np.asarray(r["out"], dtype=np.float32) for r in res.results], axis=0)
    return out
```
